# Optimizing a Trainium2 kernel written in Bass

```python
import jax, jax.numpy as jnp
from jax import lax
import numpy as np

D_MODEL = 2048
BATCH = 1
SEQ = 8192
DEPTH = 4

D_MIX = D_MODEL
HGRN_WIDTH = D_MIX // 2
HGRN_KDIM = 128
HGRN_HEADS = HGRN_WIDTH // HGRN_KDIM
HGRN_VDIM = HGRN_WIDTH // HGRN_HEADS
HGRN_CHUNK = 64
NSA_WIDTH = D_MIX - HGRN_WIDTH
NSA_HEAD_DIM = 64
NSA_HEADS = NSA_WIDTH // NSA_HEAD_DIM
NSA_KV_GROUPS = 4
NSA_HPG = NSA_HEADS // NSA_KV_GROUPS
NSA_KV_WIDTH = NSA_KV_GROUPS * NSA_HEAD_DIM
CMP_BLOCK = 32
CMP_STRIDE = 16
CMP_HIDDEN = 4 * NSA_HEAD_DIM
SLC_BLOCK = 64
SLC_TOPN = 16
WINDOW = 512
Q_BLOCK = 128
D_FF = 256 * ((8 * D_MODEL // 3 + 255) // 256)
CONV_WIDTH = 3
ROPE_THETA = 10000.0
LN_EPS = 1e-5
RMS_EPS = 1e-6
F_MIN = 1e-30
DN_ALPHA = (2 * DEPTH) ** 0.25
DN_BETA = (8 * DEPTH) ** -0.25
NEG_INF = -1e30
FORCE_SCORE = 1e9
IN_SPLITS = (HGRN_WIDTH, HGRN_WIDTH, HGRN_WIDTH, HGRN_WIDTH, NSA_WIDTH,
             NSA_KV_WIDTH, NSA_KV_WIDTH, NSA_KV_WIDTH, NSA_KV_WIDTH, NSA_KV_WIDTH, NSA_KV_WIDTH,
             NSA_HEADS * 3)
N_IN = sum(IN_SPLITS)

kernel_name = 'hgrn2_nsa_parallel_deepnorm_convffn'


def layer_norm(x, g, b):
    xf = x.astype(jnp.float32)
    mu = jnp.mean(xf, -1, keepdims=True)
    var = jnp.mean(jnp.square(xf - mu), -1, keepdims=True)
    return ((xf - mu) * lax.rsqrt(var + LN_EPS) * g + b).astype(x.dtype)


def rope(x, pos):
    half = x.shape[-1] // 2
    inv = ROPE_THETA ** (-jnp.arange(half, dtype=jnp.float32) / half)
    ang = pos.astype(jnp.float32)[:, None] * inv[None, :]
    cos, sin = jnp.cos(ang), jnp.sin(ang)
    xf = x.astype(jnp.float32)
    x1, x2 = xf[..., :half], xf[..., half:]
    return jnp.concatenate([x1 * cos - x2 * sin, x2 * cos + x1 * sin], -1).astype(x.dtype)


def hgrn2_mixer(q, f_pre, i, g, lb, norm_w):
    B, S, _ = q.shape
    H, K, V, C = HGRN_HEADS, HGRN_KDIM, HGRN_VDIM, HGRN_CHUNK
    n = S // C
    z = f_pre.astype(jnp.float32)
    lbf = lb.astype(jnp.float32)
    f = lbf + (1.0 - lbf) * jax.nn.sigmoid(z)
    log_f = jnp.log(jnp.maximum(f, F_MIN))
    k = (1.0 - lbf) * jax.nn.sigmoid(-z)
    qs = jax.nn.silu(q.astype(jnp.float32))

    def heads(t, d):
        return t.astype(jnp.float32).reshape(B, n, C, H, d).transpose(1, 0, 3, 2, 4)

    causal = jnp.tril(jnp.ones((C, C), dtype=bool))

    def step(state, xs):
        qc, kc, vc, lfc = xs
        b = jnp.cumsum(lfc, axis=-2)
        o_inter = jnp.einsum('bhtk,bhkv->bhtv', qc * jnp.exp(b), state)
        diff = b[:, :, :, None, :] - b[:, :, None, :, :]
        decay = jnp.exp(jnp.where(causal[:, :, None], diff, NEG_INF))
        att = jnp.einsum('bhtk,bhtsk,bhsk->bhts', qc, decay, kc)
        o_intra = jnp.einsum('bhts,bhsv->bhtv', att, vc)
        b_last = b[:, :, -1:, :]
        new_state = jnp.exp(b_last[:, :, 0, :])[..., None] * state + jnp.einsum(
            'bhsk,bhsv->bhkv', kc * jnp.exp(b_last - b), vc)
        return new_state, o_inter + o_intra

    s0 = jnp.zeros((B, H, K, V), jnp.float32)
    _, o = lax.scan(step, s0, (heads(qs, K), heads(k, K), heads(i, V), heads(log_f, K)))
    o = o.transpose(1, 0, 3, 2, 4).reshape(B, S, H, V)
    o = o * lax.rsqrt(jnp.mean(jnp.square(o), -1, keepdims=True) + RMS_EPS)
    o = o.reshape(B, S, H * V) * norm_w * jax.nn.silu(g.astype(jnp.float32))
    return o.astype(q.dtype)


def compress(kv, pe, w1, w2):
    S = kv.shape[2]
    n_cmp = (S - CMP_BLOCK) // CMP_STRIDE + 1
    idx = jnp.arange(n_cmp)[:, None] * CMP_STRIDE + jnp.arange(CMP_BLOCK)[None, :]
    blk = kv[:, :, idx] + pe
    flat = blk.reshape(blk.shape[0], blk.shape[1], n_cmp, CMP_BLOCK * NSA_HEAD_DIM)
    return jax.nn.silu(flat @ w1) @ w2


def nsa_mixer(q, k_c, v_c, k_s, v_s, k_w, v_w, gate_pre, pos, pe_k, pe_v, w1_k, w2_k, w1_v, w2_v):
    B, S, _ = q.shape
    G, HP, DK = NSA_KV_GROUPS, NSA_HPG, NSA_HEAD_DIM
    nq = S // Q_BLOCK
    n_cmp = (S - CMP_BLOCK) // CMP_STRIDE + 1
    n_slc = S // SLC_BLOCK
    n_sel = min(SLC_TOPN, n_slc)
    scale = DK ** -0.5
    qh = rope(q.reshape(B, S, G, HP, DK).transpose(0, 2, 3, 1, 4), pos)

    def kv_heads(t):
        return t.reshape(B, S, G, DK).transpose(0, 2, 1, 3)

    k_c, k_s, k_w = (rope(kv_heads(t), pos) for t in (k_c, k_s, k_w))
    v_c, v_s, v_w = (kv_heads(t) for t in (v_c, v_s, v_w))
    k_cmp = compress(k_c, pe_k, w1_k, w2_k)
    v_cmp = compress(v_c, pe_v, w1_v, w2_v).astype(jnp.float32)
    cmp_start = jnp.arange(n_cmp) * CMP_STRIDE
    cmp_end = cmp_start + CMP_BLOCK - 1
    slc_start = jnp.arange(n_slc) * SLC_BLOCK
    overlap = ((cmp_start[:, None] < slc_start[None, :] + SLC_BLOCK)
               & (cmp_start[:, None] + CMP_BLOCK > slc_start[None, :])).astype(jnp.float32)
    ks_blk = k_s.reshape(B, G, n_slc, SLC_BLOCK, DK)
    vs_blk = v_s.reshape(B, G, n_slc, SLC_BLOCK, DK)
    kw_pad = jnp.pad(k_w, ((0, 0), (0, 0), (WINDOW, 0), (0, 0)))
    vw_pad = jnp.pad(v_w, ((0, 0), (0, 0), (WINDOW, 0), (0, 0)))
    gates = jax.nn.sigmoid(gate_pre.astype(jnp.float32)).reshape(B, S, G, HP, 3).transpose(0, 2, 3, 1, 4)

    def chunks(t):
        return t.reshape(B, G, HP, nq, Q_BLOCK, t.shape[-1]).transpose(3, 0, 1, 2, 4, 5)

    b_idx = jnp.arange(B)[:, None, None, None]
    g_idx = jnp.arange(G)[None, :, None, None]
    blk_ids = jnp.arange(n_slc)

    def block(args):
        qb, gb, bi = args
        t = bi * Q_BLOCK + jnp.arange(Q_BLOCK)
        m_c = cmp_end[None, :] <= t[:, None]
        s_c = jnp.einsum('bghqd,bgnd->bghqn', qb, k_cmp).astype(jnp.float32) * scale
        p_c = jax.nn.softmax(jnp.where(m_c, s_c, NEG_INF), -1) * jnp.any(m_c, -1)[:, None]
        o_c = jnp.einsum('bghqn,bgnd->bghqd', p_c, v_cmp)
        imp = jnp.einsum('bghqn,ns->bgqs', p_c, overlap)
        cur = t // SLC_BLOCK
        forced = ((blk_ids[None, :] == 0) | (blk_ids[None, :] == cur[:, None])
                  | (blk_ids[None, :] == cur[:, None] - 1))
        causal_blk = slc_start[None, :] <= t[:, None]
        score = jnp.where(forced, FORCE_SCORE, jnp.where(causal_blk, imp, -1.0))
        top_val, top_idx = lax.top_k(score, n_sel)
        blk_ok = top_val >= 0.0
        kb = ks_blk[b_idx, g_idx, top_idx].reshape(B, G, Q_BLOCK, n_sel * SLC_BLOCK, DK)
        vb = vs_blk[b_idx, g_idx, top_idx].reshape(B, G, Q_BLOCK, n_sel * SLC_BLOCK, DK)
        tok = (top_idx[..., None] * SLC_BLOCK + jnp.arange(SLC_BLOCK)).reshape(B, G, Q_BLOCK, n_sel * SLC_BLOCK)
        m_s = (tok <= t[:, None]) & jnp.repeat(blk_ok, SLC_BLOCK, axis=-1)
        s_s = jnp.einsum('bghqd,bgqkd->bghqk', qb, kb).astype(jnp.float32) * scale
        p_s = jax.nn.softmax(jnp.where(m_s[:, :, None], s_s, NEG_INF), -1)
        o_s = jnp.einsum('bghqk,bgqkd->bghqd', p_s, vb.astype(jnp.float32))
        kw = lax.dynamic_slice_in_dim(kw_pad, bi * Q_BLOCK, Q_BLOCK + WINDOW, axis=2)
        vw = lax.dynamic_slice_in_dim(vw_pad, bi * Q_BLOCK, Q_BLOCK + WINDOW, axis=2)
        j = bi * Q_BLOCK - WINDOW + jnp.arange(Q_BLOCK + WINDOW)
        m_w = (j[None, :] >= 0) & (j[None, :] <= t[:, None]) & (t[:, None] - j[None, :] < WINDOW)
        s_w = jnp.einsum('bghqd,bgkd->bghqk', qb, kw).astype(jnp.float32) * scale
        p_w = jax.nn.softmax(jnp.where(m_w, s_w, NEG_INF), -1)
        o_w = jnp.einsum('bghqk,bgkd->bghqd', p_w, vw.astype(jnp.float32))
        return gb[..., 0:1] * o_c + gb[..., 1:2] * o_s + gb[..., 2:3] * o_w

    out = lax.map(block, (chunks(qh), chunks(gates), jnp.arange(nq)))
    return out.transpose(1, 0, 4, 2, 3, 5).reshape(B, S, NSA_WIDTH).astype(q.dtype)


def conv_ffn(x, w_up, conv_w, conv_b, w_down):
    h = x @ w_up
    h = lax.conv_general_dilated(h, conv_w[:, None, :], window_strides=(1,),
                                 padding=[(CONV_WIDTH - 1, 0)],
                                 dimension_numbers=('NWC', 'WIO', 'NWC'),
                                 feature_group_count=h.shape[-1]) + conv_b
    gate, up = jnp.split(h, 2, axis=-1)
    return (jax.nn.silu(gate) * up) @ w_down


def setup_inputs(seed: int = 0) -> dict:
    key = jax.random.key(seed)
    ks = jax.random.split(key, 19)
    L = DEPTH
    lk = CMP_BLOCK * NSA_HEAD_DIM

    def nrm(k, shape, s):
        return jax.random.normal(k, shape, jnp.float32) * s

    return {
        'x': nrm(ks[0], (BATCH, SEQ, D_MODEL), 1.0),
        'w_in': nrm(ks[1], (L, D_MODEL, N_IN), D_MODEL ** -0.5),
        'w_out': nrm(ks[2], (L, D_MIX, D_MODEL), D_MIX ** -0.5 * DN_BETA),
        'hgrn_lb_logits': nrm(ks[3], (L, HGRN_WIDTH), 0.5),
        'hgrn_norm_w': 1.0 + nrm(ks[4], (L, HGRN_WIDTH), 0.02),
        'cmp_pe_k': nrm(ks[5], (L, CMP_BLOCK, NSA_HEAD_DIM), 0.1),
        'cmp_pe_v': nrm(ks[6], (L, CMP_BLOCK, NSA_HEAD_DIM), 0.1),
        'cmp_w1_k': nrm(ks[7], (L, lk, CMP_HIDDEN), lk ** -0.5),
        'cmp_w2_k': nrm(ks[8], (L, CMP_HIDDEN, NSA_HEAD_DIM), CMP_HIDDEN ** -0.5),
        'cmp_w1_v': nrm(ks[9], (L, lk, CMP_HIDDEN), lk ** -0.5),
        'cmp_w2_v': nrm(ks[10], (L, CMP_HIDDEN, NSA_HEAD_DIM), CMP_HIDDEN ** -0.5),
        'ln1_g': 1.0 + nrm(ks[11], (L, D_MODEL), 0.02),
        'ln1_b': nrm(ks[12], (L, D_MODEL), 0.01),
        'w_up': nrm(ks[13], (L, D_MODEL, 2 * D_FF), D_MODEL ** -0.5),
        'conv_w': nrm(ks[14], (L, CONV_WIDTH, 2 * D_FF), CONV_WIDTH ** -0.5),
        'conv_b': nrm(ks[15], (L, 2 * D_FF), 0.01),
        'w_down': nrm(ks[16], (L, D_FF, D_MODEL), D_FF ** -0.5 * DN_BETA),
        'ln2_g': 1.0 + nrm(ks[17], (L, D_MODEL), 0.02),
        'ln2_b': nrm(ks[18], (L, D_MODEL), 0.01),
    }


def reference(x, w_in, w_out, hgrn_lb_logits, hgrn_norm_w, cmp_pe_k, cmp_pe_v, cmp_w1_k, cmp_w2_k,
              cmp_w1_v, cmp_w2_v, ln1_g, ln1_b, w_up, conv_w, conv_b, w_down, ln2_g, ln2_b):
    S = x.shape[1]
    pos = jnp.arange(S)
    p_lb = jax.nn.softmax(hgrn_lb_logits.astype(jnp.float32), axis=0)
    lower_bounds = jnp.cumsum(p_lb, axis=0) - p_lb[0:1]
    offsets = [int(v) for v in np.cumsum(IN_SPLITS)[:-1]]
    for l in range(DEPTH):
        h = x @ w_in[l]
        hq, hf, hi, hg, nq_, kc, vc, ks_, vs_, kw, vw, gt = jnp.split(h, offsets, axis=-1)
        o_h = hgrn2_mixer(hq, hf, hi, hg, lower_bounds[l], hgrn_norm_w[l])
        o_n = nsa_mixer(nq_, kc, vc, ks_, vs_, kw, vw, gt, pos, cmp_pe_k[l], cmp_pe_v[l],
                        cmp_w1_k[l], cmp_w2_k[l], cmp_w1_v[l], cmp_w2_v[l])
        y = jnp.concatenate([o_h, o_n], axis=-1) @ w_out[l]
        x = layer_norm(DN_ALPHA * x + y, ln1_g[l], ln1_b[l])
        f = conv_ffn(x, w_up[l], conv_w[l], conv_b[l], w_down[l])
        x = layer_norm(DN_ALPHA * x + f, ln2_g[l], ln2_b[l])
    return x
```

```python
import contextlib
import numpy as np
import ml_dtypes
import concourse.bass as bass
import concourse.mybir as mybir
from concourse.bass_utils import run_bass_kernel_spmd

F32 = mybir.dt.float32
BF16 = mybir.dt.bfloat16
AF = mybir.ActivationFunctionType
ALU = mybir.AluOpType
AX = mybir.AxisListType
NPBF = ml_dtypes.bfloat16

D = 2048
SEQ = 8192
DEPTH = 4
NCORE = 8
TPC = SEQ // NCORE
NT = TPC + 2
DFF = 5632
NIN = 6704
ALPHA = (2 * DEPTH) ** 0.25
LN_EPS = 1e-5
RMS_EPS = 1e-6
NF = 5120
NTM = 1584

ENGS = ("pe", "act", "dve", "pool", "sp")


class Op:
    __slots__ = ("eng", "fn", "reads", "writes", "dma", "deps", "need_inc", "ord", "dwait")

    def __init__(self, eng, fn, reads, writes, dma):
        self.eng, self.fn, self.reads, self.writes, self.dma = eng, fn, reads, writes, dma
        self.deps = []
        self.need_inc = False
        self.ord = 0


class Prog:
    def __init__(self, nc):
        self.nc = nc
        self.ops = []
        self.last_w = {}
        self.readers = {}
        self.dseen = {}

    def op(self, eng, fn, reads=(), writes=(), dma=None):
        o = Op(eng, fn, tuple(reads), tuple(writes), dma)
        deps = set()
        for k in o.reads:
            w = self.last_w.get(k)
            if w is not None:
                deps.add(w)
        for k in o.writes:
            w = self.last_w.get(k)
            if w is not None:
                deps.add(w)
            for r in self.readers.get(k, ()):
                deps.add(r)
        for k in o.reads:
            self.readers.setdefault(k, []).append(o)
        for k in o.writes:
            self.last_w[k] = o
            self.readers[k] = []
        deps.discard(o)
        o.deps = [d for d in deps if not (d.eng == "pe" and eng == "pe" and d.dma is None and dma is None)]
        for d in o.deps:
            d.need_inc = True
        o.dwait = {d.dma: self.dseen[d.dma] for d in o.deps if d.dma is not None}
        if dma is not None:
            self.dseen[dma] = self.dseen.get(dma, 0) + 1
        self.ops.append(o)
        return o

    def dma(self, out, in_, reads=(), writes=(), key="c", eng="sp"):
        return self.op(eng, lambda e: e.dma_start(out=out, in_=in_), reads, writes, dma=key)

    def emit(self):
        nc = self.nc
        cnt = {e: 0 for e in ENGS}
        dcnt = {}
        for o in self.ops:
            if o.dma is not None:
                dcnt[o.dma] = dcnt.get(o.dma, 0) + 1
                o.ord = dcnt[o.dma]
            elif o.need_inc:
                cnt[o.eng] += 1
                o.ord = cnt[o.eng]
        with contextlib.ExitStack() as st:
            esem = {e: st.enter_context(nc.semaphore("s_" + e)) for e in ENGS if e != "sp"}
            dsem = {k: st.enter_context(nc.semaphore("d_" + str(k))) for k in dcnt}
            block = st.enter_context(nc.Block())
            by_eng = {e: [o for o in self.ops if o.eng == e] for e in ENGS}

            def run(eng_name, e):
                waited = {}
                for o in by_eng[eng_name]:
                    need = {}
                    for d in o.deps:
                        if d.dma is not None:
                            s = ("d", d.dma)
                            v = 16 * o.dwait[d.dma]
                        else:
                            s = ("e", d.eng)
                            v = d.ord
                        if v > need.get(s, 0):
                            need[s] = v
                    for s, v in need.items():
                        if waited.get(s, 0) >= v:
                            continue
                        waited[s] = v
                        e.wait_ge(dsem[s[1]] if s[0] == "d" else esem[s[1]], v)
                    ins = o.fn(e)
                    if o.dma is not None:
                        ins.then_inc(dsem[o.dma], 16)
                    elif o.need_inc:
                        ins.then_inc(esem[o.eng], 1)
                if eng_name == "sp":
                    for k, n in dcnt.items():
                        e.wait_ge(dsem[k], 16 * n)

            block.tensor(lambda e: run("pe", e))
            block.scalar(lambda e: run("act", e))
            block.vector(lambda e: run("dve", e))
            block.gpsimd(lambda e: run("pool", e))
            block.sync(lambda e: run("sp", e))


class Ctx:
    def __init__(self, nc, st):
        self.nc, self.st = nc, st
        self.n = 0

    def sb(self, shape, dt, name=None):
        self.n += 1
        return self.st.enter_context(self.nc.sbuf_tensor(name or f"sb{self.n}", list(shape), dt))

    def ps(self, shape, dt, name=None):
        self.n += 1
        return self.st.enter_context(self.nc.psum_tensor(name or f"ps{self.n}", list(shape), dt))


def make_identity(P, C, dt=BF16, name="ident"):
    idf = C.sb([128, 128], F32, name + "f")
    ident = C.sb([128, 128], dt, name)
    P.op("pool", lambda e: e.memset(idf[:], 1.0), writes=[name + "f"])
    P.op("pool", lambda e: e.affine_select(out=idf[:], in_=idf[:], pattern=[[-1, 128]], compare_op=ALU.is_equal,
                                           fill=0.0, base=0, channel_multiplier=1), reads=[name + "f"], writes=[name + "f"])
    P.op("dve", lambda e: e.tensor_copy(out=ident[:], in_=idf[:]), reads=[name + "f"], writes=[name])
    return ident


class Linear:
    def __init__(self, P, C, name, KCmax, nbuf=3, npsum=2, pn=512):
        self.P, self.C, self.name = P, C, name
        self.w = [C.sb([128, KCmax, 128], BF16, f"{name}_w{i}") for i in range(nbuf)]
        self.ps = [C.ps([128, pn], F32, f"{name}_p{i}") for i in range(npsum)]
        self.nbuf, self.npsum = nbuf, npsum
        self.wi = 0
        self.pi = 0

    def load(self, W, KC, c0, cw):
        i = self.wi % self.nbuf
        self.wi += 1
        wt = self.w[i]
        key = (self.name, "w", i)
        self.P.dma(wt[:, 0:KC, 0:cw], W[:, c0:c0 + cw].rearrange("(kc p) n -> p kc n", p=128),
                   writes=[key], key=f"{self.name}w{i}", eng="pool")
        return wt, key

    def next_ps(self):
        i = self.pi % self.npsum
        self.pi += 1
        return self.ps[i], (self.name, "p", i)

    def run(self, W, KC, chunks, in_ap, in_key, groups, epilogue):
        P = self.P
        for ci, (c0, cw) in enumerate(chunks):
            wt, wkey = self.load(W, KC, c0, cw)
            for gi, (g0, gn) in enumerate(groups):
                pt, pkey = self.next_ps()
                for kc in range(KC):
                    P.op("pe", lambda e, kc=kc, wt=wt, pt=pt, g0=g0, gn=gn, cw=cw: e.matmul(
                        pt[0:cw, 0:gn], lhsT=wt[:, kc, 0:cw], rhs=in_ap(kc, g0, gn), start=(kc == 0), stop=(kc == KC - 1)),
                        reads=[wkey, in_key(kc)], writes=[pkey])
                epilogue(ci, gi, g0, gn, cw, pt, pkey)


GROUPS = [(0, 512), (512, 512), (1024, 2)]
MGROUPS = [(0, 512), (512, 512)]


def build_T(do_c, do_a, dbg_fm=40, dbg_tm=True, dbg_rope=True):
    nc = bass.Bass("TRN2", target_bir_lowering=False)
    dt_in = lambda n, s, d: nc.dram_tensor(n, list(s), d, kind="ExternalInput").ap()
    dt_out = lambda n, s, d: nc.dram_tensor(n, list(s), d, kind="ExternalOutput").ap()
    xT_d = dt_in("xT", [D, NT], F32)
    if do_c:
        oT_d = dt_in("oT", [D, NT], BF16)
        wout_d = dt_in("w_out", [D, D], F32)
        ln_d = dt_in("lnp", [128, 4, 16], F32)
        wup_d = dt_in("w_up", [D, 2 * DFF], F32)
        cw_d = dt_in("convp", [128, 4, 88], F32)
        wdn_d = dt_in("w_down", [DFF, D], F32)
        hmask_d = dt_in("hmask", [128, 2], F32)
        xo_d = dt_out("xT_out", [D, TPC], F32)
    if do_a:
        wf_d = dt_in("w_f", [D, NF], F32)
        wt_d = dt_in("w_t", [D, NTM], F32)
        cos_d = dt_in("cosT", [128, TPC], F32)
        sin_d = dt_in("sinT", [128, TPC], F32)
        rot_d = dt_in("rotm", [128, 128], F32)
        hf32_d = dt_out("hf32", [3072, TPC], F32)
        hfb_d = dt_out("hfb", [2048, TPC], BF16)
        htb_d = dt_out("htb", [TPC, 1536], BF16)
        htg_d = dt_out("htg", [TPC, 48], F32)

    with contextlib.ExitStack() as st:
        C = Ctx(nc, st)
        P = Prog(nc)
        xb = C.sb([128, 16, NT], BF16, "xb")
        ones = C.sb([128, 128], F32, "ones")
        P.op("pool", lambda e: e.memset(ones[:], 1.0), writes=["ones"])
        epsc = C.sb([128, 1], F32, "epsc")
        P.op("pool", lambda e: e.memset(epsc[:], LN_EPS), writes=["epsc"])
        L = Linear(P, C, "L", 16)

        def layer_norm(src_ap, src_key, gcol, bcol, lnp, out32, out32_key, outb, outb_key, groups, sq, st_ps, stat):
            for (g0, gn) in groups:
                ps_s, ps_q = st_ps
                for kc in range(16):
                    P.op("act", lambda e, kc=kc, g0=g0, gn=gn: e.activation(out=sq[:, 0:gn], in_=src_ap(kc, g0, gn), func=AF.Square),
                         reads=[src_key(kc)], writes=["lnsq"])
                    P.op("pe", lambda e, kc=kc, g0=g0, gn=gn: e.matmul(ps_s[:, 0:gn], lhsT=ones[:], rhs=src_ap(kc, g0, gn), start=(kc == 0), stop=(kc == 15)),
                         reads=["ones", src_key(kc)], writes=["ln_ps_s"])
                    P.op("pe", lambda e, kc=kc, g0=g0, gn=gn: e.matmul(ps_q[:, 0:gn], lhsT=ones[:], rhs=sq[:, 0:gn], start=(kc == 0), stop=(kc == 15)),
                         reads=["ones", "lnsq"], writes=["ln_ps_q"])
                mean, rstd, tmp = stat
                P.op("act", lambda e, gn=gn: e.mul(out=mean[:, 0:gn], in_=ps_s[:, 0:gn], mul=1.0 / D), reads=["ln_ps_s"], writes=["ln_mean"])
                P.op("dve", lambda e, gn=gn: e.tensor_tensor(out=tmp[:, 0:gn], in0=mean[:, 0:gn], in1=mean[:, 0:gn], op=ALU.mult), reads=["ln_mean"], writes=["ln_tmp"])
                P.op("act", lambda e, gn=gn: e.mul(out=rstd[:, 0:gn], in_=ps_q[:, 0:gn], mul=1.0 / D), reads=["ln_ps_q"], writes=["ln_rstd"])
                P.op("dve", lambda e, gn=gn: e.tensor_tensor(out=tmp[:, 0:gn], in0=rstd[:, 0:gn], in1=tmp[:, 0:gn], op=ALU.subtract),
                     reads=["ln_rstd", "ln_tmp"], writes=["ln_tmp"])
                P.op("act", lambda e, gn=gn: e.activation(out=rstd[:, 0:gn], in_=tmp[:, 0:gn], func=AF.Ln, bias=epsc[:, 0:1]), reads=["ln_tmp", "epsc"], writes=["ln_rstd"])
                P.op("act", lambda e, gn=gn: e.activation(out=rstd[:, 0:gn], in_=rstd[:, 0:gn], func=AF.Exp, scale=-0.5), reads=["ln_rstd"], writes=["ln_rstd"])
                for kc in range(16):
                    P.op("dve", lambda e, kc=kc, g0=g0, gn=gn: e.tensor_tensor(out=tmp[:, 0:gn], in0=src_ap(kc, g0, gn), in1=mean[:, 0:gn], op=ALU.subtract),
                         reads=[src_key(kc), "ln_mean"], writes=["ln_tmp"])
                    P.op("pool", lambda e, gn=gn: e.tensor_tensor(out=tmp[:, 0:gn], in0=tmp[:, 0:gn], in1=rstd[:, 0:gn], op=ALU.mult),
                         reads=["ln_tmp", "ln_rstd"], writes=["ln_tmp"])
                    P.op("act", lambda e, kc=kc, g0=g0, gn=gn: e.activation(out=out32(kc, g0, gn), in_=tmp[:, 0:gn], func=AF.Identity,
                                                                           scale=lnp[:, gcol, kc:kc + 1], bias=lnp[:, bcol, kc:kc + 1]),
                         reads=["ln_tmp", "lnp"], writes=[out32_key(kc)])
                    P.op("dve", lambda e, kc=kc, g0=g0, gn=gn: e.tensor_copy(out=outb(kc, g0, gn), in_=out32(kc, g0, gn)),
                         reads=[out32_key(kc)], writes=[outb_key(kc)])

        if do_c:
            x32 = C.sb([128, 16, NT], F32, "x32")
            lnp = C.sb([128, 4, 16], F32, "lnp_sb")
            cvp = C.sb([128, 4, 88], F32, "cvp")
            hmask = C.sb([128, 2], F32, "hmask_sb")
            P.dma(lnp[:], ln_d, writes=["lnp"])
            P.dma(cvp[:], cw_d, writes=["cvp"])
            P.dma(hmask[:], hmask_d, writes=["hmask"])
            for kc in range(16):
                P.dma(x32[:, kc, :], xT_d[kc * 128:(kc + 1) * 128, :], writes=[("x32", kc)])
                P.dma(xb[:, kc, :], oT_d[kc * 128:(kc + 1) * 128, :], writes=[("xb", kc)])
            sq = C.sb([128, 512], F32, "lnsq")
            st_ps = (C.ps([128, 512], F32, "ln_ps_s"), C.ps([128, 512], F32, "ln_ps_q"))
            stat = (C.sb([128, 512], F32, "ln_mean"), C.sb([128, 512], F32, "ln_rstd"), C.sb([128, 512], F32, "ln_tmp"))

            evb = [C.sb([128, 512], F32, f"evb{i}") for i in range(2)]
            evc = [0]

            def evac(pt, pkey, gn):
                i = evc[0] % 2
                evc[0] += 1
                P.op("act", lambda e: e.copy(out=evb[i][:, 0:gn], in_=pt[:, 0:gn]), reads=[pkey], writes=[("evb", i)])
                return evb[i], ("evb", i)

            def ep_out(ci, gi, g0, gn, cw, pt, pkey):
                ev, ekey = evac(pt, pkey, gn)
                P.op("dve", lambda e: e.scalar_tensor_tensor(out=x32[:, ci, g0:g0 + gn], in0=x32[:, ci, g0:g0 + gn], scalar=ALPHA,
                                                             in1=ev[:, 0:gn], op0=ALU.mult, op1=ALU.add),
                     reads=[("x32", ci), ekey], writes=[("x32", ci)])
            L.run(wout_d, 16, [(c * 128, 128) for c in range(16)], lambda kc, g0, gn: xb[:, kc, g0:g0 + gn], lambda kc: ("xb", kc), GROUPS, ep_out)
            layer_norm(lambda kc, g0, gn: x32[:, kc, g0:g0 + gn], lambda kc: ("x32", kc), 0, 1, lnp,
                       lambda kc, g0, gn: x32[:, kc, g0:g0 + gn], lambda kc: ("x32", kc),
                       lambda kc, g0, gn: xb[:, kc, g0:g0 + gn], lambda kc: ("xb", kc), GROUPS, sq, st_ps, stat)
            NQ = 11
            aT = C.sb([128, NQ, TPC], BF16, "aT")
            hbuf = [C.sb([128, 2 + TPC], F32, f"hbuf{i}") for i in range(2)]
            cacc = [C.sb([128, TPC], F32, f"cacc{i}") for i in range(2)]
            for q in range(4):
                def ep_up(ci, gi, g0, gn, cw, pt, pkey, q=q):
                    jl, half = ci // 2, ci % 2
                    j = q * NQ + jl
                    hb = hbuf[half]
                    ch = j + 44 * half
                    hkeys = [("hb", half, 0), ("hb", half, 1), ("hb", half, 2)]
                    if g0 == TPC:
                        P.op("act", lambda e: e.copy(out=hb[:, 0:2], in_=pt[:, 0:2]), reads=[pkey], writes=[("hb", half, 2)])
                        P.op("dve", lambda e: e.tensor_tensor(out=hb[:, 0:2], in0=hb[:, 0:2], in1=hmask[:, 0:2], op=ALU.mult),
                             reads=[("hb", half, 2), "hmask"], writes=[("hb", half, 2)])
                        acc = cacc[half]
                        P.op("dve", lambda e: e.tensor_scalar(out=acc[:], in0=hb[:, 2:2 + TPC], scalar1=cvp[:, 2, ch:ch + 1], scalar2=cvp[:, 3, ch:ch + 1],
                                                              op0=ALU.mult, op1=ALU.add),
                             reads=hkeys + ["cvp"], writes=[("cacc", half)])
                        P.op("dve", lambda e: e.scalar_tensor_tensor(out=acc[:], in0=hb[:, 1:1 + TPC], scalar=cvp[:, 1, ch:ch + 1], in1=acc[:],
                                                                      op0=ALU.mult, op1=ALU.add),
                             reads=hkeys + ["cvp", ("cacc", half)], writes=[("cacc", half)])
                        P.op("dve", lambda e: e.scalar_tensor_tensor(out=acc[:], in0=hb[:, 0:TPC], scalar=cvp[:, 0, ch:ch + 1], in1=acc[:],
                                                                     op0=ALU.mult, op1=ALU.add),
                             reads=hkeys + ["cvp", ("cacc", half)], writes=[("cacc", half)])
                        if half == 0:
                            P.op("act", lambda e: e.activation(out=acc[:], in_=acc[:], func=AF.Silu), reads=[("cacc", 0)], writes=[("cacc", 0)])
                        else:
                            P.op("pool", lambda e: e.tensor_tensor(out=aT[:, jl, :], in0=cacc[0][:], in1=cacc[1][:], op=ALU.mult),
                                 reads=[("cacc", 0), ("cacc", 1)], writes=[("aT", jl)])
                    else:
                        P.op("act", lambda e: e.copy(out=hb[:, 2 + g0:2 + g0 + gn], in_=pt[:, 0:gn]), reads=[pkey], writes=[("hb", half, gi)])
                up_chunks = []
                for jl in range(NQ):
                    j = q * NQ + jl
                    up_chunks += [(j * 128, 128), ((44 + j) * 128, 128)]
                L.run(wup_d, 16, up_chunks, lambda kc, g0, gn: xb[:, kc, g0:g0 + gn], lambda kc: ("xb", kc), GROUPS, ep_up)

                def ep_dn(ci, gi, g0, gn, cw, pt, pkey, q=q):
                    ev, ekey = evac(pt, pkey, gn)
                    if q == 0:
                        P.op("dve", lambda e: e.scalar_tensor_tensor(out=x32[:, ci, g0:g0 + gn], in0=x32[:, ci, g0:g0 + gn], scalar=ALPHA,
                                                                     in1=ev[:, 0:gn], op0=ALU.mult, op1=ALU.add),
                             reads=[("x32", ci), ekey], writes=[("x32", ci)])
                    else:
                        P.op("dve", lambda e: e.tensor_tensor(out=x32[:, ci, g0:g0 + gn], in0=x32[:, ci, g0:g0 + gn], in1=ev[:, 0:gn], op=ALU.add),
                             reads=[("x32", ci), ekey], writes=[("x32", ci)])
                L.run(wdn_d[q * NQ * 128:(q + 1) * NQ * 128, :], NQ, [(c * 128, 128) for c in range(16)],
                      lambda kc, g0, gn: aT[:, kc, g0:g0 + gn], lambda kc: ("aT", kc), MGROUPS, ep_dn)
            layer_norm(lambda kc, g0, gn: x32[:, kc, g0:g0 + gn], lambda kc: ("x32", kc), 2, 3, lnp,
                       lambda kc, g0, gn: x32[:, kc, g0:g0 + gn], lambda kc: ("x32", kc),
                       lambda kc, g0, gn: xb[:, kc, g0:g0 + gn], lambda kc: ("xb", kc), MGROUPS, sq, st_ps, stat)
            for kc in range(16):
                P.dma(xo_d[kc * 128:(kc + 1) * 128, :], x32[:, kc, 0:TPC], reads=[("x32", kc)], key="xo")
        else:
            xs = C.sb([128, NT], F32, "xs")
            for kc in range(16):
                P.dma(xs[:], xT_d[kc * 128:(kc + 1) * 128, :], writes=["xs"], key="xs")
                P.op("dve", lambda e, kc=kc: e.tensor_copy(out=xb[:, kc, :], in_=xs[:]), reads=["xs"], writes=[("xb", kc)])

        if do_a:
            cosT = C.sb([128, TPC], F32, "cos_sb")
            sinT = C.sb([128, TPC], F32, "sin_sb")
            rotf = C.sb([128, 128], F32, "rotf")
            rotm = C.sb([128, 128], BF16, "rotm_sb")
            P.dma(cosT[:], cos_d, writes=["cosT"])
            P.dma(sinT[:], sin_d, writes=["sinT"])
            P.dma(rotf[:], rot_d, writes=["rotf"])
            P.op("dve", lambda e: e.tensor_copy(out=rotm[:], in_=rotf[:]), reads=["rotf"], writes=["rotm"])
            o32 = [C.sb([128, 512], F32, f"ao32_{i}") for i in range(2)]
            obf = [C.sb([128, 512], BF16, f"aobf_{i}") for i in range(2)]
            rb = C.sb([128, 512], BF16, "rope_b")
            rt = C.sb([128, 512], F32, "rope_t")
            rps = C.ps([128, 512], F32, "rope_ps")
            cnt = [0]

            def ep_a(ci, gi, g0, gn, cw, pt, pkey):
                i = cnt[0] % 2
                cnt[0] += 1
                if ci < 24:
                    o = o32[i]
                    okey = ("ao32", i)
                    if ci < 8 or ci >= 16:
                        P.op("act", lambda e: e.activation(out=o[:, 0:gn], in_=pt[:, 0:gn], func=AF.Silu), reads=[pkey], writes=[okey])
                    else:
                        P.op("act", lambda e: e.copy(out=o[:, 0:gn], in_=pt[:, 0:gn]), reads=[pkey], writes=[okey])
                    P.dma(hf32_d[ci * 128:(ci + 1) * 128, g0:g0 + gn], o[:, 0:gn], reads=[okey], key=f"ao{i}")
                else:
                    o = obf[i]
                    okey = ("aobf", i)
                    if ci < 38 and dbg_rope:
                        P.op("act", lambda e: e.copy(out=rb[:, 0:gn], in_=pt[:, 0:gn]), reads=[pkey], writes=["rope_b"])
                        P.op("pe", lambda e: e.matmul(rps[:, 0:gn], lhsT=rotm[:], rhs=rb[:, 0:gn], start=True, stop=True), reads=["rotm", "rope_b"], writes=["rope_ps"])
                        P.op("act", lambda e: e.copy(out=rt[:, 0:gn], in_=pt[:, 0:gn]), reads=[pkey], writes=["rope_t"])
                        P.op("act", lambda e: e.copy(out=o32[i][:, 0:gn], in_=rps[:, 0:gn]), reads=["rope_ps"], writes=[("ao32", i)])
                        P.op("dve", lambda e: e.tensor_tensor(out=rt[:, 0:gn], in0=rt[:, 0:gn], in1=cosT[:, g0:g0 + gn], op=ALU.mult), reads=["rope_t", "cosT"], writes=["rope_t"])
                        P.op("pool", lambda e: e.tensor_tensor(out=o32[i][:, 0:gn], in0=o32[i][:, 0:gn], in1=sinT[:, g0:g0 + gn], op=ALU.mult), reads=[("ao32", i), "sinT"], writes=[("ao32", i)])
                        P.op("dve", lambda e: e.tensor_tensor(out=o[:, 0:gn], in0=rt[:, 0:gn], in1=o32[i][:, 0:gn], op=ALU.add), reads=["rope_t", ("ao32", i)], writes=[okey])
                    else:
                        P.op("act", lambda e: e.copy(out=o[:, 0:gn], in_=pt[:, 0:gn]), reads=[pkey], writes=[okey])
                    r0 = (ci - 24) * 128
                    P.dma(hfb_d[r0:r0 + 128, g0:g0 + gn], o[:, 0:gn], reads=[okey], key=f"ab{i}")
            L.run(wf_d, 16, [(c * 128, 128) for c in range(dbg_fm)], lambda kc, g0, gn: xb[:, kc, g0:g0 + gn], lambda kc: ("xb", kc), MGROUPS, ep_a)

            wtm = [C.sb([128, 16, 256], BF16, f"wtm{i}") for i in range(2)]
            tps = [C.ps([128, 512], F32, f"tm_ps{i}") for i in range(2)]
            tob = [C.sb([128, 256], BF16, f"tm_ob{i}") for i in range(2)]
            tog = C.sb([128, 48], F32, "tm_og")
            k = 0
            for bi, (c0, cw) in enumerate(([(i * 256, 256) for i in range(6)] + [(1536, 48)]) if dbg_tm else []):
                wt = wtm[bi % 2]
                wkey = ("wtm", bi % 2)
                P.dma(wt[:, :, 0:cw], wt_d[:, c0:c0 + cw].rearrange("(kc p) n -> p kc n", p=128), writes=[wkey], key=f"wtm{bi % 2}", eng="pool")
                for tt in range(TPC // 128):
                    pt = tps[k % 2]
                    pkey = ("tm_ps", k % 2)
                    for kc in range(16):
                        P.op("pe", lambda e, kc=kc, wt=wt, pt=pt, tt=tt, cw=cw: e.matmul(pt[:, 0:cw], lhsT=xb[:, kc, tt * 128:(tt + 1) * 128], rhs=wt[:, kc, 0:cw],
                                                                                         start=(kc == 0), stop=(kc == 15)),
                             reads=[wkey, ("xb", kc)], writes=[pkey])
                    if cw == 256:
                        o = tob[k % 2]
                        okey = ("tm_ob", k % 2)
                        P.op("act", lambda e, o=o, pt=pt: e.copy(out=o[:], in_=pt[:, 0:256]), reads=[pkey], writes=[okey])
                        P.dma(htb_d[tt * 128:(tt + 1) * 128, c0:c0 + 256], o[:], reads=[okey], key=f"tmo{k % 2}")
                    else:
                        P.op("act", lambda e, pt=pt: e.activation(out=tog[:], in_=pt[:, 0:48], func=AF.Sigmoid), reads=[pkey], writes=["tm_og"])
                        P.dma(htg_d[tt * 128:(tt + 1) * 128, :], tog[:], reads=["tm_og"], key="tmg")
                    k += 1
        P.emit()
    return nc


FM_COLS = np.concatenate([np.arange(0, 1024), np.arange(1024, 2048), np.arange(3072, 4096), np.arange(4096, 5120),
                          np.arange(5120, 5376), np.arange(5632, 5888), np.arange(6144, 6400), np.arange(5376, 5632)])
TM_COLS = np.concatenate([np.arange(2048, 3072), np.arange(5888, 6144), np.arange(6400, 6656), np.arange(6656, 6704)])


def rope_tables():
    half = 32
    inv = (10000.0 ** (-np.arange(half, dtype=np.float32) / half)).astype(np.float32)
    pos = np.arange(SEQ, dtype=np.float32)
    ang = pos[None, :] * inv[:, None]
    cos, sin = np.cos(ang).astype(np.float32), np.sin(ang).astype(np.float32)
    cosT = np.concatenate([cos, cos, cos, cos], 0)
    sinT = np.concatenate([-sin, sin, -sin, sin], 0)
    rot = np.zeros((128, 128), np.float32)
    for m in range(128):
        k = m + 32 if (m % 64) < 32 else m - 32
        rot[k, m] = 1.0
    return cosT, sinT, rot


def with_halo(aT, c):
    s = c * TPC
    out = np.zeros((aT.shape[0], NT), aT.dtype)
    out[:, :TPC] = aT[:, s:s + TPC]
    if c > 0:
        out[:, TPC:] = aT[:, s - 2:s]
    return out


def chunked(v):
    return np.ascontiguousarray(v.reshape(-1, 128).T)


def t_inputs(c, xT, oT, prm, lc, la, tabs):
    m = {"xT": with_halo(xT, c)}
    if lc is not None:
        m["oT"] = with_halo(oT, c)
        m["w_out"] = prm["w_out"][lc]
        m["lnp"] = np.ascontiguousarray(np.stack([chunked(prm["ln1_g"][lc]), chunked(prm["ln1_b"][lc]),
                                                  chunked(prm["ln2_g"][lc]), chunked(prm["ln2_b"][lc])], 1))
        m["w_up"] = prm["w_up"][lc]
        m["convp"] = np.ascontiguousarray(np.stack([chunked(prm["conv_w"][lc][0]), chunked(prm["conv_w"][lc][1]),
                                                    chunked(prm["conv_w"][lc][2]), chunked(prm["conv_b"][lc])], 1))
        m["w_down"] = prm["w_down"][lc]
        m["hmask"] = np.full((128, 2), 0.0 if c == 0 else 1.0, np.float32)
    if la is not None:
        cosT, sinT, rot = tabs
        m["w_f"] = prm["w_f"][la]
        m["w_t"] = prm["w_t"][la]
        m["cosT"] = np.ascontiguousarray(cosT[:, c * TPC:(c + 1) * TPC])
        m["sinT"] = np.ascontiguousarray(sinT[:, c * TPC:(c + 1) * TPC])
        m["rotm"] = rot
    return m


def build_B():
    nc = bass.Bass("TRN2", target_bir_lowering=False)
    dt_in = lambda n, s, d: nc.dram_tensor(n, list(s), d, kind="ExternalInput").ap()
    qs_d = dt_in("qsT", [128, SEQ], F32)
    z_d = dt_in("zT", [128, SEQ], F32)
    sg_d = dt_in("sgT", [128, SEQ], F32)
    v_d = dt_in("v", [SEQ, 128], BF16)
    lbl_d = dt_in("lbl", [128, 4], F32)
    lmask_d = dt_in("lmask", [128, 4], F32)
    nw_d = dt_in("nw", [128, 1], F32)
    cmask_d = dt_in("cmask", [64, 512], F32)
    smask_d = dt_in("smask", [128, 2048], F32)
    oh_d = nc.dram_tensor("ohT", [128, SEQ], BF16, kind="ExternalOutput").ap()
    NCH = SEQ // 64
    with contextlib.ExitStack() as st:
        C = Ctx(nc, st)
        P = Prog(nc)
        ident = make_identity(P, C)
        ones = C.sb([128, 128], F32, "ones")
        P.op("pool", lambda e: e.memset(ones[:], 1.0), writes=["ones"])
        epsc = C.sb([128, 1], F32, "epsc")
        P.op("pool", lambda e: e.memset(epsc[:], RMS_EPS), writes=["epsc"])
        lbl = C.sb([128, 4], F32, "lbl_sb")
        lmask = C.sb([128, 4], F32, "lmask_sb")
        nw = C.sb([128, 1], F32, "nw_sb")
        cmask = C.sb([64, 512], F32, "cmask_sb")
        smask = C.sb([128, 2048], F32, "smask_sb")
        vt = C.sb([64, NCH, 128], BF16, "vt")
        for t_, d_, k_ in ((lbl, lbl_d, "lbl"), (lmask, lmask_d, "lmask"), (nw, nw_d, "nw"), (cmask, cmask_d, "cmask"), (smask, smask_d, "smask")):
            P.dma(t_[:], d_, writes=[k_])
        P.dma(vt[:], v_d.rearrange("(c s) v -> s c v", s=64), writes=["vt"])
        sm = C.sb([128, 8], F32, "sm")
        lb = C.sb([128, 4], F32, "lb")
        P.op("act", lambda e: e.activation(out=sm[:, 0:4], in_=lbl[:], func=AF.Exp), reads=["lbl"], writes=["sm0"])
        P.op("dve", lambda e: e.tensor_reduce(out=lb[:, 0:1], in_=sm[:, 0:4], axis=AX.X, op=ALU.add), reads=["sm0"], writes=["lb0"])
        P.op("dve", lambda e: e.tensor_tensor(out=sm[:, 4:8], in0=sm[:, 0:4], in1=lmask[:], op=ALU.mult), reads=["sm0", "lmask"], writes=["sm1"])
        P.op("dve", lambda e: e.tensor_reduce(out=lb[:, 1:2], in_=sm[:, 4:8], axis=AX.X, op=ALU.add), reads=["sm1"], writes=["lb1"])
        P.op("dve", lambda e: e.reciprocal(out=lb[:, 0:1], in_=lb[:, 0:1]), reads=["lb0"], writes=["lb0"])
        P.op("dve", lambda e: e.tensor_tensor(out=lb[:, 2:3], in0=lb[:, 1:2], in1=lb[:, 0:1], op=ALU.mult), reads=["lb0", "lb1"], writes=["lb2"])
        P.op("dve", lambda e: e.tensor_scalar(out=lb[:, 3:4], in0=lb[:, 2:3], scalar1=-1.0, scalar2=1.0, op0=ALU.mult, op1=ALU.add), reads=["lb2"], writes=["lb3"])
        LBK = ["lb2", "lb3"]

        qb = C.sb([128, SEQ], BF16, "qb")
        kb = C.sb([128, SEQ], BF16, "kb")
        ebl = C.sb([128, NCH], F32, "ebl")
        W = 2048
        zt = C.sb([128, W], F32, "zt")
        qt = C.sb([128, W], F32, "qt")
        t1 = C.sb([128, W], F32, "t1")
        t2 = C.sb([128, W], F32, "t2")
        t3 = C.sb([128, W], F32, "t3")
        for bi in range(SEQ // W):
            c0 = bi * W
            P.dma(zt[:], z_d[:, c0:c0 + W], writes=["zt"], key="zt")
            P.dma(qt[:], qs_d[:, c0:c0 + W], writes=["qt"], key="qt")
            P.op("act", lambda e: e.activation(out=t1[:], in_=zt[:], func=AF.Sigmoid), reads=["zt"], writes=["t1"])
            P.op("dve", lambda e: e.tensor_scalar(out=t1[:], in0=t1[:], scalar1=lb[:, 3:4], scalar2=lb[:, 2:3], op0=ALU.mult, op1=ALU.add), reads=["t1"] + LBK, writes=["t1"])
            P.op("dve", lambda e: e.tensor_scalar_max(out=t1[:], in0=t1[:], scalar1=1e-30), reads=["t1"], writes=["t1"])
            P.op("act", lambda e: e.activation(out=t1[:], in_=t1[:], func=AF.Ln), reads=["t1"], writes=["t1"])
            P.op("dve", lambda e: e.tensor_tensor_scan(out=t2[:], data0=smask[:], data1=t1[:], initial=0.0, op0=ALU.mult, op1=ALU.add), reads=["t1", "smask"], writes=["t2"])
            P.op("act", lambda e: e.activation(out=t1[:], in_=t2[:], func=AF.Exp), reads=["t2"], writes=["t1"])
            P.op("act", lambda e: e.activation(out=t3[:], in_=t2[:], func=AF.Exp, scale=-1.0), reads=["t2"], writes=["t3"])
            P.op("act", lambda e: e.activation(out=t2[:], in_=zt[:], func=AF.Sigmoid, scale=-1.0), reads=["zt", "t2"], writes=["t2"])
            P.op("dve", lambda e: e.tensor_scalar(out=t2[:], in0=t2[:], scalar1=lb[:, 3:4], scalar2=None, op0=ALU.mult), reads=["t2"] + LBK, writes=["t2"])
            P.op("pool", lambda e, c0=c0: e.tensor_tensor(out=kb[:, c0:c0 + W], in0=t2[:], in1=t3[:], op=ALU.mult), reads=["t2", "t3"], writes=[("kb", bi)])
            P.op("pool", lambda e, c0=c0: e.tensor_tensor(out=qb[:, c0:c0 + W], in0=qt[:], in1=t1[:], op=ALU.mult), reads=["qt", "t1"], writes=[("qb", bi)])
            P.op("dve", lambda e, bi=bi: e.tensor_copy(out=ebl[:, bi * 32:(bi + 1) * 32], in_=t1[:].rearrange("p (c s) -> p c s", s=64)[:, :, 63]),
                 reads=["t1"], writes=[("ebl", bi)])

        state = C.sb([128, 128], F32, "state")
        stmp = C.sb([128, 128], F32, "stmp")
        P.op("pool", lambda e: e.memset(state[:], 0.0), writes=["state"])
        sbf = [C.sb([128, 128], BF16, f"sbf{i}") for i in range(8)]
        attp = C.ps([64, 512], F32, "attp")
        ktp = C.ps([64, 8, 128], BF16, "ktp")
        up = [C.ps([128, 4, 128], F32, f"up{i}") for i in range(2)]
        outp = C.ps([128, 512], F32, "outp")
        ssp = C.ps([128, 512], F32, "ssp")
        atts = C.sb([64, 512], F32, "atts")
        attm = C.sb([64, 512], BF16, "attm")
        kbt = C.sb([64, 8, 128], BF16, "kbt")
        us = C.sb([128, 8, 128], F32, "us")
        os_ = C.sb([128, 512], F32, "os")
        osq = C.sb([128, 512], F32, "osq")
        rr = C.sb([128, 512], F32, "rr")
        sgt = [C.sb([128, 512], F32, f"sgt{i}") for i in range(2)]
        obo = [C.sb([128, 512], BF16, f"obo{i}") for i in range(2)]
        for g in range(SEQ // 512):
            t0 = g * 512
            bi = t0 // W
            sg = sgt[g % 2]
            P.dma(sg[:], sg_d[:, t0:t0 + 512], writes=[("sgt", g % 2)], key=f"sgt{g % 2}")
            for ch in range(8):
                a = t0 + ch * 64
                P.op("pe", lambda e, a=a, ch=ch: e.matmul(attp[:, ch * 64:(ch + 1) * 64], lhsT=kb[:, a:a + 64], rhs=qb[:, a:a + 64], start=True, stop=True),
                     reads=[("kb", bi), ("qb", bi)], writes=["attp"])
            for ch in range(8):
                a = t0 + ch * 64
                P.op("pe", lambda e, a=a, ch=ch: e.transpose(out=ktp[:, ch, :], in_=kb[:, a:a + 64], identity=ident[:]),
                     reads=[("kb", bi), "ident"], writes=["ktp"])
            P.op("act", lambda e: e.copy(out=atts[:], in_=attp[:]), reads=["attp"], writes=["atts"])
            P.op("pool", lambda e: e.tensor_tensor(out=attm[:], in0=atts[:], in1=cmask[:], op=ALU.mult), reads=["atts", "cmask"], writes=["attm"])
            P.op("act", lambda e: e.copy(out=kbt[:], in_=ktp[:]), reads=["ktp"], writes=["kbt"])
            for ch in range(8):
                cidx = g * 8 + ch
                P.op("pe", lambda e, ch=ch, cidx=cidx: e.matmul(up[ch // 4][:, ch % 4, :], lhsT=kbt[:, ch, :], rhs=vt[:, cidx, :], start=True, stop=True),
                     reads=["kbt", "vt"], writes=[("up", ch // 4)])
            for h2 in range(2):
                P.op("act", lambda e, h2=h2: e.copy(out=us[:, h2 * 4:(h2 + 1) * 4, :], in_=up[h2][:]), reads=[("up", h2)], writes=[("us", h2)])
            for ch in range(8):
                cidx = g * 8 + ch
                P.op("pool", lambda e, ch=ch: e.tensor_copy(out=sbf[ch][:], in_=state[:]), reads=["state"], writes=[("sbf", ch)])
                P.op("dve", lambda e, ch=ch: e.tensor_tensor(out=stmp[:], in0=state[:], in1=us[:, ch, :], op=ALU.add), reads=["state", ("us", ch // 4)], writes=["stmp"])
                P.op("dve", lambda e, cidx=cidx: e.tensor_scalar(out=state[:], in0=stmp[:], scalar1=ebl[:, cidx:cidx + 1], scalar2=None, op0=ALU.mult),
                     reads=["stmp", ("ebl", cidx // 32)], writes=["state"])
            for ch in range(8):
                a = t0 + ch * 64
                cidx = g * 8 + ch
                P.op("pe", lambda e, a=a, ch=ch: e.matmul(outp[:, ch * 64:(ch + 1) * 64], lhsT=sbf[ch][:], rhs=qb[:, a:a + 64], start=True, stop=False),
                     reads=[("sbf", ch), ("qb", bi)], writes=["outp"])
                P.op("pe", lambda e, ch=ch, cidx=cidx: e.matmul(outp[:, ch * 64:(ch + 1) * 64], lhsT=vt[:, cidx, :], rhs=attm[:, ch * 64:(ch + 1) * 64], start=False, stop=True),
                     reads=["vt", "attm"], writes=["outp"])
            P.op("act", lambda e: e.copy(out=os_[:], in_=outp[:]), reads=["outp"], writes=["os"])
            P.op("act", lambda e: e.activation(out=osq[:], in_=outp[:], func=AF.Square), reads=["outp"], writes=["osq"])
            P.op("pe", lambda e: e.matmul(ssp[:], lhsT=ones[:], rhs=osq[:], start=True, stop=True), reads=["ones", "osq"], writes=["ssp"])
            P.op("act", lambda e: e.activation(out=rr[:], in_=ssp[:], func=AF.Ln, scale=1.0 / 128, bias=epsc[:, 0:1]), reads=["ssp", "epsc"], writes=["rr"])
            P.op("act", lambda e: e.activation(out=rr[:], in_=rr[:], func=AF.Exp, scale=-0.5), reads=["rr"], writes=["rr"])
            P.op("dve", lambda e: e.tensor_tensor(out=os_[:], in0=os_[:], in1=rr[:], op=ALU.mult), reads=["os", "rr"], writes=["os"])
            P.op("dve", lambda e: e.tensor_scalar(out=os_[:], in0=os_[:], scalar1=nw[:, 0:1], scalar2=None, op0=ALU.mult), reads=["os", "nw"], writes=["os"])
            ob_ = obo[g % 2]
            P.op("pool", lambda e, ob_=ob_, sg=sg: e.tensor_tensor(out=ob_[:], in0=os_[:], in1=sg[:], op=ALU.mult), reads=["os", ("sgt", g % 2)], writes=[("obo", g % 2)])
            P.dma(oh_d[:, t0:t0 + 512], ob_[:], reads=[("obo", g % 2)], key=f"oh{g % 2}")
        P.emit()
    return nc


def b_consts(layer):
    cm = np.zeros((64, 8, 64), np.float32)
    s = np.arange(64)
    cm[:] = (s[:, None] <= s[None, :]).astype(np.float32)[:, None, :]
    sm = np.ones((128, 2048), np.float32)
    sm[:, ::64] = 0.0
    lm = np.zeros((128, 4), np.float32)
    lm[:, 1:layer + 1] = 1.0
    return cm.reshape(64, 512), sm, lm


NQB = 32


def build_N(blocks=None):
    parity = 1
    nc = bass.Bass("TRN2", target_bir_lowering=False)
    dt_in = lambda n, s, d: nc.dram_tensor(n, list(s), d, kind="ExternalInput").ap()
    q_d = dt_in("qsel", [64, 4, NQB * 128], BF16)
    kc_d = dt_in("kcT", [64, SEQ], BF16)
    vc_d = dt_in("vcT", [64, SEQ], BF16)
    ks_d = dt_in("ksT", [64, SEQ], BF16)
    kw_d = dt_in("kwT", [64, SEQ], BF16)
    vs_d = dt_in("vs", [SEQ, 65], BF16)
    vw_d = dt_in("vw", [SEQ, 65], BF16)
    f0_d = dt_in("F0", [128, 128], F32)
    vm_d = dt_in("vmask", [128, 4], F32)
    gt_d = dt_in("gsel", [128, NQB, 12], F32)
    pek_d = dt_in("pekT", [64, 32], F32)
    pev_d = dt_in("pevT", [64, 32], F32)
    w1k_d = dt_in("w1k", [2048, 256], F32)
    w1v_d = dt_in("w1v", [2048, 256], F32)
    w2k_d = dt_in("w2k", [256, 64], F32)
    w2v_d = dt_in("w2v", [256, 64], F32)
    ovl_d = dt_in("ovl", [128, 4, 128], F32)
    E_d = dt_in("Etab", [128, 64, 128], F32)
    mm_d = dt_in("Mm", [128, 254], F32)
    ma_d = dt_in("Ma", [128, 254], F32)
    on_d = nc.dram_tensor("onT", [256, NQB * 128], BF16, kind="ExternalOutput").ap()
    with contextlib.ExitStack() as st:
        C = Ctx(nc, st)
        P = Prog(nc)
        ident = make_identity(P, C)
        q_sb = C.sb([64, 4, NQB * 128], BF16, "q_sb")
        kcT = C.sb([64, SEQ], BF16, "kcT_sb")
        vcT = C.sb([64, SEQ], BF16, "vcT_sb")
        ksT = C.sb([64, SEQ], BF16, "ksT_sb")
        kwT = C.sb([64, SEQ], BF16, "kwT_sb")
        vs_sb = C.sb([128, 64, 65], BF16, "vs_sb")
        vw_sb = C.sb([128, 64, 65], BF16, "vw_sb")
        gsb = C.sb([128, NQB, 12], F32, "gsb")
        pek = C.sb([64, 32], BF16, "pek")
        pev = C.sb([64, 32], BF16, "pev")
        w1k = C.sb([64, 32, 256], BF16, "w1k_sb")
        w1v = C.sb([64, 32, 256], BF16, "w1v_sb")
        w2k = C.sb([128, 2, 64], BF16, "w2k_sb")
        w2v = C.sb([128, 2, 64], BF16, "w2v_sb")
        ovl = C.sb([128, 4, 128], BF16, "ovl_sb")
        Et = C.sb([128, 64, 128], BF16, "E_sb")
        Mm = C.sb([128, 254], F32, "Mm_sb")
        Ma = C.sb([128, 254], F32, "Ma_sb")
        for t_, d_, k_ in ((q_sb, q_d, "q"), (kcT, kc_d, "kcT"), (vcT, vc_d, "vcT"), (ksT, ks_d, "ksT"), (kwT, kw_d, "kwT"),
                           (gsb, gt_d, "gsb"), (Mm, mm_d, "Mm"), (Ma, ma_d, "Ma")):
            P.dma(t_[:], d_, writes=[k_])
        P.dma(vs_sb[:], vs_d.rearrange("(kt p) d -> p kt d", p=128), writes=["vs"])
        P.dma(vw_sb[:], vw_d.rearrange("(kt p) d -> p kt d", p=128), writes=["vw"])
        F0 = C.sb([128, 128], F32, "F0_sb")
        vmask = C.sb([128, 4], F32, "vmask_sb")
        P.dma(F0[:], f0_d, writes=["F0"])
        P.dma(vmask[:], vm_d, writes=["vmask"])
        P.dma(pek[:], pek_d, writes=["pek"], key="cc", eng="pool")
        P.dma(pev[:], pev_d, writes=["pev"], key="cc", eng="pool")
        P.dma(w1k[:], w1k_d.rearrange("(j d) h -> d j h", d=64), writes=["w1k"], key="cc", eng="pool")
        P.dma(w1v[:], w1v_d.rearrange("(j d) h -> d j h", d=64), writes=["w1v"], key="cc", eng="pool")
        P.dma(w2k[:], w2k_d.rearrange("(c p) d -> p c d", p=128), writes=["w2k"], key="cc", eng="pool")
        P.dma(w2v[:], w2v_d.rearrange("(c p) d -> p c d", p=128), writes=["w2v"], key="cc", eng="pool")
        P.dma(ovl[:], ovl_d, writes=["ovl"], key="cc", eng="pool")
        P.dma(Et[:], E_d, writes=["E"], key="cc", eng="pool")

        sp = [C.ps([128, 512], F32, f"sp{i}") for i in range(2)]
        accC = C.ps([128, 512], F32, "accC")[:, 0:260].rearrange("p (h d) -> p h d", d=65)
        accS = C.ps([128, 512], F32, "accS")[:, 0:260].rearrange("p (h d) -> p h d", d=65)
        accW = C.ps([128, 512], F32, "accW")[:, 0:260].rearrange("p (h d) -> p h d", d=65)
        impP = C.ps([128, 4, 128], F32, "impP")
        tpP = C.ps([128, 8, 128], BF16, "tpP")

        kcmpT = C.sb([64, 512], BF16, "kcmpT")
        vcmp = C.sb([128, 4, 65], BF16, "vcmp")
        P.op("pool", lambda e: e.memset(vcmp[:], 1.0), writes=["vcmp"])
        hid = [C.sb([128, 512], BF16, f"hid{i}") for i in range(2)]
        cb = C.sb([128, 2], F32, "cbias")
        for which, (srcT, skey, w1, w1key, w2, w2key, pe, pekey) in enumerate(((kcT, "kcT", w1k, "w1k", w2k, "w2k", pek, "pek"),
                                                                                 (vcT, "vcT", w1v, "w1v", w2v, "w2v", pev, "pev"))):
            view = srcT[:].rearrange("p (n s) -> p n s", s=16)
            for hc in range(2):
                for j in range(32):
                    P.op("pe", lambda e, j=j, hc=hc, w1=w1, pe=pe: e.matmul(impP[:, 0, 0:1], lhsT=w1[:, j, hc * 128:(hc + 1) * 128], rhs=pe[:, j:j + 1],
                                                                          start=(j == 0), stop=(j == 31)), reads=[w1key, pekey], writes=["impP"])
                P.op("act", lambda e, hc=hc: e.copy(out=cb[:, hc:hc + 1], in_=impP[:, 0, 0:1]), reads=["impP"], writes=[("cb", hc)])
                for j in range(32):
                    n0, s_ = j // 16, j % 16
                    P.op("pe", lambda e, j=j, hc=hc, w1=w1, n0=n0, s_=s_, view=view: e.matmul(sp[hc][:, 0:511], lhsT=w1[:, j, hc * 128:(hc + 1) * 128],
                                                                                              rhs=view[:, n0:n0 + 511, s_], start=(j == 0), stop=(j == 31)),
                         reads=[w1key, skey], writes=[("sp", hc)])
                P.op("pool", lambda e, hc=hc: e.memset(hid[hc][:], 0.0), writes=[("hid", hc)])
                P.op("act", lambda e, hc=hc: e.activation(out=hid[hc][:, 0:511], in_=sp[hc][:, 0:511], func=AF.Silu, bias=cb[:, hc:hc + 1]),
                     reads=[("sp", hc), ("cb", hc)], writes=[("hid", hc)])
            if which == 0:
                for hc in range(2):
                    P.op("pe", lambda e, hc=hc, w2=w2: e.matmul(sp[0][0:64, :], lhsT=w2[:, hc, :], rhs=hid[hc][:], start=(hc == 0), stop=(hc == 1)),
                         reads=[w2key, ("hid", hc)], writes=[("sp", 0)])
                P.op("act", lambda e: e.copy(out=kcmpT[:], in_=sp[0][0:64, :]), reads=[("sp", 0)], writes=["kcmpT"])
            else:
                for nt in range(4):
                    for hc in range(2):
                        P.op("pe", lambda e, hc=hc, nt=nt, w2=w2: e.matmul(sp[1][:, nt * 64:(nt + 1) * 64], lhsT=hid[hc][:, nt * 128:(nt + 1) * 128], rhs=w2[:, hc, :],
                                                                            start=(hc == 0), stop=(hc == 1)), reads=[w2key, ("hid", hc)], writes=[("sp", 1)])
                P.op("act", lambda e: e.copy(out=vcmp[:, :, 0:64], in_=sp[1][:, 0:256].rearrange("p (n d) -> p n d", d=64)), reads=[("sp", 1), "vcmp"], writes=["vcmp"])
                for nt in range(4):
                    P.op("dve", lambda e, nt=nt: e.tensor_scalar(out=vcmp[:, nt, :], in0=vcmp[:, nt, :], scalar1=vmask[:, nt:nt + 1], scalar2=None, op0=ALU.mult),
                         reads=["vcmp", "vmask"], writes=["vcmp"])

        pex = [C.sb([128, 512], BF16, f"pex{i}") for i in range(3)]
        pcnt = [0]
        scnt = [0]
        accs = [C.sb([128, 4, 65], F32, f"accs{i}") for i in range(3)]
        imps = C.sb([128, 4, 128], F32, "imps")
        rc = C.sb([128, 12], F32, "rc")
        wg = C.sb([128, 12], F32, "wg")
        impa = C.sb([128, 128], F32, "impa")
        sc = C.sb([128, 128], F32, "sc")
        sc2 = C.sb([128, 128], F32, "sc2")
        m8 = C.sb([128, 16], F32, "m8")
        selb = C.sb([128, 128], BF16, "selb")
        selT = C.sb([128, 4, 128], BF16, "selT")
        o32 = C.sb([128, 4, 64], F32, "o32")
        obf = C.sb([128, 256], BF16, "obf")
        onb = [C.sb([128, 2, 128], BF16, f"onb{i}") for i in range(2)]
        PAT = [[0, 4], [1, 128]]

        def score_tile(kT, kkey, c0, qv, extra=None):
            s_i = scnt[0] % 2
            scnt[0] += 1
            spt, spk = sp[s_i], ("sp", s_i)
            P.op("pe", lambda e: e.matmul(spt[:], lhsT=kT[:, c0:c0 + 128], rhs=qv, start=True, stop=(extra is None)), reads=[kkey, "q"], writes=[spk])
            if extra is not None:
                extra(spt, spk)
            p_i = pcnt[0] % 3
            pcnt[0] += 1
            pt, pk = pex[p_i], ("pex", p_i)
            P.op("act", lambda e: e.activation(out=pt[:], in_=spt[:], func=AF.Exp, scale=0.125), reads=[spk], writes=[pk])
            return pt, pk

        def mask_tile(pt, pk, base, cm, pat):
            P.op("pool", lambda e: e.affine_select(out=pt[:], in_=pt[:], pattern=pat, compare_op=ALU.is_ge, fill=0.0, base=base, channel_multiplier=cm),
                 reads=[pk], writes=[pk])

        def pv(pt, pk, acc, akey, vsb, vkey, vidx, first, last):
            for h in range(4):
                P.op("pe", lambda e, h=h: e.matmul(acc[:, h, :], lhsT=pt[:, h * 128:(h + 1) * 128], rhs=vsb[:, vidx, :], start=(first and h == 0), stop=last),
                     reads=[pk, vkey], writes=[akey])

        for i in (range(NQB) if blocks is None else blocks):
            qb = 2 * i + parity
            t0 = qb * 128
            qv = q_sb[:, :, i * 128:(i + 1) * 128]
            ncmp = min(4, (8 * qb + 6) // 128 + 1)
            for j in range(ncmp):
                pt, pk = score_tile(kcmpT, "kcmpT", j * 128, qv)
                if (j + 1) * 128 - 1 >= 8 * qb - 1:
                    mask_tile(pt, pk, t0 - 31 - 16 * 128 * j, -16, PAT)
                pv(pt, pk, accC, "accC", vcmp, "vcmp", j, j == 0, j == ncmp - 1)
                for h in range(4):
                    P.op("pe", lambda e, h=h, pt=pt, j=j, ncmp=ncmp: e.matmul(impP[:, h, :], lhsT=pt[:, h * 128:(h + 1) * 128], rhs=ovl[:, j, :], start=(j == 0 and h == 0), stop=(j == ncmp - 1)),
                         reads=[pk, "ovl"], writes=["impP"])
            P.op("act", lambda e: e.copy(out=accs[0][:], in_=accC[:]), reads=["accC"], writes=[("accs", 0)])
            P.op("act", lambda e: e.copy(out=imps[:], in_=impP[:]), reads=["impP"], writes=["imps"])
            P.op("dve", lambda e: e.tensor_scalar_max(out=rc[:, 0:4], in0=accs[0][:, :, 64], scalar1=1e-30), reads=[("accs", 0)], writes=[("rc", 0)])
            P.op("dve", lambda e: e.reciprocal(out=rc[:, 0:4], in_=rc[:, 0:4]), reads=[("rc", 0)], writes=[("rc", 0)])
            P.op("dve", lambda e: e.tensor_scalar(out=impa[:], in0=imps[:, 0, :], scalar1=rc[:, 0:1], scalar2=None, op0=ALU.mult), reads=["imps", ("rc", 0)], writes=["impa"])
            for h in range(1, 4):
                P.op("dve", lambda e, h=h: e.scalar_tensor_tensor(out=impa[:], in0=imps[:, h, :], scalar=rc[:, h:h + 1], in1=impa[:], op0=ALU.mult, op1=ALU.add),
                     reads=["imps", ("rc", 0), "impa"], writes=["impa"])
            so = 126 - 2 * qb
            P.op("dve", lambda e, so=so: e.tensor_tensor(out=sc[:], in0=impa[:], in1=Mm[:, so:so + 128], op=ALU.mult), reads=["impa", "Mm"], writes=["sc"])
            P.op("dve", lambda e, so=so: e.tensor_tensor(out=sc[:], in0=sc[:], in1=Ma[:, so:so + 128], op=ALU.add), reads=["sc", "Ma"], writes=["sc"])
            P.op("dve", lambda e: e.tensor_tensor(out=sc[:], in0=sc[:], in1=F0[:], op=ALU.max), reads=["sc", "F0"], writes=["sc"])
            P.op("dve", lambda e: e.max(out=m8[:, 0:8], in_=sc[:]), reads=["sc"], writes=[("m8", 0)])
            P.op("dve", lambda e: e.match_replace(out=sc2[:], in_to_replace=m8[:, 0:8], in_values=sc[:], imm_value=-2.0), reads=["sc", ("m8", 0)], writes=["sc2"])
            P.op("dve", lambda e: e.max(out=m8[:, 8:16], in_=sc2[:]), reads=["sc2"], writes=[("m8", 1)])
            P.op("dve", lambda e: e.tensor_scalar_max(out=m8[:, 15:16], in0=m8[:, 15:16], scalar1=0.0), reads=[("m8", 1)], writes=[("m8", 1)])
            P.op("dve", lambda e: e.tensor_scalar(out=sc2[:], in0=sc[:], scalar1=m8[:, 15:16], scalar2=None, op0=ALU.is_ge), reads=["sc", ("m8", 1), "sc2"], writes=["sc2"])
            P.op("dve", lambda e: e.tensor_scalar(out=selb[:], in0=sc2[:], scalar1=-1.0, scalar2=30000.0, op0=ALU.add, op1=ALU.mult), reads=["sc2"], writes=["selb"])
            for h in range(4):
                P.op("pe", lambda e, h=h: e.transpose(out=tpP[:, h, :], in_=selb[:], identity=ident[:]), reads=["selb", "ident"], writes=["tpP"])
            P.op("act", lambda e: e.copy(out=selT[:], in_=tpP[:, 0:4, :]), reads=["tpP"], writes=["selT"])
            for kt in range(qb + 1):
                def extra(spt, spk, kt=kt):
                    P.op("pe", lambda e: e.matmul(spt[:], lhsT=Et[:, kt, :], rhs=selT[:], start=False, stop=True), reads=["E", "selT"], writes=[spk])
                pt, pk = score_tile(ksT, "ksT", kt * 128, qv, extra)
                if kt == qb:
                    mask_tile(pt, pk, 0, -1, PAT)
                pv(pt, pk, accS, "accS", vs_sb, "vs", kt, kt == 0, kt == qb)
            P.op("act", lambda e: e.copy(out=accs[1][:], in_=accS[:]), reads=["accS"], writes=[("accs", 1)])
            k0 = max(0, qb - 4)
            for kt in range(k0, qb + 1):
                pt, pk = score_tile(kwT, "kwT", kt * 128, qv)
                if kt == qb:
                    mask_tile(pt, pk, 0, -1, PAT)
                elif kt == qb - 4:
                    mask_tile(pt, pk, -1, 1, [[0, 4], [-1, 128]])
                pv(pt, pk, accW, "accW", vw_sb, "vw", kt, kt == k0, kt == qb)
            P.op("act", lambda e: e.copy(out=accs[2][:], in_=accW[:]), reads=["accW"], writes=[("accs", 2)])
            gv = gsb[:, i, :].rearrange("p (h b) -> p h b", b=3)
            for b in range(3):
                if b > 0:
                    P.op("dve", lambda e, b=b: e.tensor_scalar_max(out=rc[:, 4 * b:4 * b + 4], in0=accs[b][:, :, 64], scalar1=1e-30), reads=[("accs", b)], writes=[("rc", b)])
                    P.op("dve", lambda e, b=b: e.reciprocal(out=rc[:, 4 * b:4 * b + 4], in_=rc[:, 4 * b:4 * b + 4]), reads=[("rc", b)], writes=[("rc", b)])
                P.op("dve", lambda e, b=b, gv=gv: e.tensor_tensor(out=wg[:, 4 * b:4 * b + 4], in0=rc[:, 4 * b:4 * b + 4], in1=gv[:, :, b], op=ALU.mult), reads=[("rc", b), "gsb"], writes=[("wg", b)])
                for h in range(4):
                    if b == 0:
                        P.op("dve", lambda e, h=h: e.tensor_scalar(out=o32[:, h, :], in0=accs[0][:, h, 0:64], scalar1=wg[:, h:h + 1], scalar2=None, op0=ALU.mult),
                             reads=[("accs", 0), ("wg", 0)], writes=[("o32", h)])
                    else:
                        P.op("dve", lambda e, h=h, b=b: e.scalar_tensor_tensor(out=o32[:, h, :], in0=accs[b][:, h, 0:64], scalar=wg[:, 4 * b + h:4 * b + h + 1], in1=o32[:, h, :],
                                                                               op0=ALU.mult, op1=ALU.add), reads=[("accs", b), ("wg", b), ("o32", h)], writes=[("o32", h)])
            P.op("pool", lambda e: e.tensor_copy(out=obf[:], in_=o32[:].rearrange("p h d -> p (h d)")), reads=[("o32", h) for h in range(4)], writes=["obf"])
            for h2 in range(2):
                P.op("pe", lambda e, h2=h2: e.transpose(out=tpP[:, h2, :], in_=obf[:, h2 * 128:(h2 + 1) * 128], identity=ident[:]), reads=["obf", "ident"], writes=["tpP"])
            ob_ = onb[i % 2]
            P.op("act", lambda e, ob_=ob_: e.copy(out=ob_[:], in_=tpP[:, 0:2, :]), reads=["tpP"], writes=[("onb", i % 2)])
            P.dma(on_d[:, i * 128:(i + 1) * 128].rearrange("(c p) t -> p c t", p=128), ob_[:], reads=[("onb", i % 2)], key=f"on{i % 2}")
        P.emit()
    return nc


def n_consts():
    n = np.arange(512)
    blk = np.arange(128)
    ov = ((16 * n[:, None] < 64 * blk[None, :] + 64) & (16 * n[:, None] + 32 > 64 * blk[None, :])).astype(np.float32)
    ov[511] = 0.0
    ovl = np.ascontiguousarray(ov.reshape(4, 128, 128).transpose(1, 0, 2))
    E = np.zeros((128, 64, 128), np.float32)
    for kt in range(64):
        E[2 * kt, kt, 0:64] = 1.0
        E[2 * kt + 1, kt, 64:128] = 1.0
    tl = np.arange(128)
    rel = np.arange(254) - 126
    cur = (tl // 64)[:, None]
    Mm = (rel[None, :] <= cur - 2).astype(np.float32)
    Ma = np.where((rel[None, :] == cur) | (rel[None, :] == cur - 1), 1e9, np.where(rel[None, :] > cur, -1.0, 0.0)).astype(np.float32)
    return ovl, E, Mm, Ma


def n_inputs(c, hfb, htb, htg, prm, l, consts):
    g, par = c // 2, c % 2
    ovl, E, Mm, Ma = consts
    sh = 128 if par == 0 else 0

    def shT(a):
        if sh == 0:
            return np.ascontiguousarray(a)
        out = np.zeros_like(a)
        out[:, sh:] = a[:, :SEQ - sh]
        return out

    def shV(a):
        out = np.zeros((SEQ, 65), NPBF)
        out[sh:, 0:64] = a[:SEQ - sh]
        out[sh:, 64] = 1.0
        return out
    qg = hfb[g * 256:(g + 1) * 256].reshape(4, 64, SEQ // 128, 128)[:, :, par::2, :]
    qsel = np.ascontiguousarray(qg.transpose(1, 0, 2, 3).reshape(64, 4, NQB * 128))
    gs = htg[:, g * 12:(g + 1) * 12].reshape(SEQ // 128, 128, 12)[par::2]
    nsh, bsh = sh // 16, sh // 64
    ov = ovl.transpose(1, 0, 2).reshape(512, 128)
    ov2 = np.zeros_like(ov)
    ov2[nsh:, bsh:] = ov[:512 - nsh, :128 - bsh]
    ov2[511] = 0.0
    F0 = np.full((128, 128), -3.0e38, np.float32)
    F0[:, bsh] = 1e9
    vm = np.ones(512, np.float32)
    vm[:nsh] = 0.0
    return {"qsel": qsel, "kcT": shT(hfb[1024 + g * 64:1024 + (g + 1) * 64]),
            "ksT": shT(hfb[1280 + g * 64:1280 + (g + 1) * 64]), "kwT": shT(hfb[1536 + g * 64:1536 + (g + 1) * 64]),
            "vcT": shT(hfb[1792 + g * 64:1792 + (g + 1) * 64]),
            "vs": shV(htb[:, 1024 + g * 64:1024 + (g + 1) * 64]), "vw": shV(htb[:, 1280 + g * 64:1280 + (g + 1) * 64]),
            "gsel": np.ascontiguousarray(gs.transpose(1, 0, 2)),
            "pekT": np.ascontiguousarray(prm["cmp_pe_k"][l].T), "pevT": np.ascontiguousarray(prm["cmp_pe_v"][l].T),
            "w1k": prm["cmp_w1_k"][l], "w1v": prm["cmp_w1_v"][l], "w2k": prm["cmp_w2_k"][l], "w2v": prm["cmp_w2_v"][l],
            "ovl": np.ascontiguousarray(ov2.reshape(4, 128, 128).transpose(1, 0, 2)), "Etab": E, "Mm": Mm, "Ma": Ma,
            "F0": F0, "vmask": np.ascontiguousarray(vm.reshape(4, 128).T)}


_CACHE = {}
_DBG = None


def _prog(name, fn):
    if name not in _CACHE:
        _CACHE[name] = fn()
    return _CACHE[name]


def kernel(**inp):
    prm = {k: np.asarray(v) for k, v in inp.items()}
    prm["w_f"] = [np.ascontiguousarray(prm["w_in"][l][:, FM_COLS]) for l in range(DEPTH)]
    prm["w_t"] = [np.ascontiguousarray(prm["w_in"][l][:, TM_COLS]) for l in range(DEPTH)]
    tabs = rope_tables()
    nconst = n_consts()
    cores = list(range(NCORE))
    xT = np.ascontiguousarray(prm["x"][0].T)
    res = run_bass_kernel_spmd(_prog("A", lambda: build_T(False, True)),
                               [t_inputs(c, xT, None, prm, None, 0, tabs) for c in cores], core_ids=cores).results
    for l in range(DEPTH):
        hf32 = np.concatenate([r["hf32"] for r in res], axis=1)
        htb = np.concatenate([r["htb"] for r in res], axis=0)
        cm, sm, lm = b_consts(l)
        maps = []
        for c in cores:
            cs = slice(c * 128, (c + 1) * 128)
            maps.append({"qsT": np.ascontiguousarray(hf32[0:1024][cs]), "zT": np.ascontiguousarray(hf32[1024:2048][cs]),
                         "sgT": np.ascontiguousarray(hf32[2048:3072][cs]), "v": np.ascontiguousarray(htb[:, cs]),
                         "lbl": np.ascontiguousarray(prm["hgrn_lb_logits"][:, cs].T), "lmask": lm,
                         "nw": np.ascontiguousarray(prm["hgrn_norm_w"][l][cs].reshape(128, 1)), "cmask": cm, "smask": sm})
        resB = run_bass_kernel_spmd(_prog("B", build_B), maps, core_ids=cores).results
        hfb = np.concatenate([r["hfb"] for r in res], axis=1)
        htg = np.concatenate([r["htg"] for r in res], axis=0)
        resN = run_bass_kernel_spmd(_prog("N", build_N), [n_inputs(c, hfb, htb, htg, prm, l, nconst) for c in cores], core_ids=cores).results
        oT = np.zeros((D, SEQ), NPBF)
        for c in cores:
            oT[c * 128:(c + 1) * 128] = resB[c]["ohT"]
            g, par = c // 2, c % 2
            on = resN[c]["onT"].reshape(256, NQB, 128)
            oT[1024 + g * 256:1024 + (g + 1) * 256].reshape(256, SEQ // 128, 128)[:, par::2, :] = on
        if l < DEPTH - 1:
            res = run_bass_kernel_spmd(_prog("CA", lambda: build_T(True, True)),
                                       [t_inputs(c, xT, oT, prm, l, l + 1, tabs) for c in cores], core_ids=cores).results
        else:
            res = run_bass_kernel_spmd(_prog("C", lambda: build_T(True, False)),
                                       [t_inputs(c, xT, oT, prm, l, None, tabs) for c in cores], core_ids=cores).results
        xT = np.concatenate([r["xT_out"] for r in res], axis=1)
        if _DBG is not None:
            _DBG[("oT", l)] = oT
            _DBG[("xT", l)] = xT
            if l == _DBG.get("stop", DEPTH):
                break
    return np.ascontiguousarray(xT.T)[None].astype(np.float32)
```

```python
import contextlib
import numpy as np
import ml_dtypes
import concourse.bass as bass
import concourse.mybir as mybir
from concourse.bass_utils import run_bass_kernel_spmd

F32 = mybir.dt.float32
BF16 = mybir.dt.bfloat16
AF = mybir.ActivationFunctionType
ALU = mybir.AluOpType
AX = mybir.AxisListType
NPBF = ml_dtypes.bfloat16

D = 2048
SEQ = 8192
DEPTH = 4
NCORE = 8
TPC = SEQ // NCORE
NT = TPC + 2
DFF = 5632
NIN = 6704
ALPHA = (2 * DEPTH) ** 0.25
LN_EPS = 1e-5
RMS_EPS = 1e-6
NF = 5120
NTM = 1584

ENGS = ("pe", "act", "dve", "pool", "sp")


class Op:
    __slots__ = ("eng", "fn", "reads", "writes", "dma", "deps", "need_inc", "ord", "dwait")

    def __init__(self, eng, fn, reads, writes, dma):
        self.eng, self.fn, self.reads, self.writes, self.dma = eng, fn, reads, writes, dma
        self.deps = []
        self.need_inc = False
        self.ord = 0


class Prog:
    def __init__(self, nc):
        self.nc = nc
        self.ops = []
        self.last_w = {}
        self.readers = {}
        self.dseen = {}

    def op(self, eng, fn, reads=(), writes=(), dma=None):
        o = Op(eng, fn, tuple(reads), tuple(writes), dma)
        deps = set()
        for k in o.reads:
            w = self.last_w.get(k)
            if w is not None:
                deps.add(w)
        for k in o.writes:
            w = self.last_w.get(k)
            if w is not None:
                deps.add(w)
            for r in self.readers.get(k, ()):
                deps.add(r)
        for k in o.reads:
            self.readers.setdefault(k, []).append(o)
        for k in o.writes:
            self.last_w[k] = o
            self.readers[k] = []
        deps.discard(o)
        o.deps = [d for d in deps if not (d.eng == "pe" and eng == "pe" and d.dma is None and dma is None)]
        for d in o.deps:
            d.need_inc = True
        o.dwait = {d.dma: self.dseen[d.dma] for d in o.deps if d.dma is not None}
        if dma is not None:
            self.dseen[dma] = self.dseen.get(dma, 0) + 1
        self.ops.append(o)
        return o

    def dma(self, out, in_, reads=(), writes=(), key="c", eng="sp"):
        return self.op(eng, lambda e: e.dma_start(out=out, in_=in_), reads, writes, dma=key)

    def emit(self):
        nc = self.nc
        cnt = {e: 0 for e in ENGS}
        dcnt = {}
        for o in self.ops:
            if o.dma is not None:
                dcnt[o.dma] = dcnt.get(o.dma, 0) + 1
                o.ord = dcnt[o.dma]
            elif o.need_inc:
                cnt[o.eng] += 1
                o.ord = cnt[o.eng]
        with contextlib.ExitStack() as st:
            esem = {e: st.enter_context(nc.semaphore("s_" + e)) for e in ENGS if e != "sp"}
            dsem = {k: st.enter_context(nc.semaphore("d_" + str(k))) for k in dcnt}
            block = st.enter_context(nc.Block())
            by_eng = {e: [o for o in self.ops if o.eng == e] for e in ENGS}

            def run(eng_name, e):
                waited = {}
                for o in by_eng[eng_name]:
                    need = {}
                    for d in o.deps:
                        if d.dma is not None:
                            s = ("d", d.dma)
                            v = 16 * o.dwait[d.dma]
                        else:
                            s = ("e", d.eng)
                            v = d.ord
                        if v > need.get(s, 0):
                            need[s] = v
                    for s, v in need.items():
                        if waited.get(s, 0) >= v:
                            continue
                        waited[s] = v
                        e.wait_ge(dsem[s[1]] if s[0] == "d" else esem[s[1]], v)
                    ins = o.fn(e)
                    if o.dma is not None:
                        ins.then_inc(dsem[o.dma], 16)
                    elif o.need_inc:
                        ins.then_inc(esem[o.eng], 1)
                if eng_name == "sp":
                    for k, n in dcnt.items():
                        e.wait_ge(dsem[k], 16 * n)

            block.tensor(lambda e: run("pe", e))
            block.scalar(lambda e: run("act", e))
            block.vector(lambda e: run("dve", e))
            block.gpsimd(lambda e: run("pool", e))
            block.sync(lambda e: run("sp", e))


class Ctx:
    def __init__(self, nc, st):
        self.nc, self.st = nc, st
        self.n = 0

    def sb(self, shape, dt, name=None):
        self.n += 1
        return self.st.enter_context(self.nc.sbuf_tensor(name or f"sb{self.n}", list(shape), dt))

    def ps(self, shape, dt, name=None):
        self.n += 1
        return self.st.enter_context(self.nc.psum_tensor(name or f"ps{self.n}", list(shape), dt))


def make_identity(P, C, dt=BF16, name="ident"):
    idf = C.sb([128, 128], F32, name + "f")
    ident = C.sb([128, 128], dt, name)
    P.op("pool", lambda e: e.memset(idf[:], 1.0), writes=[name + "f"])
    P.op("pool", lambda e: e.affine_select(out=idf[:], in_=idf[:], pattern=[[-1, 128]], compare_op=ALU.is_equal,
                                           fill=0.0, base=0, channel_multiplier=1), reads=[name + "f"], writes=[name + "f"])
    P.op("dve", lambda e: e.tensor_copy(out=ident[:], in_=idf[:]), reads=[name + "f"], writes=[name])
    return ident


class Linear:
    def __init__(self, P, C, name, KCmax, nbuf=3, npsum=2, pn=512):
        self.P, self.C, self.name = P, C, name
        self.w = [C.sb([128, KCmax, 128], BF16, f"{name}_w{i}") for i in range(nbuf)]
        self.ps = [C.ps([128, pn], F32, f"{name}_p{i}") for i in range(npsum)]
        self.nbuf, self.npsum = nbuf, npsum
        self.wi = 0
        self.pi = 0

    def load(self, W, KC, c0, cw):
        i = self.wi % self.nbuf
        self.wi += 1
        wt = self.w[i]
        key = (self.name, "w", i)
        self.P.dma(wt[:, 0:KC, 0:cw], W[:, c0:c0 + cw].rearrange("(kc p) n -> p kc n", p=128),
                   writes=[key], key=f"{self.name}w{i}", eng="pool")
        return wt, key

    def next_ps(self):
        i = self.pi % self.npsum
        self.pi += 1
        return self.ps[i], (self.name, "p", i)

    def run(self, W, KC, chunks, in_ap, in_key, groups, epilogue):
        P = self.P
        for ci, (c0, cw) in enumerate(chunks):
            wt, wkey = self.load(W, KC, c0, cw)
            for gi, (g0, gn) in enumerate(groups):
                pt, pkey = self.next_ps()
                for kc in range(KC):
                    P.op("pe", lambda e, kc=kc, wt=wt, pt=pt, g0=g0, gn=gn, cw=cw: e.matmul(
                        pt[0:cw, 0:gn], lhsT=wt[:, kc, 0:cw], rhs=in_ap(kc, g0, gn), start=(kc == 0), stop=(kc == KC - 1)),
                        reads=[wkey, in_key(kc)], writes=[pkey])
                epilogue(ci, gi, g0, gn, cw, pt, pkey)


GROUPS = [(0, 512), (512, 512), (1024, 2)]
MGROUPS = [(0, 512), (512, 512)]


def build_T(do_c, do_a, dbg_fm=40, dbg_tm=True, dbg_rope=True):
    nc = bass.Bass("TRN2", target_bir_lowering=False)
    dt_in = lambda n, s, d: nc.dram_tensor(n, list(s), d, kind="ExternalInput").ap()
    dt_out = lambda n, s, d: nc.dram_tensor(n, list(s), d, kind="ExternalOutput").ap()
    xT_d = dt_in("xT", [D, NT], F32)
    if do_c:
        oT_d = dt_in("oT", [D, NT], BF16)
        wout_d = dt_in("w_out", [D, D], F32)
        ln_d = dt_in("lnp", [128, 4, 16], F32)
        wup_d = dt_in("w_up", [D, 2 * DFF], F32)
        cw_d = dt_in("convp", [128, 4, 88], F32)
        wdn_d = dt_in("w_down", [DFF, D], F32)
        hmask_d = dt_in("hmask", [128, 2], F32)
        xo_d = dt_out("xT_out", [D, TPC], F32)
    if do_a:
        wf_d = dt_in("w_f", [D, NF], F32)
        wt_d = dt_in("w_t", [D, NTM], F32)
        cos_d = dt_in("cosT", [128, TPC], F32)
        sin_d = dt_in("sinT", [128, TPC], F32)
        rot_d = dt_in("rotm", [128, 128], F32)
        hf32_d = dt_out("hf32", [3072, TPC], F32)
        hfb_d = dt_out("hfb", [2048, TPC], BF16)
        htb_d = dt_out("htb", [TPC, 1536], BF16)
        htg_d = dt_out("htg", [TPC, 48], F32)

    with contextlib.ExitStack() as st:
        C = Ctx(nc, st)
        P = Prog(nc)
        xb = C.sb([128, 16, NT], BF16, "xb")
        ones = C.sb([128, 128], F32, "ones")
        P.op("pool", lambda e: e.memset(ones[:], 1.0), writes=["ones"])
        epsc = C.sb([128, 1], F32, "epsc")
        P.op("pool", lambda e: e.memset(epsc[:], LN_EPS), writes=["epsc"])
        L = Linear(P, C, "L", 16)

        def layer_norm(src_ap, src_key, gcol, bcol, lnp, out32, out32_key, outb, outb_key, groups, sq, st_ps, stat):
            for (g0, gn) in groups:
                ps_s, ps_q = st_ps
                for kc in range(16):
                    P.op("act", lambda e, kc=kc, g0=g0, gn=gn: e.activation(out=sq[:, 0:gn], in_=src_ap(kc, g0, gn), func=AF.Square),
                         reads=[src_key(kc)], writes=["lnsq"])
                    P.op("pe", lambda e, kc=kc, g0=g0, gn=gn: e.matmul(ps_s[:, 0:gn], lhsT=ones[:], rhs=src_ap(kc, g0, gn), start=(kc == 0), stop=(kc == 15)),
                         reads=["ones", src_key(kc)], writes=["ln_ps_s"])
                    P.op("pe", lambda e, kc=kc, g0=g0, gn=gn: e.matmul(ps_q[:, 0:gn], lhsT=ones[:], rhs=sq[:, 0:gn], start=(kc == 0), stop=(kc == 15)),
                         reads=["ones", "lnsq"], writes=["ln_ps_q"])
                mean, rstd, tmp = stat
                P.op("act", lambda e, gn=gn: e.mul(out=mean[:, 0:gn], in_=ps_s[:, 0:gn], mul=1.0 / D), reads=["ln_ps_s"], writes=["ln_mean"])
                P.op("dve", lambda e, gn=gn: e.tensor_tensor(out=tmp[:, 0:gn], in0=mean[:, 0:gn], in1=mean[:, 0:gn], op=ALU.mult), reads=["ln_mean"], writes=["ln_tmp"])
                P.op("act", lambda e, gn=gn: e.mul(out=rstd[:, 0:gn], in_=ps_q[:, 0:gn], mul=1.0 / D), reads=["ln_ps_q"], writes=["ln_rstd"])
                P.op("dve", lambda e, gn=gn: e.tensor_tensor(out=tmp[:, 0:gn], in0=rstd[:, 0:gn], in1=tmp[:, 0:gn], op=ALU.subtract),
                     reads=["ln_rstd", "ln_tmp"], writes=["ln_tmp"])
                P.op("act", lambda e, gn=gn: e.activation(out=rstd[:, 0:gn], in_=tmp[:, 0:gn], func=AF.Ln, bias=epsc[:, 0:1]), reads=["ln_tmp", "epsc"], writes=["ln_rstd"])
                P.op("act", lambda e, gn=gn: e.activation(out=rstd[:, 0:gn], in_=rstd[:, 0:gn], func=AF.Exp, scale=-0.5), reads=["ln_rstd"], writes=["ln_rstd"])
                for kc in range(16):
                    P.op("dve", lambda e, kc=kc, g0=g0, gn=gn: e.tensor_tensor(out=tmp[:, 0:gn], in0=src_ap(kc, g0, gn), in1=mean[:, 0:gn], op=ALU.subtract),
                         reads=[src_key(kc), "ln_mean"], writes=["ln_tmp"])
                    P.op("pool", lambda e, gn=gn: e.tensor_tensor(out=tmp[:, 0:gn], in0=tmp[:, 0:gn], in1=rstd[:, 0:gn], op=ALU.mult),
                         reads=["ln_tmp", "ln_rstd"], writes=["ln_tmp"])
                    P.op("act", lambda e, kc=kc, g0=g0, gn=gn: e.activation(out=out32(kc, g0, gn), in_=tmp[:, 0:gn], func=AF.Identity,
                                                                           scale=lnp[:, gcol, kc:kc + 1], bias=lnp[:, bcol, kc:kc + 1]),
                         reads=["ln_tmp", "lnp"], writes=[out32_key(kc)])
                    P.op("dve", lambda e, kc=kc, g0=g0, gn=gn: e.tensor_copy(out=outb(kc, g0, gn), in_=out32(kc, g0, gn)),
                         reads=[out32_key(kc)], writes=[outb_key(kc)])

        if do_c:
            x32 = C.sb([128, 16, NT], F32, "x32")
            lnp = C.sb([128, 4, 16], F32, "lnp_sb")
            cvp = C.sb([128, 4, 88], F32, "cvp")
            hmask = C.sb([128, 2], F32, "hmask_sb")
            P.dma(lnp[:], ln_d, writes=["lnp"])
            P.dma(cvp[:], cw_d, writes=["cvp"])
            P.dma(hmask[:], hmask_d, writes=["hmask"])
            for kc in range(16):
                P.dma(x32[:, kc, :], xT_d[kc * 128:(kc + 1) * 128, :], writes=[("x32", kc)])
                P.dma(xb[:, kc, :], oT_d[kc * 128:(kc + 1) * 128, :], writes=[("xb", kc)])
            sq = C.sb([128, 512], F32, "lnsq")
            st_ps = (C.ps([128, 512], F32, "ln_ps_s"), C.ps([128, 512], F32, "ln_ps_q"))
            stat = (C.sb([128, 512], F32, "ln_mean"), C.sb([128, 512], F32, "ln_rstd"), C.sb([128, 512], F32, "ln_tmp"))

            evb = [C.sb([128, 512], F32, f"evb{i}") for i in range(2)]
            evc = [0]

            def evac(pt, pkey, gn):
                i = evc[0] % 2
                evc[0] += 1
                P.op("act", lambda e: e.copy(out=evb[i][:, 0:gn], in_=pt[:, 0:gn]), reads=[pkey], writes=[("evb", i)])
                return evb[i], ("evb", i)

            def ep_out(ci, gi, g0, gn, cw, pt, pkey):
                ev, ekey = evac(pt, pkey, gn)
                P.op("dve", lambda e: e.scalar_tensor_tensor(out=x32[:, ci, g0:g0 + gn], in0=x32[:, ci, g0:g0 + gn], scalar=ALPHA,
                                                             in1=ev[:, 0:gn], op0=ALU.mult, op1=ALU.add),
                     reads=[("x32", ci), ekey], writes=[("x32", ci)])
            L.run(wout_d, 16, [(c * 128, 128) for c in range(16)], lambda kc, g0, gn: xb[:, kc, g0:g0 + gn], lambda kc: ("xb", kc), GROUPS, ep_out)
            layer_norm(lambda kc, g0, gn: x32[:, kc, g0:g0 + gn], lambda kc: ("x32", kc), 0, 1, lnp,
                       lambda kc, g0, gn: x32[:, kc, g0:g0 + gn], lambda kc: ("x32", kc),
                       lambda kc, g0, gn: xb[:, kc, g0:g0 + gn], lambda kc: ("xb", kc), GROUPS, sq, st_ps, stat)
            NQ = 11
            aT = C.sb([128, NQ, TPC], BF16, "aT")
            hbuf = [C.sb([128, 2 + TPC], F32, f"hbuf{i}") for i in range(2)]
            cacc = [C.sb([128, TPC], F32, f"cacc{i}") for i in range(2)]
            for q in range(4):
                def ep_up(ci, gi, g0, gn, cw, pt, pkey, q=q):
                    jl, half = ci // 2, ci % 2
                    j = q * NQ + jl
                    hb = hbuf[half]
                    ch = j + 44 * half
                    hkeys = [("hb", half, 0), ("hb", half, 1), ("hb", half, 2)]
                    if g0 == TPC:
                        P.op("act", lambda e: e.copy(out=hb[:, 0:2], in_=pt[:, 0:2]), reads=[pkey], writes=[("hb", half, 2)])
                        P.op("dve", lambda e: e.tensor_tensor(out=hb[:, 0:2], in0=hb[:, 0:2], in1=hmask[:, 0:2], op=ALU.mult),
                             reads=[("hb", half, 2), "hmask"], writes=[("hb", half, 2)])
                        acc = cacc[half]
                        P.op("dve", lambda e: e.tensor_scalar(out=acc[:], in0=hb[:, 2:2 + TPC], scalar1=cvp[:, 2, ch:ch + 1], scalar2=cvp[:, 3, ch:ch + 1],
                                                              op0=ALU.mult, op1=ALU.add),
                             reads=hkeys + ["cvp"], writes=[("cacc", half)])
                        P.op("dve", lambda e: e.scalar_tensor_tensor(out=acc[:], in0=hb[:, 1:1 + TPC], scalar=cvp[:, 1, ch:ch + 1], in1=acc[:],
                                                                      op0=ALU.mult, op1=ALU.add),
                             reads=hkeys + ["cvp", ("cacc", half)], writes=[("cacc", half)])
                        P.op("dve", lambda e: e.scalar_tensor_tensor(out=acc[:], in0=hb[:, 0:TPC], scalar=cvp[:, 0, ch:ch + 1], in1=acc[:],
                                                                     op0=ALU.mult, op1=ALU.add),
                             reads=hkeys + ["cvp", ("cacc", half)], writes=[("cacc", half)])
                        if half == 0:
                            P.op("act", lambda e: e.activation(out=acc[:], in_=acc[:], func=AF.Silu), reads=[("cacc", 0)], writes=[("cacc", 0)])
                        else:
                            P.op("pool", lambda e: e.tensor_tensor(out=aT[:, jl, :], in0=cacc[0][:], in1=cacc[1][:], op=ALU.mult),
                                 reads=[("cacc", 0), ("cacc", 1)], writes=[("aT", jl)])
                    else:
                        P.op("act", lambda e: e.copy(out=hb[:, 2 + g0:2 + g0 + gn], in_=pt[:, 0:gn]), reads=[pkey], writes=[("hb", half, gi)])
                up_chunks = []
                for jl in range(NQ):
                    j = q * NQ + jl
                    up_chunks += [(j * 128, 128), ((44 + j) * 128, 128)]
                L.run(wup_d, 16, up_chunks, lambda kc, g0, gn: xb[:, kc, g0:g0 + gn], lambda kc: ("xb", kc), GROUPS, ep_up)

                def ep_dn(ci, gi, g0, gn, cw, pt, pkey, q=q):
                    ev, ekey = evac(pt, pkey, gn)
                    if q == 0:
                        P.op("dve", lambda e: e.scalar_tensor_tensor(out=x32[:, ci, g0:g0 + gn], in0=x32[:, ci, g0:g0 + gn], scalar=ALPHA,
                                                                     in1=ev[:, 0:gn], op0=ALU.mult, op1=ALU.add),
                             reads=[("x32", ci), ekey], writes=[("x32", ci)])
                    else:
                        P.op("dve", lambda e: e.tensor_tensor(out=x32[:, ci, g0:g0 + gn], in0=x32[:, ci, g0:g0 + gn], in1=ev[:, 0:gn], op=ALU.add),
                             reads=[("x32", ci), ekey], writes=[("x32", ci)])
                L.run(wdn_d[q * NQ * 128:(q + 1) * NQ * 128, :], NQ, [(c * 128, 128) for c in range(16)],
                      lambda kc, g0, gn: aT[:, kc, g0:g0 + gn], lambda kc: ("aT", kc), MGROUPS, ep_dn)
            layer_norm(lambda kc, g0, gn: x32[:, kc, g0:g0 + gn], lambda kc: ("x32", kc), 2, 3, lnp,
                       lambda kc, g0, gn: x32[:, kc, g0:g0 + gn], lambda kc: ("x32", kc),
                       lambda kc, g0, gn: xb[:, kc, g0:g0 + gn], lambda kc: ("xb", kc), MGROUPS, sq, st_ps, stat)
            for kc in range(16):
                P.dma(xo_d[kc * 128:(kc + 1) * 128, :], x32[:, kc, 0:TPC], reads=[("x32", kc)], key="xo")
        else:
            xs = C.sb([128, NT], F32, "xs")
            for kc in range(16):
                P.dma(xs[:], xT_d[kc * 128:(kc + 1) * 128, :], writes=["xs"], key="xs")
                P.op("dve", lambda e, kc=kc: e.tensor_copy(out=xb[:, kc, :], in_=xs[:]), reads=["xs"], writes=[("xb", kc)])

        if do_a:
            cosT = C.sb([128, TPC], F32, "cos_sb")
            sinT = C.sb([128, TPC], F32, "sin_sb")
            rotf = C.sb([128, 128], F32, "rotf")
            rotm = C.sb([128, 128], BF16, "rotm_sb")
            P.dma(cosT[:], cos_d, writes=["cosT"])
            P.dma(sinT[:], sin_d, writes=["sinT"])
            P.dma(rotf[:], rot_d, writes=["rotf"])
            P.op("dve", lambda e: e.tensor_copy(out=rotm[:], in_=rotf[:]), reads=["rotf"], writes=["rotm"])
            o32 = [C.sb([128, 512], F32, f"ao32_{i}") for i in range(2)]
            obf = [C.sb([128, 512], BF16, f"aobf_{i}") for i in range(2)]
            rb = C.sb([128, 512], BF16, "rope_b")
            rt = C.sb([128, 512], F32, "rope_t")
            rps = C.ps([128, 512], F32, "rope_ps")
            cnt = [0]

            def ep_a(ci, gi, g0, gn, cw, pt, pkey):
                i = cnt[0] % 2
                cnt[0] += 1
                if ci < 24:
                    o = o32[i]
                    okey = ("ao32", i)
                    if ci < 8 or ci >= 16:
                        P.op("act", lambda e: e.activation(out=o[:, 0:gn], in_=pt[:, 0:gn], func=AF.Silu), reads=[pkey], writes=[okey])
                    else:
                        P.op("act", lambda e: e.copy(out=o[:, 0:gn], in_=pt[:, 0:gn]), reads=[pkey], writes=[okey])
                    P.dma(hf32_d[ci * 128:(ci + 1) * 128, g0:g0 + gn], o[:, 0:gn], reads=[okey], key=f"ao{i}")
                else:
                    o = obf[i]
                    okey = ("aobf", i)
                    if ci < 38 and dbg_rope:
                        P.op("act", lambda e: e.copy(out=rb[:, 0:gn], in_=pt[:, 0:gn]), reads=[pkey], writes=["rope_b"])
                        P.op("pe", lambda e: e.matmul(rps[:, 0:gn], lhsT=rotm[:], rhs=rb[:, 0:gn], start=True, stop=True), reads=["rotm", "rope_b"], writes=["rope_ps"])
                        P.op("act", lambda e: e.copy(out=rt[:, 0:gn], in_=pt[:, 0:gn]), reads=[pkey], writes=["rope_t"])
                        P.op("act", lambda e: e.copy(out=o32[i][:, 0:gn], in_=rps[:, 0:gn]), reads=["rope_ps"], writes=[("ao32", i)])
                        P.op("dve", lambda e: e.tensor_tensor(out=rt[:, 0:gn], in0=rt[:, 0:gn], in1=cosT[:, g0:g0 + gn], op=ALU.mult), reads=["rope_t", "cosT"], writes=["rope_t"])
                        P.op("pool", lambda e: e.tensor_tensor(out=o32[i][:, 0:gn], in0=o32[i][:, 0:gn], in1=sinT[:, g0:g0 + gn], op=ALU.mult), reads=[("ao32", i), "sinT"], writes=[("ao32", i)])
                        P.op("dve", lambda e: e.tensor_tensor(out=o[:, 0:gn], in0=rt[:, 0:gn], in1=o32[i][:, 0:gn], op=ALU.add), reads=["rope_t", ("ao32", i)], writes=[okey])
                    else:
                        P.op("act", lambda e: e.copy(out=o[:, 0:gn], in_=pt[:, 0:gn]), reads=[pkey], writes=[okey])
                    r0 = (ci - 24) * 128
                    P.dma(hfb_d[r0:r0 + 128, g0:g0 + gn], o[:, 0:gn], reads=[okey], key=f"ab{i}")
            L.run(wf_d, 16, [(c * 128, 128) for c in range(dbg_fm)], lambda kc, g0, gn: xb[:, kc, g0:g0 + gn], lambda kc: ("xb", kc), MGROUPS, ep_a)

            wtm = [C.sb([128, 16, 256], BF16, f"wtm{i}") for i in range(2)]
            tps = [C.ps([128, 512], F32, f"tm_ps{i}") for i in range(2)]
            tob = [C.sb([128, 256], BF16, f"tm_ob{i}") for i in range(2)]
            tog = C.sb([128, 48], F32, "tm_og")
            k = 0
            for bi, (c0, cw) in enumerate(([(i * 256, 256) for i in range(6)] + [(1536, 48)]) if dbg_tm else []):
                wt = wtm[bi % 2]
                wkey = ("wtm", bi % 2)
                P.dma(wt[:, :, 0:cw], wt_d[:, c0:c0 + cw].rearrange("(kc p) n -> p kc n", p=128), writes=[wkey], key=f"wtm{bi % 2}", eng="pool")
                for tt in range(TPC // 128):
                    pt = tps[k % 2]
                    pkey = ("tm_ps", k % 2)
                    for kc in range(16):
                        P.op("pe", lambda e, kc=kc, wt=wt, pt=pt, tt=tt, cw=cw: e.matmul(pt[:, 0:cw], lhsT=xb[:, kc, tt * 128:(tt + 1) * 128], rhs=wt[:, kc, 0:cw],
                                                                                         start=(kc == 0), stop=(kc == 15)),
                             reads=[wkey, ("xb", kc)], writes=[pkey])
                    if cw == 256:
                        o = tob[k % 2]
                        okey = ("tm_ob", k % 2)
                        P.op("act", lambda e, o=o, pt=pt: e.copy(out=o[:], in_=pt[:, 0:256]), reads=[pkey], writes=[okey])
                        P.dma(htb_d[tt * 128:(tt + 1) * 128, c0:c0 + 256], o[:], reads=[okey], key=f"tmo{k % 2}")
                    else:
                        P.op("act", lambda e, pt=pt: e.activation(out=tog[:], in_=pt[:, 0:48], func=AF.Sigmoid), reads=[pkey], writes=["tm_og"])
                        P.dma(htg_d[tt * 128:(tt + 1) * 128, :], tog[:], reads=["tm_og"], key="tmg")
                    k += 1
        P.emit()
    return nc


FM_COLS = np.concatenate([np.arange(0, 1024), np.arange(1024, 2048), np.arange(3072, 4096), np.arange(4096, 5120),
                          np.arange(5120, 5376), np.arange(5632, 5888), np.arange(6144, 6400), np.arange(5376, 5632)])
TM_COLS = np.concatenate([np.arange(2048, 3072), np.arange(5888, 6144), np.arange(6400, 6656), np.arange(6656, 6704)])


def rope_tables():
    half = 32
    inv = (10000.0 ** (-np.arange(half, dtype=np.float32) / half)).astype(np.float32)
    pos = np.arange(SEQ, dtype=np.float32)
    ang = pos[None, :] * inv[:, None]
    cos, sin = np.cos(ang).astype(np.float32), np.sin(ang).astype(np.float32)
    cosT = np.concatenate([cos, cos, cos, cos], 0)
    sinT = np.concatenate([-sin, sin, -sin, sin], 0)
    rot = np.zeros((128, 128), np.float32)
    for m in range(128):
        k = m + 32 if (m % 64) < 32 else m - 32
        rot[k, m] = 1.0
    return cosT, sinT, rot


def with_halo(aT, c):
    s = c * TPC
    out = np.zeros((aT.shape[0], NT), aT.dtype)
    out[:, :TPC] = aT[:, s:s + TPC]
    if c > 0:
        out[:, TPC:] = aT[:, s - 2:s]
    return out


def chunked(v):
    return np.ascontiguousarray(v.reshape(-1, 128).T)


def t_inputs(c, xT, oT, prm, lc, la, tabs):
    m = {"xT": with_halo(xT, c)}
    if lc is not None:
        m["oT"] = with_halo(oT, c)
        m["w_out"] = prm["w_out"][lc]
        m["lnp"] = np.ascontiguousarray(np.stack([chunked(prm["ln1_g"][lc]), chunked(prm["ln1_b"][lc]),
                                                  chunked(prm["ln2_g"][lc]), chunked(prm["ln2_b"][lc])], 1))
        m["w_up"] = prm["w_up"][lc]
        m["convp"] = np.ascontiguousarray(np.stack([chunked(prm["conv_w"][lc][0]), chunked(prm["conv_w"][lc][1]),
                                                    chunked(prm["conv_w"][lc][2]), chunked(prm["conv_b"][lc])], 1))
        m["w_down"] = prm["w_down"][lc]
        m["hmask"] = np.full((128, 2), 0.0 if c == 0 else 1.0, np.float32)
    if la is not None:
        cosT, sinT, rot = tabs
        m["w_f"] = prm["w_f"][la]
        m["w_t"] = prm["w_t"][la]
        m["cosT"] = np.ascontiguousarray(cosT[:, c * TPC:(c + 1) * TPC])
        m["sinT"] = np.ascontiguousarray(sinT[:, c * TPC:(c + 1) * TPC])
        m["rotm"] = rot
    return m


def build_B():
    nc = bass.Bass("TRN2", target_bir_lowering=False)
    dt_in = lambda n, s, d: nc.dram_tensor(n, list(s), d, kind="ExternalInput").ap()
    qs_d = dt_in("qsT", [128, SEQ], F32)
    z_d = dt_in("zT", [128, SEQ], F32)
    sg_d = dt_in("sgT", [128, SEQ], F32)
    v_d = dt_in("v", [SEQ, 128], BF16)
    lbl_d = dt_in("lbl", [128, 4], F32)
    lmask_d = dt_in("lmask", [128, 4], F32)
    nw_d = dt_in("nw", [128, 1], F32)
    cmask_d = dt_in("cmask", [64, 512], F32)
    smask_d = dt_in("smask", [128, 2048], F32)
    oh_d = nc.dram_tensor("ohT", [128, SEQ], BF16, kind="ExternalOutput").ap()
    NCH = SEQ // 64
    with contextlib.ExitStack() as st:
        C = Ctx(nc, st)
        P = Prog(nc)
        ident = make_identity(P, C)
        ones = C.sb([128, 128], F32, "ones")
        P.op("pool", lambda e: e.memset(ones[:], 1.0), writes=["ones"])
        epsc = C.sb([128, 1], F32, "epsc")
        P.op("pool", lambda e: e.memset(epsc[:], RMS_EPS), writes=["epsc"])
        lbl = C.sb([128, 4], F32, "lbl_sb")
        lmask = C.sb([128, 4], F32, "lmask_sb")
        nw = C.sb([128, 1], F32, "nw_sb")
        cmask = C.sb([64, 512], F32, "cmask_sb")
        smask = C.sb([128, 2048], F32, "smask_sb")
        vt = C.sb([64, NCH, 128], BF16, "vt")
        for t_, d_, k_ in ((lbl, lbl_d, "lbl"), (lmask, lmask_d, "lmask"), (nw, nw_d, "nw"), (cmask, cmask_d, "cmask"), (smask, smask_d, "smask")):
            P.dma(t_[:], d_, writes=[k_])
        P.dma(vt[:], v_d.rearrange("(c s) v -> s c v", s=64), writes=["vt"])
        sm = C.sb([128, 8], F32, "sm")
        lb = C.sb([128, 4], F32, "lb")
        P.op("act", lambda e: e.activation(out=sm[:, 0:4], in_=lbl[:], func=AF.Exp), reads=["lbl"], writes=["sm0"])
        P.op("dve", lambda e: e.tensor_reduce(out=lb[:, 0:1], in_=sm[:, 0:4], axis=AX.X, op=ALU.add), reads=["sm0"], writes=["lb0"])
        P.op("dve", lambda e: e.tensor_tensor(out=sm[:, 4:8], in0=sm[:, 0:4], in1=lmask[:], op=ALU.mult), reads=["sm0", "lmask"], writes=["sm1"])
        P.op("dve", lambda e: e.tensor_reduce(out=lb[:, 1:2], in_=sm[:, 4:8], axis=AX.X, op=ALU.add), reads=["sm1"], writes=["lb1"])
        P.op("dve", lambda e: e.reciprocal(out=lb[:, 0:1], in_=lb[:, 0:1]), reads=["lb0"], writes=["lb0"])
        P.op("dve", lambda e: e.tensor_tensor(out=lb[:, 2:3], in0=lb[:, 1:2], in1=lb[:, 0:1], op=ALU.mult), reads=["lb0", "lb1"], writes=["lb2"])
        P.op("dve", lambda e: e.tensor_scalar(out=lb[:, 3:4], in0=lb[:, 2:3], scalar1=-1.0, scalar2=1.0, op0=ALU.mult, op1=ALU.add), reads=["lb2"], writes=["lb3"])
        LBK = ["lb2", "lb3"]

        qb = C.sb([128, SEQ], BF16, "qb")
        kb = C.sb([128, SEQ], BF16, "kb")
        ebl = C.sb([128, NCH], F32, "ebl")
        W = 2048
        zt = C.sb([128, W], F32, "zt")
        qt = C.sb([128, W], F32, "qt")
        t1 = C.sb([128, W], F32, "t1")
        t2 = C.sb([128, W], F32, "t2")
        t3 = C.sb([128, W], F32, "t3")
        for bi in range(SEQ // W):
            c0 = bi * W
            P.dma(zt[:], z_d[:, c0:c0 + W], writes=["zt"], key="zt")
            P.dma(qt[:], qs_d[:, c0:c0 + W], writes=["qt"], key="qt")
            P.op("act", lambda e: e.activation(out=t1[:], in_=zt[:], func=AF.Sigmoid), reads=["zt"], writes=["t1"])
            P.op("dve", lambda e: e.tensor_scalar(out=t1[:], in0=t1[:], scalar1=lb[:, 3:4], scalar2=lb[:, 2:3], op0=ALU.mult, op1=ALU.add), reads=["t1"] + LBK, writes=["t1"])
            P.op("dve", lambda e: e.tensor_scalar_max(out=t1[:], in0=t1[:], scalar1=1e-30), reads=["t1"], writes=["t1"])
            P.op("act", lambda e: e.activation(out=t1[:], in_=t1[:], func=AF.Ln), reads=["t1"], writes=["t1"])
            P.op("dve", lambda e: e.tensor_tensor_scan(out=t2[:], data0=smask[:], data1=t1[:], initial=0.0, op0=ALU.mult, op1=ALU.add), reads=["t1", "smask"], writes=["t2"])
            P.op("act", lambda e: e.activation(out=t1[:], in_=t2[:], func=AF.Exp), reads=["t2"], writes=["t1"])
            P.op("act", lambda e: e.activation(out=t3[:], in_=t2[:], func=AF.Exp, scale=-1.0), reads=["t2"], writes=["t3"])
            P.op("act", lambda e: e.activation(out=t2[:], in_=zt[:], func=AF.Sigmoid, scale=-1.0), reads=["zt", "t2"], writes=["t2"])
            P.op("dve", lambda e: e.tensor_scalar(out=t2[:], in0=t2[:], scalar1=lb[:, 3:4], scalar2=None, op0=ALU.mult), reads=["t2"] + LBK, writes=["t2"])
            P.op("pool", lambda e, c0=c0: e.tensor_tensor(out=kb[:, c0:c0 + W], in0=t2[:], in1=t3[:], op=ALU.mult), reads=["t2", "t3"], writes=[("kb", bi)])
            P.op("pool", lambda e, c0=c0: e.tensor_tensor(out=qb[:, c0:c0 + W], in0=qt[:], in1=t1[:], op=ALU.mult), reads=["qt", "t1"], writes=[("qb", bi)])
            P.op("dve", lambda e, bi=bi: e.tensor_copy(out=ebl[:, bi * 32:(bi + 1) * 32], in_=t1[:].rearrange("p (c s) -> p c s", s=64)[:, :, 63]),
                 reads=["t1"], writes=[("ebl", bi)])

        state = C.sb([128, 128], F32, "state")
        stmp = C.sb([128, 128], F32, "stmp")
        P.op("pool", lambda e: e.memset(state[:], 0.0), writes=["state"])
        sbf = [C.sb([128, 128], BF16, f"sbf{i}") for i in range(8)]
        attp = C.ps([64, 512], F32, "attp")
        ktp = C.ps([64, 8, 128], BF16, "ktp")
        up = [C.ps([128, 4, 128], F32, f"up{i}") for i in range(2)]
        outp = C.ps([128, 512], F32, "outp")
        ssp = C.ps([128, 512], F32, "ssp")
        atts = C.sb([64, 512], F32, "atts")
        attm = C.sb([64, 512], BF16, "attm")
        kbt = C.sb([64, 8, 128], BF16, "kbt")
        us = C.sb([128, 8, 128], F32, "us")
        os_ = C.sb([128, 512], F32, "os")
        osq = C.sb([128, 512], F32, "osq")
        rr = C.sb([128, 512], F32, "rr")
        sgt = [C.sb([128, 512], F32, f"sgt{i}") for i in range(2)]
        obo = [C.sb([128, 512], BF16, f"obo{i}") for i in range(2)]
        for g in range(SEQ // 512):
            t0 = g * 512
            bi = t0 // W
            sg = sgt[g % 2]
            P.dma(sg[:], sg_d[:, t0:t0 + 512], writes=[("sgt", g % 2)], key=f"sgt{g % 2}")
            for ch in range(8):
                a = t0 + ch * 64
                P.op("pe", lambda e, a=a, ch=ch: e.matmul(attp[:, ch * 64:(ch + 1) * 64], lhsT=kb[:, a:a + 64], rhs=qb[:, a:a + 64], start=True, stop=True),
                     reads=[("kb", bi), ("qb", bi)], writes=["attp"])
            for ch in range(8):
                a = t0 + ch * 64
                P.op("pe", lambda e, a=a, ch=ch: e.transpose(out=ktp[:, ch, :], in_=kb[:, a:a + 64], identity=ident[:]),
                     reads=[("kb", bi), "ident"], writes=["ktp"])
            P.op("act", lambda e: e.copy(out=atts[:], in_=attp[:]), reads=["attp"], writes=["atts"])
            P.op("pool", lambda e: e.tensor_tensor(out=attm[:], in0=atts[:], in1=cmask[:], op=ALU.mult), reads=["atts", "cmask"], writes=["attm"])
            P.op("act", lambda e: e.copy(out=kbt[:], in_=ktp[:]), reads=["ktp"], writes=["kbt"])
            for ch in range(8):
                cidx = g * 8 + ch
                P.op("pe", lambda e, ch=ch, cidx=cidx: e.matmul(up[ch // 4][:, ch % 4, :], lhsT=kbt[:, ch, :], rhs=vt[:, cidx, :], start=True, stop=True),
                     reads=["kbt", "vt"], writes=[("up", ch // 4)])
            for h2 in range(2):
                P.op("act", lambda e, h2=h2: e.copy(out=us[:, h2 * 4:(h2 + 1) * 4, :], in_=up[h2][:]), reads=[("up", h2)], writes=[("us", h2)])
            for ch in range(8):
                cidx = g * 8 + ch
                P.op("pool", lambda e, ch=ch: e.tensor_copy(out=sbf[ch][:], in_=state[:]), reads=["state"], writes=[("sbf", ch)])
                P.op("dve", lambda e, ch=ch: e.tensor_tensor(out=stmp[:], in0=state[:], in1=us[:, ch, :], op=ALU.add), reads=["state", ("us", ch // 4)], writes=["stmp"])
                P.op("dve", lambda e, cidx=cidx: e.tensor_scalar(out=state[:], in0=stmp[:], scalar1=ebl[:, cidx:cidx + 1], scalar2=None, op0=ALU.mult),
                     reads=["stmp", ("ebl", cidx // 32)], writes=["state"])
            for ch in range(8):
                a = t0 + ch * 64
                cidx = g * 8 + ch
                P.op("pe", lambda e, a=a, ch=ch: e.matmul(outp[:, ch * 64:(ch + 1) * 64], lhsT=sbf[ch][:], rhs=qb[:, a:a + 64], start=True, stop=False),
                     reads=[("sbf", ch), ("qb", bi)], writes=["outp"])
                P.op("pe", lambda e, ch=ch, cidx=cidx: e.matmul(outp[:, ch * 64:(ch + 1) * 64], lhsT=vt[:, cidx, :], rhs=attm[:, ch * 64:(ch + 1) * 64], start=False, stop=True),
                     reads=["vt", "attm"], writes=["outp"])
            P.op("act", lambda e: e.copy(out=os_[:], in_=outp[:]), reads=["outp"], writes=["os"])
            P.op("act", lambda e: e.activation(out=osq[:], in_=outp[:], func=AF.Square), reads=["outp"], writes=["osq"])
            P.op("pe", lambda e: e.matmul(ssp[:], lhsT=ones[:], rhs=osq[:], start=True, stop=True), reads=["ones", "osq"], writes=["ssp"])
            P.op("act", lambda e: e.activation(out=rr[:], in_=ssp[:], func=AF.Ln, scale=1.0 / 128, bias=epsc[:, 0:1]), reads=["ssp", "epsc"], writes=["rr"])
            P.op("act", lambda e: e.activation(out=rr[:], in_=rr[:], func=AF.Exp, scale=-0.5), reads=["rr"], writes=["rr"])
            P.op("dve", lambda e: e.tensor_tensor(out=os_[:], in0=os_[:], in1=rr[:], op=ALU.mult), reads=["os", "rr"], writes=["os"])
            P.op("dve", lambda e: e.tensor_scalar(out=os_[:], in0=os_[:], scalar1=nw[:, 0:1], scalar2=None, op0=ALU.mult), reads=["os", "nw"], writes=["os"])
            ob_ = obo[g % 2]
            P.op("pool", lambda e, ob_=ob_, sg=sg: e.tensor_tensor(out=ob_[:], in0=os_[:], in1=sg[:], op=ALU.mult), reads=["os", ("sgt", g % 2)], writes=[("obo", g % 2)])
            P.dma(oh_d[:, t0:t0 + 512], ob_[:], reads=[("obo", g % 2)], key=f"oh{g % 2}")
        P.emit()
    return nc


def b_consts(layer):
    cm = np.zeros((64, 8, 64), np.float32)
    s = np.arange(64)
    cm[:] = (s[:, None] <= s[None, :]).astype(np.float32)[:, None, :]
    sm = np.ones((128, 2048), np.float32)
    sm[:, ::64] = 0.0
    lm = np.zeros((128, 4), np.float32)
    lm[:, 1:layer + 1] = 1.0
    return cm.reshape(64, 512), sm, lm


NQB = 32


def build_N(blocks=None):
    parity = 1
    nc = bass.Bass("TRN2", target_bir_lowering=False)
    dt_in = lambda n, s, d: nc.dram_tensor(n, list(s), d, kind="ExternalInput").ap()
    q_d = dt_in("qsel", [64, 4, NQB * 128], BF16)
    kc_d = dt_in("kcT", [64, SEQ], BF16)
    vc_d = dt_in("vcT", [64, SEQ], BF16)
    ks_d = dt_in("ksT", [64, SEQ], BF16)
    kw_d = dt_in("kwT", [64, SEQ], BF16)
    vs_d = dt_in("vs", [SEQ, 65], BF16)
    vw_d = dt_in("vw", [SEQ, 65], BF16)
    f0_d = dt_in("F0", [128, 128], F32)
    vm_d = dt_in("vmask", [128, 4], F32)
    gt_d = dt_in("gsel", [128, NQB, 12], F32)
    pek_d = dt_in("pekT", [64, 32], F32)
    pev_d = dt_in("pevT", [64, 32], F32)
    w1k_d = dt_in("w1k", [2048, 256], F32)
    w1v_d = dt_in("w1v", [2048, 256], F32)
    w2k_d = dt_in("w2k", [256, 64], F32)
    w2v_d = dt_in("w2v", [256, 64], F32)
    ovl_d = dt_in("ovl", [128, 4, 128], F32)
    E_d = dt_in("Etab", [128, 64, 128], F32)
    mm_d = dt_in("Mm", [128, 254], F32)
    ma_d = dt_in("Ma", [128, 254], F32)
    on_d = nc.dram_tensor("onT", [256, NQB * 128], BF16, kind="ExternalOutput").ap()
    with contextlib.ExitStack() as st:
        C = Ctx(nc, st)
        P = Prog(nc)
        ident = make_identity(P, C)
        q_sb = C.sb([64, 4, NQB * 128], BF16, "q_sb")
        kcT = C.sb([64, SEQ], BF16, "kcT_sb")
        vcT = C.sb([64, SEQ], BF16, "vcT_sb")
        ksT = C.sb([64, SEQ], BF16, "ksT_sb")
        kwT = C.sb([64, SEQ], BF16, "kwT_sb")
        vs_sb = C.sb([128, 64, 65], BF16, "vs_sb")
        vw_sb = C.sb([128, 64, 65], BF16, "vw_sb")
        gsb = C.sb([128, NQB, 12], F32, "gsb")
        pek = C.sb([64, 32], BF16, "pek")
        pev = C.sb([64, 32], BF16, "pev")
        w1k = C.sb([64, 32, 256], BF16, "w1k_sb")
        w1v = C.sb([64, 32, 256], BF16, "w1v_sb")
        w2k = C.sb([128, 2, 64], BF16, "w2k_sb")
        w2v = C.sb([128, 2, 64], BF16, "w2v_sb")
        ovl = C.sb([128, 4, 128], BF16, "ovl_sb")
        Et = C.sb([128, 64, 128], BF16, "E_sb")
        Mm = C.sb([128, 254], F32, "Mm_sb")
        Ma = C.sb([128, 254], F32, "Ma_sb")
        for t_, d_, k_ in ((q_sb, q_d, "q"), (kcT, kc_d, "kcT"), (vcT, vc_d, "vcT"), (ksT, ks_d, "ksT"), (kwT, kw_d, "kwT"),
                           (gsb, gt_d, "gsb"), (Mm, mm_d, "Mm"), (Ma, ma_d, "Ma")):
            P.dma(t_[:], d_, writes=[k_])
        P.dma(vs_sb[:], vs_d.rearrange("(kt p) d -> p kt d", p=128), writes=["vs"])
        P.dma(vw_sb[:], vw_d.rearrange("(kt p) d -> p kt d", p=128), writes=["vw"])
        F0 = C.sb([128, 128], F32, "F0_sb")
        vmask = C.sb([128, 4], F32, "vmask_sb")
        P.dma(F0[:], f0_d, writes=["F0"])
        P.dma(vmask[:], vm_d, writes=["vmask"])
        P.dma(pek[:], pek_d, writes=["pek"], key="cc", eng="pool")
        P.dma(pev[:], pev_d, writes=["pev"], key="cc", eng="pool")
        P.dma(w1k[:], w1k_d.rearrange("(j d) h -> d j h", d=64), writes=["w1k"], key="cc", eng="pool")
        P.dma(w1v[:], w1v_d.rearrange("(j d) h -> d j h", d=64), writes=["w1v"], key="cc", eng="pool")
        P.dma(w2k[:], w2k_d.rearrange("(c p) d -> p c d", p=128), writes=["w2k"], key="cc", eng="pool")
        P.dma(w2v[:], w2v_d.rearrange("(c p) d -> p c d", p=128), writes=["w2v"], key="cc", eng="pool")
        P.dma(ovl[:], ovl_d, writes=["ovl"], key="cc", eng="pool")
        P.dma(Et[:], E_d, writes=["E"], key="cc", eng="pool")

        sp = [C.ps([128, 512], F32, f"sp{i}") for i in range(2)]
        accC = C.ps([128, 512], F32, "accC")[:, 0:260].rearrange("p (h d) -> p h d", d=65)
        accS = C.ps([128, 512], F32, "accS")[:, 0:260].rearrange("p (h d) -> p h d", d=65)
        accW = C.ps([128, 512], F32, "accW")[:, 0:260].rearrange("p (h d) -> p h d", d=65)
        impP = C.ps([128, 4, 128], F32, "impP")
        tpP = C.ps([128, 8, 128], BF16, "tpP")

        kcmpT = C.sb([64, 512], BF16, "kcmpT")
        vcmp = C.sb([128, 4, 65], BF16, "vcmp")
        P.op("pool", lambda e: e.memset(vcmp[:], 1.0), writes=["vcmp"])
        hid = [C.sb([128, 512], BF16, f"hid{i}") for i in range(2)]
        cb = C.sb([128, 2], F32, "cbias")
        for which, (srcT, skey, w1, w1key, w2, w2key, pe, pekey) in enumerate(((kcT, "kcT", w1k, "w1k", w2k, "w2k", pek, "pek"),
                                                                                 (vcT, "vcT", w1v, "w1v", w2v, "w2v", pev, "pev"))):
            view = srcT[:].rearrange("p (n s) -> p n s", s=16)
            for hc in range(2):
                for j in range(32):
                    P.op("pe", lambda e, j=j, hc=hc, w1=w1, pe=pe: e.matmul(impP[:, 0, 0:1], lhsT=w1[:, j, hc * 128:(hc + 1) * 128], rhs=pe[:, j:j + 1],
                                                                          start=(j == 0), stop=(j == 31)), reads=[w1key, pekey], writes=["impP"])
                P.op("act", lambda e, hc=hc: e.copy(out=cb[:, hc:hc + 1], in_=impP[:, 0, 0:1]), reads=["impP"], writes=[("cb", hc)])
                for j in range(32):
                    n0, s_ = j // 16, j % 16
                    P.op("pe", lambda e, j=j, hc=hc, w1=w1, n0=n0, s_=s_, view=view: e.matmul(sp[hc][:, 0:511], lhsT=w1[:, j, hc * 128:(hc + 1) * 128],
                                                                                              rhs=view[:, n0:n0 + 511, s_], start=(j == 0), stop=(j == 31)),
                         reads=[w1key, skey], writes=[("sp", hc)])
                P.op("pool", lambda e, hc=hc: e.memset(hid[hc][:], 0.0), writes=[("hid", hc)])
                P.op("act", lambda e, hc=hc: e.activation(out=hid[hc][:, 0:511], in_=sp[hc][:, 0:511], func=AF.Silu, bias=cb[:, hc:hc + 1]),
                     reads=[("sp", hc), ("cb", hc)], writes=[("hid", hc)])
            if which == 0:
                for hc in range(2):
                    P.op("pe", lambda e, hc=hc, w2=w2: e.matmul(sp[0][0:64, :], lhsT=w2[:, hc, :], rhs=hid[hc][:], start=(hc == 0), stop=(hc == 1)),
                         reads=[w2key, ("hid", hc)], writes=[("sp", 0)])
                P.op("act", lambda e: e.copy(out=kcmpT[:], in_=sp[0][0:64, :]), reads=[("sp", 0)], writes=["kcmpT"])
            else:
                for nt in range(4):
                    for hc in range(2):
                        P.op("pe", lambda e, hc=hc, nt=nt, w2=w2: e.matmul(sp[1][:, nt * 64:(nt + 1) * 64], lhsT=hid[hc][:, nt * 128:(nt + 1) * 128], rhs=w2[:, hc, :],
                                                                            start=(hc == 0), stop=(hc == 1)), reads=[w2key, ("hid", hc)], writes=[("sp", 1)])
                P.op("act", lambda e: e.copy(out=vcmp[:, :, 0:64], in_=sp[1][:, 0:256].rearrange("p (n d) -> p n d", d=64)), reads=[("sp", 1), "vcmp"], writes=["vcmp"])
                for nt in range(4):
                    P.op("dve", lambda e, nt=nt: e.tensor_scalar(out=vcmp[:, nt, :], in0=vcmp[:, nt, :], scalar1=vmask[:, nt:nt + 1], scalar2=None, op0=ALU.mult),
                         reads=["vcmp", "vmask"], writes=["vcmp"])

        pex = [C.sb([128, 512], BF16, f"pex{i}") for i in range(3)]
        pcnt = [0]
        scnt = [0]
        accs = [C.sb([128, 4, 65], F32, f"accs{i}") for i in range(3)]
        imps = C.sb([128, 4, 128], F32, "imps")
        rc = C.sb([128, 12], F32, "rc")
        wg = C.sb([128, 12], F32, "wg")
        impa = C.sb([128, 128], F32, "impa")
        sc = C.sb([128, 128], F32, "sc")
        sc2 = C.sb([128, 128], F32, "sc2")
        m8 = C.sb([128, 16], F32, "m8")
        selb = C.sb([128, 128], BF16, "selb")
        selT = C.sb([128, 4, 128], BF16, "selT")
        o32 = C.sb([128, 4, 64], F32, "o32")
        obf = C.sb([128, 256], BF16, "obf")
        onb = [C.sb([128, 2, 128], BF16, f"onb{i}") for i in range(2)]
        PAT = [[0, 4], [1, 128]]

        def score_tile(kT, kkey, c0, qv, extra=None):
            s_i = scnt[0] % 2
            scnt[0] += 1
            spt, spk = sp[s_i], ("sp", s_i)
            P.op("pe", lambda e: e.matmul(spt[:], lhsT=kT[:, c0:c0 + 128], rhs=qv, start=True, stop=(extra is None)), reads=[kkey, "q"], writes=[spk])
            if extra is not None:
                extra(spt, spk)
            p_i = pcnt[0] % 3
            pcnt[0] += 1
            pt, pk = pex[p_i], ("pex", p_i)
            P.op("act", lambda e: e.activation(out=pt[:], in_=spt[:], func=AF.Exp, scale=0.125), reads=[spk], writes=[pk])
            return pt, pk

        def mask_tile(pt, pk, base, cm, pat):
            P.op("pool", lambda e: e.affine_select(out=pt[:], in_=pt[:], pattern=pat, compare_op=ALU.is_ge, fill=0.0, base=base, channel_multiplier=cm),
                 reads=[pk], writes=[pk])

        def pv(pt, pk, acc, akey, vsb, vkey, vidx, first, last):
            for h in range(4):
                P.op("pe", lambda e, h=h: e.matmul(acc[:, h, :], lhsT=pt[:, h * 128:(h + 1) * 128], rhs=vsb[:, vidx, :], start=(first and h == 0), stop=last),
                     reads=[pk, vkey], writes=[akey])

        pend = []

        def submit(fn):
            pend.append(fn)
            if len(pend) > 1:
                pend.pop(0)()

        def flush():
            while pend:
                pend.pop(0)()

        for i in (range(NQB) if blocks is None else blocks):
            qb = 2 * i + parity
            t0 = qb * 128
            qv = q_sb[:, :, i * 128:(i + 1) * 128]
            ncmp = min(4, (8 * qb + 6) // 128 + 1)
            for j in range(ncmp):
                pt, pk = score_tile(kcmpT, "kcmpT", j * 128, qv)
                if (j + 1) * 128 - 1 >= 8 * qb - 1:
                    mask_tile(pt, pk, t0 - 31 - 16 * 128 * j, -16, PAT)
                def fin_c(pt=pt, pk=pk, j=j, ncmp=ncmp):
                    pv(pt, pk, accC, "accC", vcmp, "vcmp", j, j == 0, j == ncmp - 1)
                    for h in range(4):
                        P.op("pe", lambda e, h=h: e.matmul(impP[:, h, :], lhsT=pt[:, h * 128:(h + 1) * 128], rhs=ovl[:, j, :], start=(j == 0 and h == 0), stop=(j == ncmp - 1)),
                             reads=[pk, "ovl"], writes=["impP"])
                submit(fin_c)
            flush()
            P.op("act", lambda e: e.copy(out=accs[0][:], in_=accC[:]), reads=["accC"], writes=[("accs", 0)])
            P.op("act", lambda e: e.copy(out=imps[:], in_=impP[:]), reads=["impP"], writes=["imps"])
            k0 = max(0, qb - 4)
            for kt in range(k0, qb + 1):
                pt, pk = score_tile(kwT, "kwT", kt * 128, qv)
                if kt == qb:
                    mask_tile(pt, pk, 0, -1, PAT)
                elif kt == qb - 4:
                    mask_tile(pt, pk, -1, 1, [[0, 4], [-1, 128]])
                submit(lambda pt=pt, pk=pk, kt=kt, qb=qb, k0=k0: pv(pt, pk, accW, "accW", vw_sb, "vw", kt, kt == k0, kt == qb))
            flush()
            P.op("act", lambda e: e.copy(out=accs[2][:], in_=accW[:]), reads=["accW"], writes=[("accs", 2)])
            P.op("dve", lambda e: e.tensor_scalar_max(out=rc[:, 0:4], in0=accs[0][:, :, 64], scalar1=1e-30), reads=[("accs", 0)], writes=[("rc", 0)])
            P.op("dve", lambda e: e.reciprocal(out=rc[:, 0:4], in_=rc[:, 0:4]), reads=[("rc", 0)], writes=[("rc", 0)])
            P.op("dve", lambda e: e.tensor_scalar(out=impa[:], in0=imps[:, 0, :], scalar1=rc[:, 0:1], scalar2=None, op0=ALU.mult), reads=["imps", ("rc", 0)], writes=["impa"])
            for h in range(1, 4):
                P.op("dve", lambda e, h=h: e.scalar_tensor_tensor(out=impa[:], in0=imps[:, h, :], scalar=rc[:, h:h + 1], in1=impa[:], op0=ALU.mult, op1=ALU.add),
                     reads=["imps", ("rc", 0), "impa"], writes=["impa"])
            so = 126 - 2 * qb
            P.op("dve", lambda e, so=so: e.tensor_tensor(out=sc[:], in0=impa[:], in1=Mm[:, so:so + 128], op=ALU.mult), reads=["impa", "Mm"], writes=["sc"])
            P.op("dve", lambda e, so=so: e.tensor_tensor(out=sc[:], in0=sc[:], in1=Ma[:, so:so + 128], op=ALU.add), reads=["sc", "Ma"], writes=["sc"])
            P.op("dve", lambda e: e.tensor_tensor(out=sc[:], in0=sc[:], in1=F0[:], op=ALU.max), reads=["sc", "F0"], writes=["sc"])
            P.op("dve", lambda e: e.max(out=m8[:, 0:8], in_=sc[:]), reads=["sc"], writes=[("m8", 0)])
            P.op("dve", lambda e: e.match_replace(out=sc2[:], in_to_replace=m8[:, 0:8], in_values=sc[:], imm_value=-2.0), reads=["sc", ("m8", 0)], writes=["sc2"])
            P.op("dve", lambda e: e.max(out=m8[:, 8:16], in_=sc2[:]), reads=["sc2"], writes=[("m8", 1)])
            P.op("dve", lambda e: e.tensor_scalar_max(out=m8[:, 15:16], in0=m8[:, 15:16], scalar1=0.0), reads=[("m8", 1)], writes=[("m8", 1)])
            P.op("dve", lambda e: e.tensor_scalar(out=sc2[:], in0=sc[:], scalar1=m8[:, 15:16], scalar2=None, op0=ALU.is_ge), reads=["sc", ("m8", 1), "sc2"], writes=["sc2"])
            P.op("dve", lambda e: e.tensor_scalar(out=selb[:], in0=sc2[:], scalar1=-1.0, scalar2=30000.0, op0=ALU.add, op1=ALU.mult), reads=["sc2"], writes=["selb"])
            for h in range(4):
                P.op("pe", lambda e, h=h: e.transpose(out=tpP[:, h, :], in_=selb[:], identity=ident[:]), reads=["selb", "ident"], writes=["tpP"])
            P.op("act", lambda e: e.copy(out=selT[:], in_=tpP[:, 0:4, :]), reads=["tpP"], writes=["selT"])
            for kt in range(qb + 1):
                def extra(spt, spk, kt=kt):
                    P.op("pe", lambda e: e.matmul(spt[:], lhsT=Et[:, kt, :], rhs=selT[:], start=False, stop=True), reads=["E", "selT"], writes=[spk])
                pt, pk = score_tile(ksT, "ksT", kt * 128, qv, extra)
                if kt == qb:
                    mask_tile(pt, pk, 0, -1, PAT)
                submit(lambda pt=pt, pk=pk, kt=kt, qb=qb: pv(pt, pk, accS, "accS", vs_sb, "vs", kt, kt == 0, kt == qb))
            flush()
            P.op("act", lambda e: e.copy(out=accs[1][:], in_=accS[:]), reads=["accS"], writes=[("accs", 1)])
            gv = gsb[:, i, :].rearrange("p (h b) -> p h b", b=3)
            for b in range(3):
                if b > 0:
                    P.op("dve", lambda e, b=b: e.tensor_scalar_max(out=rc[:, 4 * b:4 * b + 4], in0=accs[b][:, :, 64], scalar1=1e-30), reads=[("accs", b)], writes=[("rc", b)])
                    P.op("dve", lambda e, b=b: e.reciprocal(out=rc[:, 4 * b:4 * b + 4], in_=rc[:, 4 * b:4 * b + 4]), reads=[("rc", b)], writes=[("rc", b)])
                P.op("dve", lambda e, b=b, gv=gv: e.tensor_tensor(out=wg[:, 4 * b:4 * b + 4], in0=rc[:, 4 * b:4 * b + 4], in1=gv[:, :, b], op=ALU.mult), reads=[("rc", b), "gsb"], writes=[("wg", b)])
                for h in range(4):
                    if b == 0:
                        P.op("dve", lambda e, h=h: e.tensor_scalar(out=o32[:, h, :], in0=accs[0][:, h, 0:64], scalar1=wg[:, h:h + 1], scalar2=None, op0=ALU.mult),
                             reads=[("accs", 0), ("wg", 0)], writes=[("o32", h)])
                    else:
                        P.op("dve", lambda e, h=h, b=b: e.scalar_tensor_tensor(out=o32[:, h, :], in0=accs[b][:, h, 0:64], scalar=wg[:, 4 * b + h:4 * b + h + 1], in1=o32[:, h, :],
                                                                               op0=ALU.mult, op1=ALU.add), reads=[("accs", b), ("wg", b), ("o32", h)], writes=[("o32", h)])
            P.op("pool", lambda e: e.tensor_copy(out=obf[:], in_=o32[:].rearrange("p h d -> p (h d)")), reads=[("o32", h) for h in range(4)], writes=["obf"])
            for h2 in range(2):
                P.op("pe", lambda e, h2=h2: e.transpose(out=tpP[:, h2, :], in_=obf[:, h2 * 128:(h2 + 1) * 128], identity=ident[:]), reads=["obf", "ident"], writes=["tpP"])
            ob_ = onb[i % 2]
            P.op("act", lambda e, ob_=ob_: e.copy(out=ob_[:], in_=tpP[:, 0:2, :]), reads=["tpP"], writes=[("onb", i % 2)])
            P.dma(on_d[:, i * 128:(i + 1) * 128].rearrange("(c p) t -> p c t", p=128), ob_[:], reads=[("onb", i % 2)], key=f"on{i % 2}")
        P.emit()
    return nc


def n_consts():
    n = np.arange(512)
    blk = np.arange(128)
    ov = ((16 * n[:, None] < 64 * blk[None, :] + 64) & (16 * n[:, None] + 32 > 64 * blk[None, :])).astype(np.float32)
    ov[511] = 0.0
    ovl = np.ascontiguousarray(ov.reshape(4, 128, 128).transpose(1, 0, 2))
    E = np.zeros((128, 64, 128), np.float32)
    for kt in range(64):
        E[2 * kt, kt, 0:64] = 1.0
        E[2 * kt + 1, kt, 64:128] = 1.0
    tl = np.arange(128)
    rel = np.arange(254) - 126
    cur = (tl // 64)[:, None]
    Mm = (rel[None, :] <= cur - 2).astype(np.float32)
    Ma = np.where((rel[None, :] == cur) | (rel[None, :] == cur - 1), 1e9, np.where(rel[None, :] > cur, -1.0, 0.0)).astype(np.float32)
    return ovl, E, Mm, Ma


def n_inputs(c, hfb, htb, htg, prm, l, consts):
    g, par = c // 2, c % 2
    ovl, E, Mm, Ma = consts
    sh = 128 if par == 0 else 0

    def shT(a):
        if sh == 0:
            return np.ascontiguousarray(a)
        out = np.zeros_like(a)
        out[:, sh:] = a[:, :SEQ - sh]
        return out

    def shV(a):
        out = np.zeros((SEQ, 65), NPBF)
        out[sh:, 0:64] = a[:SEQ - sh]
        out[sh:, 64] = 1.0
        return out
    qg = hfb[g * 256:(g + 1) * 256].reshape(4, 64, SEQ // 128, 128)[:, :, par::2, :]
    qsel = np.ascontiguousarray(qg.transpose(1, 0, 2, 3).reshape(64, 4, NQB * 128))
    gs = htg[:, g * 12:(g + 1) * 12].reshape(SEQ // 128, 128, 12)[par::2]
    nsh, bsh = sh // 16, sh // 64
    ov = ovl.transpose(1, 0, 2).reshape(512, 128)
    ov2 = np.zeros_like(ov)
    ov2[nsh:, bsh:] = ov[:512 - nsh, :128 - bsh]
    ov2[511] = 0.0
    F0 = np.full((128, 128), -3.0e38, np.float32)
    F0[:, bsh] = 1e9
    vm = np.ones(512, np.float32)
    vm[:nsh] = 0.0
    return {"qsel": qsel, "kcT": shT(hfb[1024 + g * 64:1024 + (g + 1) * 64]),
            "ksT": shT(hfb[1280 + g * 64:1280 + (g + 1) * 64]), "kwT": shT(hfb[1536 + g * 64:1536 + (g + 1) * 64]),
            "vcT": shT(hfb[1792 + g * 64:1792 + (g + 1) * 64]),
            "vs": shV(htb[:, 1024 + g * 64:1024 + (g + 1) * 64]), "vw": shV(htb[:, 1280 + g * 64:1280 + (g + 1) * 64]),
            "gsel": np.ascontiguousarray(gs.transpose(1, 0, 2)),
            "pekT": np.ascontiguousarray(prm["cmp_pe_k"][l].T), "pevT": np.ascontiguousarray(prm["cmp_pe_v"][l].T),
            "w1k": prm["cmp_w1_k"][l], "w1v": prm["cmp_w1_v"][l], "w2k": prm["cmp_w2_k"][l], "w2v": prm["cmp_w2_v"][l],
            "ovl": np.ascontiguousarray(ov2.reshape(4, 128, 128).transpose(1, 0, 2)), "Etab": E, "Mm": Mm, "Ma": Ma,
            "F0": F0, "vmask": np.ascontiguousarray(vm.reshape(4, 128).T)}


_CACHE = {}
_DBG = None


def _prog(name, fn):
    if name not in _CACHE:
        _CACHE[name] = fn()
    return _CACHE[name]


def kernel(**inp):
    prm = {k: np.asarray(v) for k, v in inp.items()}
    prm["w_f"] = [np.ascontiguousarray(prm["w_in"][l][:, FM_COLS]) for l in range(DEPTH)]
    prm["w_t"] = [np.ascontiguousarray(prm["w_in"][l][:, TM_COLS]) for l in range(DEPTH)]
    tabs = rope_tables()
    nconst = n_consts()
    cores = list(range(NCORE))
    xT = np.ascontiguousarray(prm["x"][0].T)
    res = run_bass_kernel_spmd(_prog("A", lambda: build_T(False, True)),
                               [t_inputs(c, xT, None, prm, None, 0, tabs) for c in cores], core_ids=cores).results
    for l in range(DEPTH):
        hf32 = np.concatenate([r["hf32"] for r in res], axis=1)
        htb = np.concatenate([r["htb"] for r in res], axis=0)
        cm, sm, lm = b_consts(l)
        maps = []
        for c in cores:
            cs = slice(c * 128, (c + 1) * 128)
            maps.append({"qsT": np.ascontiguousarray(hf32[0:1024][cs]), "zT": np.ascontiguousarray(hf32[1024:2048][cs]),
                         "sgT": np.ascontiguousarray(hf32[2048:3072][cs]), "v": np.ascontiguousarray(htb[:, cs]),
                         "lbl": np.ascontiguousarray(prm["hgrn_lb_logits"][:, cs].T), "lmask": lm,
                         "nw": np.ascontiguousarray(prm["hgrn_norm_w"][l][cs].reshape(128, 1)), "cmask": cm, "smask": sm})
        resB = run_bass_kernel_spmd(_prog("B", build_B), maps, core_ids=cores).results
        hfb = np.concatenate([r["hfb"] for r in res], axis=1)
        htg = np.concatenate([r["htg"] for r in res], axis=0)
        resN = run_bass_kernel_spmd(_prog("N", build_N), [n_inputs(c, hfb, htb, htg, prm, l, nconst) for c in cores], core_ids=cores).results
        oT = np.zeros((D, SEQ), NPBF)
        for c in cores:
            oT[c * 128:(c + 1) * 128] = resB[c]["ohT"]
            g, par = c // 2, c % 2
            on = resN[c]["onT"].reshape(256, NQB, 128)
            oT[1024 + g * 256:1024 + (g + 1) * 256].reshape(256, SEQ // 128, 128)[:, par::2, :] = on
        if l < DEPTH - 1:
            res = run_bass_kernel_spmd(_prog("CA", lambda: build_T(True, True)),
                                       [t_inputs(c, xT, oT, prm, l, l + 1, tabs) for c in cores], core_ids=cores).results
        else:
            res = run_bass_kernel_spmd(_prog("C", lambda: build_T(True, False)),
                                       [t_inputs(c, xT, oT, prm, l, None, tabs) for c in cores], core_ids=cores).results
        xT = np.concatenate([r["xT_out"] for r in res], axis=1)
        if _DBG is not None:
            _DBG[("oT", l)] = oT
            _DBG[("xT", l)] = xT
            if l == _DBG.get("stop", DEPTH):
                break
    return np.ascontiguousarray(xT.T)[None].astype(np.float32)
```

```python
import contextlib
import numpy as np
import ml_dtypes
import concourse.bass as bass
import concourse.mybir as mybir
from concourse.bass_utils import run_bass_kernel_spmd

F32 = mybir.dt.float32
BF16 = mybir.dt.bfloat16
AF = mybir.ActivationFunctionType
ALU = mybir.AluOpType
AX = mybir.AxisListType
NPBF = ml_dtypes.bfloat16

D = 2048
SEQ = 8192
DEPTH = 4
NCORE = 8
TPC = SEQ // NCORE
NT = TPC + 2
DFF = 5632
NIN = 6704
ALPHA = (2 * DEPTH) ** 0.25
LN_EPS = 1e-5
RMS_EPS = 1e-6
NF = 5120
NTM = 1584

ENGS = ("pe", "act", "dve", "pool", "sp")


class Op:
    __slots__ = ("eng", "fn", "reads", "writes", "dma", "deps", "need_inc", "ord", "dwait")

    def __init__(self, eng, fn, reads, writes, dma):
        self.eng, self.fn, self.reads, self.writes, self.dma = eng, fn, reads, writes, dma
        self.deps = []
        self.need_inc = False
        self.ord = 0


class Prog:
    def __init__(self, nc):
        self.nc = nc
        self.ops = []
        self.last_w = {}
        self.readers = {}
        self.dseen = {}

    def op(self, eng, fn, reads=(), writes=(), dma=None):
        o = Op(eng, fn, tuple(reads), tuple(writes), dma)
        deps = set()
        for k in o.reads:
            w = self.last_w.get(k)
            if w is not None:
                deps.add(w)
        for k in o.writes:
            w = self.last_w.get(k)
            if w is not None:
                deps.add(w)
            for r in self.readers.get(k, ()):
                deps.add(r)
        for k in o.reads:
            self.readers.setdefault(k, []).append(o)
        for k in o.writes:
            self.last_w[k] = o
            self.readers[k] = []
        deps.discard(o)
        o.deps = [d for d in deps if not (d.eng == "pe" and eng == "pe" and d.dma is None and dma is None)]
        for d in o.deps:
            d.need_inc = True
        o.dwait = {d.dma: self.dseen[d.dma] for d in o.deps if d.dma is not None}
        if dma is not None:
            self.dseen[dma] = self.dseen.get(dma, 0) + 1
        self.ops.append(o)
        return o

    def dma(self, out, in_, reads=(), writes=(), key="c", eng="sp"):
        return self.op(eng, lambda e: e.dma_start(out=out, in_=in_), reads, writes, dma=key)

    def emit(self):
        nc = self.nc
        cnt = {e: 0 for e in ENGS}
        dcnt = {}
        for o in self.ops:
            if o.dma is not None:
                dcnt[o.dma] = dcnt.get(o.dma, 0) + 1
                o.ord = dcnt[o.dma]
            elif o.need_inc:
                cnt[o.eng] += 1
                o.ord = cnt[o.eng]
        with contextlib.ExitStack() as st:
            esem = {e: st.enter_context(nc.semaphore("s_" + e)) for e in ENGS if e != "sp"}
            dsem = {k: st.enter_context(nc.semaphore("d_" + str(k))) for k in dcnt}
            block = st.enter_context(nc.Block())
            by_eng = {e: [o for o in self.ops if o.eng == e] for e in ENGS}

            def run(eng_name, e):
                waited = {}
                for o in by_eng[eng_name]:
                    need = {}
                    for d in o.deps:
                        if d.dma is not None:
                            s = ("d", d.dma)
                            v = 16 * o.dwait[d.dma]
                        else:
                            s = ("e", d.eng)
                            v = d.ord
                        if v > need.get(s, 0):
                            need[s] = v
                    for s, v in need.items():
                        if waited.get(s, 0) >= v:
                            continue
                        waited[s] = v
                        e.wait_ge(dsem[s[1]] if s[0] == "d" else esem[s[1]], v)
                    ins = o.fn(e)
                    if o.dma is not None:
                        ins.then_inc(dsem[o.dma], 16)
                    elif o.need_inc:
                        ins.then_inc(esem[o.eng], 1)
                if eng_name == "sp":
                    for k, n in dcnt.items():
                        e.wait_ge(dsem[k], 16 * n)

            block.tensor(lambda e: run("pe", e))
            block.scalar(lambda e: run("act", e))
            block.vector(lambda e: run("dve", e))
            block.gpsimd(lambda e: run("pool", e))
            block.sync(lambda e: run("sp", e))


class Ctx:
    def __init__(self, nc, st):
        self.nc, self.st = nc, st
        self.n = 0

    def sb(self, shape, dt, name=None):
        self.n += 1
        return self.st.enter_context(self.nc.sbuf_tensor(name or f"sb{self.n}", list(shape), dt))

    def ps(self, shape, dt, name=None):
        self.n += 1
        return self.st.enter_context(self.nc.psum_tensor(name or f"ps{self.n}", list(shape), dt))


def make_identity(P, C, dt=BF16, name="ident"):
    idf = C.sb([128, 128], F32, name + "f")
    ident = C.sb([128, 128], dt, name)
    P.op("pool", lambda e: e.memset(idf[:], 1.0), writes=[name + "f"])
    P.op("pool", lambda e: e.affine_select(out=idf[:], in_=idf[:], pattern=[[-1, 128]], compare_op=ALU.is_equal,
                                           fill=0.0, base=0, channel_multiplier=1), reads=[name + "f"], writes=[name + "f"])
    P.op("dve", lambda e: e.tensor_copy(out=ident[:], in_=idf[:]), reads=[name + "f"], writes=[name])
    return ident


class Linear:
    def __init__(self, P, C, name, KCmax, nbuf=3, npsum=2, pn=512):
        self.P, self.C, self.name = P, C, name
        self.w = [C.sb([128, KCmax, 128], BF16, f"{name}_w{i}") for i in range(nbuf)]
        self.ps = [C.ps([128, pn], F32, f"{name}_p{i}") for i in range(npsum)]
        self.nbuf, self.npsum = nbuf, npsum
        self.wi = 0
        self.pi = 0

    def load(self, W, KC, c0, cw):
        i = self.wi % self.nbuf
        self.wi += 1
        wt = self.w[i]
        key = (self.name, "w", i)
        self.P.dma(wt[:, 0:KC, 0:cw], W[:, c0:c0 + cw].rearrange("(kc p) n -> p kc n", p=128),
                   writes=[key], key=f"{self.name}w{i}", eng="pool")
        return wt, key

    def next_ps(self):
        i = self.pi % self.npsum
        self.pi += 1
        return self.ps[i], (self.name, "p", i)

    def run(self, W, KC, chunks, in_ap, in_key, groups, epilogue):
        P = self.P
        for ci, (c0, cw) in enumerate(chunks):
            wt, wkey = self.load(W, KC, c0, cw)
            for gi, (g0, gn) in enumerate(groups):
                pt, pkey = self.next_ps()
                for kc in range(KC):
                    P.op("pe", lambda e, kc=kc, wt=wt, pt=pt, g0=g0, gn=gn, cw=cw: e.matmul(
                        pt[0:cw, 0:gn], lhsT=wt[:, kc, 0:cw], rhs=in_ap(kc, g0, gn), start=(kc == 0), stop=(kc == KC - 1)),
                        reads=[wkey, in_key(kc)], writes=[pkey])
                epilogue(ci, gi, g0, gn, cw, pt, pkey)


GROUPS = [(0, 512), (512, 512), (1024, 2)]
MGROUPS = [(0, 512), (512, 512)]


def build_T(do_c, do_a, dbg_fm=40, dbg_tm=True, dbg_rope=True):
    nc = bass.Bass("TRN2", target_bir_lowering=False)
    dt_in = lambda n, s, d: nc.dram_tensor(n, list(s), d, kind="ExternalInput").ap()
    dt_out = lambda n, s, d: nc.dram_tensor(n, list(s), d, kind="ExternalOutput").ap()
    xT_d = dt_in("xT", [D, NT], F32)
    if do_c:
        oT_d = dt_in("oT", [D, NT], BF16)
        wout_d = dt_in("w_out", [D, D], F32)
        ln_d = dt_in("lnp", [128, 4, 16], F32)
        wup_d = dt_in("w_up", [D, 2 * DFF], F32)
        cw_d = dt_in("convp", [128, 4, 88], F32)
        wdn_d = dt_in("w_down", [DFF, D], F32)
        hmask_d = dt_in("hmask", [128, 2], F32)
        xo_d = dt_out("xT_out", [D, TPC], F32)
    if do_a:
        wf_d = dt_in("w_f", [D, NF], F32)
        wt_d = dt_in("w_t", [D, NTM], F32)
        cos_d = dt_in("cosT", [128, TPC], F32)
        sin_d = dt_in("sinT", [128, TPC], F32)
        rot_d = dt_in("rotm", [128, 128], F32)
        hf32_d = dt_out("hf32", [3072, TPC], F32)
        hfb_d = dt_out("hfb", [2048, TPC], BF16)
        htb_d = dt_out("htb", [TPC, 1536], BF16)
        htg_d = dt_out("htg", [TPC, 48], F32)

    with contextlib.ExitStack() as st:
        C = Ctx(nc, st)
        P = Prog(nc)
        xb = C.sb([128, 16, NT], BF16, "xb")
        ones = C.sb([128, 128], F32, "ones")
        P.op("pool", lambda e: e.memset(ones[:], 1.0), writes=["ones"])
        epsc = C.sb([128, 1], F32, "epsc")
        P.op("pool", lambda e: e.memset(epsc[:], LN_EPS), writes=["epsc"])
        L = Linear(P, C, "L", 16, npsum=3)

        def layer_norm(src_ap, src_key, gcol, bcol, lnp, out32, out32_key, outb, outb_key, groups, sq, st_ps, stat):
            for (g0, gn) in groups:
                ps_s, ps_q = st_ps
                for kc in range(16):
                    P.op("act", lambda e, kc=kc, g0=g0, gn=gn: e.activation(out=sq[:, 0:gn], in_=src_ap(kc, g0, gn), func=AF.Square),
                         reads=[src_key(kc)], writes=["lnsq"])
                    P.op("pe", lambda e, kc=kc, g0=g0, gn=gn: e.matmul(ps_s[:, 0:gn], lhsT=ones[:], rhs=src_ap(kc, g0, gn), start=(kc == 0), stop=(kc == 15)),
                         reads=["ones", src_key(kc)], writes=["ln_ps_s"])
                    P.op("pe", lambda e, kc=kc, g0=g0, gn=gn: e.matmul(ps_q[:, 0:gn], lhsT=ones[:], rhs=sq[:, 0:gn], start=(kc == 0), stop=(kc == 15)),
                         reads=["ones", "lnsq"], writes=["ln_ps_q"])
                mean, rstd, tmp = stat
                P.op("act", lambda e, gn=gn: e.mul(out=mean[:, 0:gn], in_=ps_s[:, 0:gn], mul=1.0 / D), reads=["ln_ps_s"], writes=["ln_mean"])
                P.op("dve", lambda e, gn=gn: e.tensor_tensor(out=tmp[:, 0:gn], in0=mean[:, 0:gn], in1=mean[:, 0:gn], op=ALU.mult), reads=["ln_mean"], writes=["ln_tmp"])
                P.op("act", lambda e, gn=gn: e.mul(out=rstd[:, 0:gn], in_=ps_q[:, 0:gn], mul=1.0 / D), reads=["ln_ps_q"], writes=["ln_rstd"])
                P.op("dve", lambda e, gn=gn: e.tensor_tensor(out=tmp[:, 0:gn], in0=rstd[:, 0:gn], in1=tmp[:, 0:gn], op=ALU.subtract),
                     reads=["ln_rstd", "ln_tmp"], writes=["ln_tmp"])
                P.op("act", lambda e, gn=gn: e.activation(out=rstd[:, 0:gn], in_=tmp[:, 0:gn], func=AF.Ln, bias=epsc[:, 0:1]), reads=["ln_tmp", "epsc"], writes=["ln_rstd"])
                P.op("act", lambda e, gn=gn: e.activation(out=rstd[:, 0:gn], in_=rstd[:, 0:gn], func=AF.Exp, scale=-0.5), reads=["ln_rstd"], writes=["ln_rstd"])
                for kc in range(16):
                    P.op("dve", lambda e, kc=kc, g0=g0, gn=gn: e.tensor_tensor(out=tmp[:, 0:gn], in0=src_ap(kc, g0, gn), in1=mean[:, 0:gn], op=ALU.subtract),
                         reads=[src_key(kc), "ln_mean"], writes=["ln_tmp"])
                    P.op("pool", lambda e, gn=gn: e.tensor_tensor(out=tmp[:, 0:gn], in0=tmp[:, 0:gn], in1=rstd[:, 0:gn], op=ALU.mult),
                         reads=["ln_tmp", "ln_rstd"], writes=["ln_tmp"])
                    P.op("act", lambda e, kc=kc, g0=g0, gn=gn: e.activation(out=out32(kc, g0, gn), in_=tmp[:, 0:gn], func=AF.Identity,
                                                                           scale=lnp[:, gcol, kc:kc + 1], bias=lnp[:, bcol, kc:kc + 1]),
                         reads=["ln_tmp", "lnp"], writes=[out32_key(kc)])
                    P.op("dve", lambda e, kc=kc, g0=g0, gn=gn: e.tensor_copy(out=outb(kc, g0, gn), in_=out32(kc, g0, gn)),
                         reads=[out32_key(kc)], writes=[outb_key(kc)])

        if do_c:
            x32 = C.sb([128, 16, NT], F32, "x32")
            lnp = C.sb([128, 4, 16], F32, "lnp_sb")
            cvp = C.sb([128, 4, 88], F32, "cvp")
            hmask = C.sb([128, 2], F32, "hmask_sb")
            P.dma(lnp[:], ln_d, writes=["lnp"])
            P.dma(cvp[:], cw_d, writes=["cvp"])
            P.dma(hmask[:], hmask_d, writes=["hmask"])
            for kc in range(16):
                P.dma(x32[:, kc, :], xT_d[kc * 128:(kc + 1) * 128, :], writes=[("x32", kc)])
                P.dma(xb[:, kc, :], oT_d[kc * 128:(kc + 1) * 128, :], writes=[("xb", kc)])
            sq = C.sb([128, 512], F32, "lnsq")
            st_ps = (C.ps([128, 512], F32, "ln_ps_s"), C.ps([128, 512], F32, "ln_ps_q"))
            stat = (C.sb([128, 512], F32, "ln_mean"), C.sb([128, 512], F32, "ln_rstd"), C.sb([128, 512], F32, "ln_tmp"))

            evb = [C.sb([128, 512], F32, f"evb{i}") for i in range(2)]
            evc = [0]

            def evac(pt, pkey, gn):
                i = evc[0] % 2
                evc[0] += 1
                P.op("act", lambda e: e.copy(out=evb[i][:, 0:gn], in_=pt[:, 0:gn]), reads=[pkey], writes=[("evb", i)])
                return evb[i], ("evb", i)

            def ep_out(ci, gi, g0, gn, cw, pt, pkey):
                ev, ekey = evac(pt, pkey, gn)
                P.op("dve", lambda e: e.scalar_tensor_tensor(out=x32[:, ci, g0:g0 + gn], in0=x32[:, ci, g0:g0 + gn], scalar=ALPHA,
                                                             in1=ev[:, 0:gn], op0=ALU.mult, op1=ALU.add),
                     reads=[("x32", ci), ekey], writes=[("x32", ci)])
            L.run(wout_d, 16, [(c * 128, 128) for c in range(16)], lambda kc, g0, gn: xb[:, kc, g0:g0 + gn], lambda kc: ("xb", kc), GROUPS, ep_out)
            layer_norm(lambda kc, g0, gn: x32[:, kc, g0:g0 + gn], lambda kc: ("x32", kc), 0, 1, lnp,
                       lambda kc, g0, gn: x32[:, kc, g0:g0 + gn], lambda kc: ("x32", kc),
                       lambda kc, g0, gn: xb[:, kc, g0:g0 + gn], lambda kc: ("xb", kc), GROUPS, sq, st_ps, stat)
            NQ = 11
            aT = C.sb([128, NQ, TPC], BF16, "aT")
            hbuf = [C.sb([128, 2 + TPC], F32, f"hbuf{i}") for i in range(2)]
            cacc = [C.sb([128, TPC], F32, f"cacc{i}") for i in range(2)]
            for q in range(4):
                def ep_up(ci, gi, g0, gn, cw, pt, pkey, q=q):
                    jl, half = ci // 2, ci % 2
                    j = q * NQ + jl
                    hb = hbuf[half]
                    ch = j + 44 * half
                    hkeys = [("hb", half, 0), ("hb", half, 1), ("hb", half, 2)]
                    if g0 == TPC:
                        P.op("act", lambda e: e.copy(out=hb[:, 0:2], in_=pt[:, 0:2]), reads=[pkey], writes=[("hb", half, 2)])
                        P.op("dve", lambda e: e.tensor_tensor(out=hb[:, 0:2], in0=hb[:, 0:2], in1=hmask[:, 0:2], op=ALU.mult),
                             reads=[("hb", half, 2), "hmask"], writes=[("hb", half, 2)])
                        acc = cacc[half]
                        P.op("dve", lambda e: e.tensor_scalar(out=acc[:], in0=hb[:, 2:2 + TPC], scalar1=cvp[:, 2, ch:ch + 1], scalar2=cvp[:, 3, ch:ch + 1],
                                                              op0=ALU.mult, op1=ALU.add),
                             reads=hkeys + ["cvp"], writes=[("cacc", half)])
                        P.op("dve", lambda e: e.scalar_tensor_tensor(out=acc[:], in0=hb[:, 1:1 + TPC], scalar=cvp[:, 1, ch:ch + 1], in1=acc[:],
                                                                      op0=ALU.mult, op1=ALU.add),
                             reads=hkeys + ["cvp", ("cacc", half)], writes=[("cacc", half)])
                        P.op("dve", lambda e: e.scalar_tensor_tensor(out=acc[:], in0=hb[:, 0:TPC], scalar=cvp[:, 0, ch:ch + 1], in1=acc[:],
                                                                     op0=ALU.mult, op1=ALU.add),
                             reads=hkeys + ["cvp", ("cacc", half)], writes=[("cacc", half)])
                        if half == 0:
                            P.op("act", lambda e: e.activation(out=acc[:], in_=acc[:], func=AF.Silu), reads=[("cacc", 0)], writes=[("cacc", 0)])
                        else:
                            P.op("pool", lambda e: e.tensor_tensor(out=aT[:, jl, :], in0=cacc[0][:], in1=cacc[1][:], op=ALU.mult),
                                 reads=[("cacc", 0), ("cacc", 1)], writes=[("aT", jl)])
                    else:
                        P.op("act", lambda e: e.copy(out=hb[:, 2 + g0:2 + g0 + gn], in_=pt[:, 0:gn]), reads=[pkey], writes=[("hb", half, gi)])
                up_chunks = []
                for jl in range(NQ):
                    j = q * NQ + jl
                    up_chunks += [(j * 128, 128), ((44 + j) * 128, 128)]
                L.run(wup_d, 16, up_chunks, lambda kc, g0, gn: xb[:, kc, g0:g0 + gn], lambda kc: ("xb", kc), GROUPS, ep_up)

                def ep_dn(ci, gi, g0, gn, cw, pt, pkey, q=q):
                    ev, ekey = evac(pt, pkey, gn)
                    if q == 0:
                        P.op("dve", lambda e: e.scalar_tensor_tensor(out=x32[:, ci, g0:g0 + gn], in0=x32[:, ci, g0:g0 + gn], scalar=ALPHA,
                                                                     in1=ev[:, 0:gn], op0=ALU.mult, op1=ALU.add),
                             reads=[("x32", ci), ekey], writes=[("x32", ci)])
                    else:
                        P.op("dve", lambda e: e.tensor_tensor(out=x32[:, ci, g0:g0 + gn], in0=x32[:, ci, g0:g0 + gn], in1=ev[:, 0:gn], op=ALU.add),
                             reads=[("x32", ci), ekey], writes=[("x32", ci)])
                L.run(wdn_d[q * NQ * 128:(q + 1) * NQ * 128, :], NQ, [(c * 128, 128) for c in range(16)],
                      lambda kc, g0, gn: aT[:, kc, g0:g0 + gn], lambda kc: ("aT", kc), MGROUPS, ep_dn)
            layer_norm(lambda kc, g0, gn: x32[:, kc, g0:g0 + gn], lambda kc: ("x32", kc), 2, 3, lnp,
                       lambda kc, g0, gn: x32[:, kc, g0:g0 + gn], lambda kc: ("x32", kc),
                       lambda kc, g0, gn: xb[:, kc, g0:g0 + gn], lambda kc: ("xb", kc), MGROUPS, sq, st_ps, stat)
            for kc in range(16):
                P.dma(xo_d[kc * 128:(kc + 1) * 128, :], x32[:, kc, 0:TPC], reads=[("x32", kc)], key="xo")
        else:
            xs = C.sb([128, NT], F32, "xs")
            for kc in range(16):
                P.dma(xs[:], xT_d[kc * 128:(kc + 1) * 128, :], writes=["xs"], key="xs")
                P.op("dve", lambda e, kc=kc: e.tensor_copy(out=xb[:, kc, :], in_=xs[:]), reads=["xs"], writes=[("xb", kc)])

        if do_a:
            cosT = C.sb([128, TPC], F32, "cos_sb")
            sinT = C.sb([128, TPC], F32, "sin_sb")
            rotf = C.sb([128, 128], F32, "rotf")
            rotm = C.sb([128, 128], BF16, "rotm_sb")
            P.dma(cosT[:], cos_d, writes=["cosT"])
            P.dma(sinT[:], sin_d, writes=["sinT"])
            P.dma(rotf[:], rot_d, writes=["rotf"])
            P.op("dve", lambda e: e.tensor_copy(out=rotm[:], in_=rotf[:]), reads=["rotf"], writes=["rotm"])
            o32 = [C.sb([128, 512], F32, f"ao32_{i}") for i in range(2)]
            obf = [C.sb([128, 512], BF16, f"aobf_{i}") for i in range(2)]
            rb = C.sb([128, 512], BF16, "rope_b")
            rt = C.sb([128, 512], F32, "rope_t")
            rps = C.ps([128, 512], F32, "rope_ps")
            cnt = [0]

            def ep_a(ci, gi, g0, gn, cw, pt, pkey):
                i = cnt[0] % 2
                cnt[0] += 1
                if ci < 24:
                    o = o32[i]
                    okey = ("ao32", i)
                    if ci < 8 or ci >= 16:
                        P.op("act", lambda e: e.activation(out=o[:, 0:gn], in_=pt[:, 0:gn], func=AF.Silu), reads=[pkey], writes=[okey])
                    else:
                        P.op("act", lambda e: e.copy(out=o[:, 0:gn], in_=pt[:, 0:gn]), reads=[pkey], writes=[okey])
                    P.dma(hf32_d[ci * 128:(ci + 1) * 128, g0:g0 + gn], o[:, 0:gn], reads=[okey], key=f"ao{i}")
                else:
                    o = obf[i]
                    okey = ("aobf", i)
                    if ci < 38 and dbg_rope:
                        P.op("act", lambda e: e.copy(out=rb[:, 0:gn], in_=pt[:, 0:gn]), reads=[pkey], writes=["rope_b"])
                        P.op("pe", lambda e: e.matmul(rps[:, 0:gn], lhsT=rotm[:], rhs=rb[:, 0:gn], start=True, stop=True), reads=["rotm", "rope_b"], writes=["rope_ps"])
                        P.op("act", lambda e: e.copy(out=rt[:, 0:gn], in_=pt[:, 0:gn]), reads=[pkey], writes=["rope_t"])
                        P.op("act", lambda e: e.copy(out=o32[i][:, 0:gn], in_=rps[:, 0:gn]), reads=["rope_ps"], writes=[("ao32", i)])
                        P.op("dve", lambda e: e.tensor_tensor(out=rt[:, 0:gn], in0=rt[:, 0:gn], in1=cosT[:, g0:g0 + gn], op=ALU.mult), reads=["rope_t", "cosT"], writes=["rope_t"])
                        P.op("pool", lambda e: e.tensor_tensor(out=o32[i][:, 0:gn], in0=o32[i][:, 0:gn], in1=sinT[:, g0:g0 + gn], op=ALU.mult), reads=[("ao32", i), "sinT"], writes=[("ao32", i)])
                        P.op("dve", lambda e: e.tensor_tensor(out=o[:, 0:gn], in0=rt[:, 0:gn], in1=o32[i][:, 0:gn], op=ALU.add), reads=["rope_t", ("ao32", i)], writes=[okey])
                    else:
                        P.op("act", lambda e: e.copy(out=o[:, 0:gn], in_=pt[:, 0:gn]), reads=[pkey], writes=[okey])
                    r0 = (ci - 24) * 128
                    P.dma(hfb_d[r0:r0 + 128, g0:g0 + gn], o[:, 0:gn], reads=[okey], key=f"ab{i}")
            L.run(wf_d, 16, [(c * 128, 128) for c in range(dbg_fm)], lambda kc, g0, gn: xb[:, kc, g0:g0 + gn], lambda kc: ("xb", kc), MGROUPS, ep_a)

            wtm = [C.sb([128, 16, 256], BF16, f"wtm{i}") for i in range(2)]
            tps = [C.ps([128, 512], F32, f"tm_ps{i}") for i in range(2)]
            tob = [C.sb([128, 256], BF16, f"tm_ob{i}") for i in range(2)]
            tog = C.sb([128, 48], F32, "tm_og")
            k = 0
            for bi, (c0, cw) in enumerate(([(i * 256, 256) for i in range(6)] + [(1536, 48)]) if dbg_tm else []):
                wt = wtm[bi % 2]
                wkey = ("wtm", bi % 2)
                P.dma(wt[:, :, 0:cw], wt_d[:, c0:c0 + cw].rearrange("(kc p) n -> p kc n", p=128), writes=[wkey], key=f"wtm{bi % 2}", eng="pool")
                for tt in range(TPC // 128):
                    pt = tps[k % 2]
                    pkey = ("tm_ps", k % 2)
                    for kc in range(16):
                        P.op("pe", lambda e, kc=kc, wt=wt, pt=pt, tt=tt, cw=cw: e.matmul(pt[:, 0:cw], lhsT=xb[:, kc, tt * 128:(tt + 1) * 128], rhs=wt[:, kc, 0:cw],
                                                                                         start=(kc == 0), stop=(kc == 15)),
                             reads=[wkey, ("xb", kc)], writes=[pkey])
                    if cw == 256:
                        o = tob[k % 2]
                        okey = ("tm_ob", k % 2)
                        P.op("act", lambda e, o=o, pt=pt: e.copy(out=o[:], in_=pt[:, 0:256]), reads=[pkey], writes=[okey])
                        P.dma(htb_d[tt * 128:(tt + 1) * 128, c0:c0 + 256], o[:], reads=[okey], key=f"tmo{k % 2}")
                    else:
                        P.op("act", lambda e, pt=pt: e.activation(out=tog[:], in_=pt[:, 0:48], func=AF.Sigmoid), reads=[pkey], writes=["tm_og"])
                        P.dma(htg_d[tt * 128:(tt + 1) * 128, :], tog[:], reads=["tm_og"], key="tmg")
                    k += 1
        P.emit()
    return nc


FM_COLS = np.concatenate([np.arange(0, 1024), np.arange(1024, 2048), np.arange(3072, 4096), np.arange(4096, 5120),
                          np.arange(5120, 5376), np.arange(5632, 5888), np.arange(6144, 6400), np.arange(5376, 5632)])
TM_COLS = np.concatenate([np.arange(2048, 3072), np.arange(5888, 6144), np.arange(6400, 6656), np.arange(6656, 6704)])


def rope_tables():
    half = 32
    inv = (10000.0 ** (-np.arange(half, dtype=np.float32) / half)).astype(np.float32)
    pos = np.arange(SEQ, dtype=np.float32)
    ang = pos[None, :] * inv[:, None]
    cos, sin = np.cos(ang).astype(np.float32), np.sin(ang).astype(np.float32)
    cosT = np.concatenate([cos, cos, cos, cos], 0)
    sinT = np.concatenate([-sin, sin, -sin, sin], 0)
    rot = np.zeros((128, 128), np.float32)
    for m in range(128):
        k = m + 32 if (m % 64) < 32 else m - 32
        rot[k, m] = 1.0
    return cosT, sinT, rot


def with_halo(aT, c):
    s = c * TPC
    out = np.zeros((aT.shape[0], NT), aT.dtype)
    out[:, :TPC] = aT[:, s:s + TPC]
    if c > 0:
        out[:, TPC:] = aT[:, s - 2:s]
    return out


def chunked(v):
    return np.ascontiguousarray(v.reshape(-1, 128).T)


def t_inputs(c, xT, oT, prm, lc, la, tabs):
    m = {"xT": with_halo(xT, c)}
    if lc is not None:
        m["oT"] = with_halo(oT, c)
        m["w_out"] = prm["w_out"][lc]
        m["lnp"] = np.ascontiguousarray(np.stack([chunked(prm["ln1_g"][lc]), chunked(prm["ln1_b"][lc]),
                                                  chunked(prm["ln2_g"][lc]), chunked(prm["ln2_b"][lc])], 1))
        m["w_up"] = prm["w_up"][lc]
        m["convp"] = np.ascontiguousarray(np.stack([chunked(prm["conv_w"][lc][0]), chunked(prm["conv_w"][lc][1]),
                                                    chunked(prm["conv_w"][lc][2]), chunked(prm["conv_b"][lc])], 1))
        m["w_down"] = prm["w_down"][lc]
        m["hmask"] = np.full((128, 2), 0.0 if c == 0 else 1.0, np.float32)
    if la is not None:
        cosT, sinT, rot = tabs
        m["w_f"] = prm["w_f"][la]
        m["w_t"] = prm["w_t"][la]
        m["cosT"] = np.ascontiguousarray(cosT[:, c * TPC:(c + 1) * TPC])
        m["sinT"] = np.ascontiguousarray(sinT[:, c * TPC:(c + 1) * TPC])
        m["rotm"] = rot
    return m


def build_B():
    nc = bass.Bass("TRN2", target_bir_lowering=False)
    dt_in = lambda n, s, d: nc.dram_tensor(n, list(s), d, kind="ExternalInput").ap()
    qs_d = dt_in("qsT", [128, SEQ], F32)
    z_d = dt_in("zT", [128, SEQ], F32)
    sg_d = dt_in("sgT", [128, SEQ], F32)
    v_d = dt_in("v", [SEQ, 128], BF16)
    lbl_d = dt_in("lbl", [128, 4], F32)
    lmask_d = dt_in("lmask", [128, 4], F32)
    nw_d = dt_in("nw", [128, 1], F32)
    cmask_d = dt_in("cmask", [64, 512], F32)
    smask_d = dt_in("smask", [128, 2048], F32)
    oh_d = nc.dram_tensor("ohT", [128, SEQ], BF16, kind="ExternalOutput").ap()
    NCH = SEQ // 64
    with contextlib.ExitStack() as st:
        C = Ctx(nc, st)
        P = Prog(nc)
        ident = make_identity(P, C)
        ones = C.sb([128, 128], F32, "ones")
        P.op("pool", lambda e: e.memset(ones[:], 1.0), writes=["ones"])
        epsc = C.sb([128, 1], F32, "epsc")
        P.op("pool", lambda e: e.memset(epsc[:], RMS_EPS), writes=["epsc"])
        lbl = C.sb([128, 4], F32, "lbl_sb")
        lmask = C.sb([128, 4], F32, "lmask_sb")
        nw = C.sb([128, 1], F32, "nw_sb")
        cmask = C.sb([64, 512], F32, "cmask_sb")
        smask = C.sb([128, 2048], F32, "smask_sb")
        vt = C.sb([64, NCH, 128], BF16, "vt")
        for t_, d_, k_ in ((lbl, lbl_d, "lbl"), (lmask, lmask_d, "lmask"), (nw, nw_d, "nw"), (cmask, cmask_d, "cmask"), (smask, smask_d, "smask")):
            P.dma(t_[:], d_, writes=[k_])
        P.dma(vt[:], v_d.rearrange("(c s) v -> s c v", s=64), writes=["vt"])
        sm = C.sb([128, 8], F32, "sm")
        lb = C.sb([128, 4], F32, "lb")
        P.op("act", lambda e: e.activation(out=sm[:, 0:4], in_=lbl[:], func=AF.Exp), reads=["lbl"], writes=["sm0"])
        P.op("dve", lambda e: e.tensor_reduce(out=lb[:, 0:1], in_=sm[:, 0:4], axis=AX.X, op=ALU.add), reads=["sm0"], writes=["lb0"])
        P.op("dve", lambda e: e.tensor_tensor(out=sm[:, 4:8], in0=sm[:, 0:4], in1=lmask[:], op=ALU.mult), reads=["sm0", "lmask"], writes=["sm1"])
        P.op("dve", lambda e: e.tensor_reduce(out=lb[:, 1:2], in_=sm[:, 4:8], axis=AX.X, op=ALU.add), reads=["sm1"], writes=["lb1"])
        P.op("dve", lambda e: e.reciprocal(out=lb[:, 0:1], in_=lb[:, 0:1]), reads=["lb0"], writes=["lb0"])
        P.op("dve", lambda e: e.tensor_tensor(out=lb[:, 2:3], in0=lb[:, 1:2], in1=lb[:, 0:1], op=ALU.mult), reads=["lb0", "lb1"], writes=["lb2"])
        P.op("dve", lambda e: e.tensor_scalar(out=lb[:, 3:4], in0=lb[:, 2:3], scalar1=-1.0, scalar2=1.0, op0=ALU.mult, op1=ALU.add), reads=["lb2"], writes=["lb3"])
        LBK = ["lb2", "lb3"]

        qb = C.sb([128, SEQ], BF16, "qb")
        kb = C.sb([128, SEQ], BF16, "kb")
        ebl = C.sb([128, NCH], F32, "ebl")
        W = 2048
        zt = C.sb([128, W], F32, "zt")
        qt = C.sb([128, W], F32, "qt")
        t1 = C.sb([128, W], F32, "t1")
        t2 = C.sb([128, W], F32, "t2")
        t3 = C.sb([128, W], F32, "t3")
        for bi in range(SEQ // W):
            c0 = bi * W
            P.dma(zt[:], z_d[:, c0:c0 + W], writes=["zt"], key="zt")
            P.dma(qt[:], qs_d[:, c0:c0 + W], writes=["qt"], key="qt")
            P.op("act", lambda e: e.activation(out=t1[:], in_=zt[:], func=AF.Sigmoid), reads=["zt"], writes=["t1"])
            P.op("dve", lambda e: e.tensor_scalar(out=t1[:], in0=t1[:], scalar1=lb[:, 3:4], scalar2=lb[:, 2:3], op0=ALU.mult, op1=ALU.add), reads=["t1"] + LBK, writes=["t1"])
            P.op("dve", lambda e: e.tensor_scalar_max(out=t1[:], in0=t1[:], scalar1=1e-30), reads=["t1"], writes=["t1"])
            P.op("act", lambda e: e.activation(out=t1[:], in_=t1[:], func=AF.Ln), reads=["t1"], writes=["t1"])
            P.op("dve", lambda e: e.tensor_tensor_scan(out=t2[:], data0=smask[:], data1=t1[:], initial=0.0, op0=ALU.mult, op1=ALU.add), reads=["t1", "smask"], writes=["t2"])
            P.op("act", lambda e: e.activation(out=t1[:], in_=t2[:], func=AF.Exp), reads=["t2"], writes=["t1"])
            P.op("act", lambda e: e.activation(out=t3[:], in_=t2[:], func=AF.Exp, scale=-1.0), reads=["t2"], writes=["t3"])
            P.op("act", lambda e: e.activation(out=t2[:], in_=zt[:], func=AF.Sigmoid, scale=-1.0), reads=["zt", "t2"], writes=["t2"])
            P.op("dve", lambda e: e.tensor_scalar(out=t2[:], in0=t2[:], scalar1=lb[:, 3:4], scalar2=None, op0=ALU.mult), reads=["t2"] + LBK, writes=["t2"])
            P.op("pool", lambda e, c0=c0: e.tensor_tensor(out=kb[:, c0:c0 + W], in0=t2[:], in1=t3[:], op=ALU.mult), reads=["t2", "t3"], writes=[("kb", bi)])
            P.op("pool", lambda e, c0=c0: e.tensor_tensor(out=qb[:, c0:c0 + W], in0=qt[:], in1=t1[:], op=ALU.mult), reads=["qt", "t1"], writes=[("qb", bi)])
            P.op("dve", lambda e, bi=bi: e.tensor_copy(out=ebl[:, bi * 32:(bi + 1) * 32], in_=t1[:].rearrange("p (c s) -> p c s", s=64)[:, :, 63]),
                 reads=["t1"], writes=[("ebl", bi)])

        state = C.sb([128, 128], F32, "state")
        stmp = C.sb([128, 128], F32, "stmp")
        P.op("pool", lambda e: e.memset(state[:], 0.0), writes=["state"])
        sbf = [C.sb([128, 128], BF16, f"sbf{i}") for i in range(8)]
        attp = C.ps([64, 512], F32, "attp")
        ktp = C.ps([64, 8, 128], BF16, "ktp")
        up = [C.ps([128, 4, 128], F32, f"up{i}") for i in range(2)]
        outp = C.ps([128, 512], F32, "outp")
        ssp = C.ps([128, 512], F32, "ssp")
        atts = C.sb([64, 512], F32, "atts")
        attm = C.sb([64, 512], BF16, "attm")
        kbt = C.sb([64, 8, 128], BF16, "kbt")
        us = C.sb([128, 8, 128], F32, "us")
        os_ = C.sb([128, 512], F32, "os")
        osq = C.sb([128, 512], F32, "osq")
        rr = C.sb([128, 512], F32, "rr")
        sgt = [C.sb([128, 512], F32, f"sgt{i}") for i in range(2)]
        obo = [C.sb([128, 512], BF16, f"obo{i}") for i in range(2)]
        for g in range(SEQ // 512):
            t0 = g * 512
            bi = t0 // W
            sg = sgt[g % 2]
            P.dma(sg[:], sg_d[:, t0:t0 + 512], writes=[("sgt", g % 2)], key=f"sgt{g % 2}")
            for ch in range(8):
                a = t0 + ch * 64
                P.op("pe", lambda e, a=a, ch=ch: e.matmul(attp[:, ch * 64:(ch + 1) * 64], lhsT=kb[:, a:a + 64], rhs=qb[:, a:a + 64], start=True, stop=True),
                     reads=[("kb", bi), ("qb", bi)], writes=["attp"])
            for ch in range(8):
                a = t0 + ch * 64
                P.op("pe", lambda e, a=a, ch=ch: e.transpose(out=ktp[:, ch, :], in_=kb[:, a:a + 64], identity=ident[:]),
                     reads=[("kb", bi), "ident"], writes=["ktp"])
            P.op("act", lambda e: e.copy(out=atts[:], in_=attp[:]), reads=["attp"], writes=["atts"])
            P.op("pool", lambda e: e.tensor_tensor(out=attm[:], in0=atts[:], in1=cmask[:], op=ALU.mult), reads=["atts", "cmask"], writes=["attm"])
            P.op("act", lambda e: e.copy(out=kbt[:], in_=ktp[:]), reads=["ktp"], writes=["kbt"])
            for ch in range(8):
                cidx = g * 8 + ch
                P.op("pe", lambda e, ch=ch, cidx=cidx: e.matmul(up[ch // 4][:, ch % 4, :], lhsT=kbt[:, ch, :], rhs=vt[:, cidx, :], start=True, stop=True),
                     reads=["kbt", "vt"], writes=[("up", ch // 4)])
            for h2 in range(2):
                P.op("act", lambda e, h2=h2: e.copy(out=us[:, h2 * 4:(h2 + 1) * 4, :], in_=up[h2][:]), reads=[("up", h2)], writes=[("us", h2)])
            for ch in range(8):
                cidx = g * 8 + ch
                P.op("pool", lambda e, ch=ch: e.tensor_copy(out=sbf[ch][:], in_=state[:]), reads=["state"], writes=[("sbf", ch)])
                P.op("dve", lambda e, ch=ch: e.tensor_tensor(out=stmp[:], in0=state[:], in1=us[:, ch, :], op=ALU.add), reads=["state", ("us", ch // 4)], writes=["stmp"])
                P.op("dve", lambda e, cidx=cidx: e.tensor_scalar(out=state[:], in0=stmp[:], scalar1=ebl[:, cidx:cidx + 1], scalar2=None, op0=ALU.mult),
                     reads=["stmp", ("ebl", cidx // 32)], writes=["state"])
            for ch in range(8):
                a = t0 + ch * 64
                cidx = g * 8 + ch
                P.op("pe", lambda e, a=a, ch=ch: e.matmul(outp[:, ch * 64:(ch + 1) * 64], lhsT=sbf[ch][:], rhs=qb[:, a:a + 64], start=True, stop=False),
                     reads=[("sbf", ch), ("qb", bi)], writes=["outp"])
                P.op("pe", lambda e, ch=ch, cidx=cidx: e.matmul(outp[:, ch * 64:(ch + 1) * 64], lhsT=vt[:, cidx, :], rhs=attm[:, ch * 64:(ch + 1) * 64], start=False, stop=True),
                     reads=["vt", "attm"], writes=["outp"])
            P.op("act", lambda e: e.copy(out=os_[:], in_=outp[:]), reads=["outp"], writes=["os"])
            P.op("act", lambda e: e.activation(out=osq[:], in_=outp[:], func=AF.Square), reads=["outp"], writes=["osq"])
            P.op("pe", lambda e: e.matmul(ssp[:], lhsT=ones[:], rhs=osq[:], start=True, stop=True), reads=["ones", "osq"], writes=["ssp"])
            P.op("act", lambda e: e.activation(out=rr[:], in_=ssp[:], func=AF.Ln, scale=1.0 / 128, bias=epsc[:, 0:1]), reads=["ssp", "epsc"], writes=["rr"])
            P.op("act", lambda e: e.activation(out=rr[:], in_=rr[:], func=AF.Exp, scale=-0.5), reads=["rr"], writes=["rr"])
            P.op("dve", lambda e: e.tensor_tensor(out=os_[:], in0=os_[:], in1=rr[:], op=ALU.mult), reads=["os", "rr"], writes=["os"])
            P.op("dve", lambda e: e.tensor_scalar(out=os_[:], in0=os_[:], scalar1=nw[:, 0:1], scalar2=None, op0=ALU.mult), reads=["os", "nw"], writes=["os"])
            ob_ = obo[g % 2]
            P.op("pool", lambda e, ob_=ob_, sg=sg: e.tensor_tensor(out=ob_[:], in0=os_[:], in1=sg[:], op=ALU.mult), reads=["os", ("sgt", g % 2)], writes=[("obo", g % 2)])
            P.dma(oh_d[:, t0:t0 + 512], ob_[:], reads=[("obo", g % 2)], key=f"oh{g % 2}")
        P.emit()
    return nc


def b_consts(layer):
    cm = np.zeros((64, 8, 64), np.float32)
    s = np.arange(64)
    cm[:] = (s[:, None] <= s[None, :]).astype(np.float32)[:, None, :]
    sm = np.ones((128, 2048), np.float32)
    sm[:, ::64] = 0.0
    lm = np.zeros((128, 4), np.float32)
    lm[:, 1:layer + 1] = 1.0
    return cm.reshape(64, 512), sm, lm


NQB = 32


def build_N(blocks=None):
    parity = 1
    nc = bass.Bass("TRN2", target_bir_lowering=False)
    dt_in = lambda n, s, d: nc.dram_tensor(n, list(s), d, kind="ExternalInput").ap()
    q_d = dt_in("qsel", [64, 4, NQB * 128], BF16)
    kc_d = dt_in("kcT", [64, SEQ], BF16)
    vc_d = dt_in("vcT", [64, SEQ], BF16)
    ks_d = dt_in("ksT", [64, SEQ], BF16)
    kw_d = dt_in("kwT", [64, SEQ], BF16)
    vs_d = dt_in("vs", [SEQ, 65], BF16)
    vw_d = dt_in("vw", [SEQ, 65], BF16)
    f0_d = dt_in("F0", [128, 128], F32)
    vm_d = dt_in("vmask", [128, 4], F32)
    gt_d = dt_in("gsel", [128, NQB, 12], F32)
    pek_d = dt_in("pekT", [64, 32], F32)
    pev_d = dt_in("pevT", [64, 32], F32)
    w1k_d = dt_in("w1k", [2048, 256], F32)
    w1v_d = dt_in("w1v", [2048, 256], F32)
    w2k_d = dt_in("w2k", [256, 64], F32)
    w2v_d = dt_in("w2v", [256, 64], F32)
    ovl_d = dt_in("ovl", [128, 4, 128], F32)
    E_d = dt_in("Etab", [128, 64, 128], F32)
    mm_d = dt_in("Mm", [128, 254], F32)
    ma_d = dt_in("Ma", [128, 254], F32)
    on_d = nc.dram_tensor("onT", [256, NQB * 128], BF16, kind="ExternalOutput").ap()
    with contextlib.ExitStack() as st:
        C = Ctx(nc, st)
        P = Prog(nc)
        ident = make_identity(P, C)
        q_sb = C.sb([64, 4, NQB * 128], BF16, "q_sb")
        kcT = C.sb([64, SEQ], BF16, "kcT_sb")
        vcT = C.sb([64, SEQ], BF16, "vcT_sb")
        ksT = C.sb([64, SEQ], BF16, "ksT_sb")
        kwT = C.sb([64, SEQ], BF16, "kwT_sb")
        vs_sb = C.sb([128, 64, 65], BF16, "vs_sb")
        vw_sb = C.sb([128, 64, 65], BF16, "vw_sb")
        gsb = C.sb([128, NQB, 12], F32, "gsb")
        pek = C.sb([64, 32], BF16, "pek")
        pev = C.sb([64, 32], BF16, "pev")
        w1k = C.sb([64, 32, 256], BF16, "w1k_sb")
        w1v = C.sb([64, 32, 256], BF16, "w1v_sb")
        w2k = C.sb([128, 2, 64], BF16, "w2k_sb")
        w2v = C.sb([128, 2, 64], BF16, "w2v_sb")
        ovl = C.sb([128, 4, 128], BF16, "ovl_sb")
        Et = C.sb([128, 64, 128], BF16, "E_sb")
        Mm = C.sb([128, 254], F32, "Mm_sb")
        Ma = C.sb([128, 254], F32, "Ma_sb")
        for t_, d_, k_ in ((q_sb, q_d, "q"), (kcT, kc_d, "kcT"), (vcT, vc_d, "vcT"), (ksT, ks_d, "ksT"), (kwT, kw_d, "kwT"),
                           (gsb, gt_d, "gsb"), (Mm, mm_d, "Mm"), (Ma, ma_d, "Ma")):
            P.dma(t_[:], d_, writes=[k_])
        P.dma(vs_sb[:], vs_d.rearrange("(kt p) d -> p kt d", p=128), writes=["vs"])
        P.dma(vw_sb[:], vw_d.rearrange("(kt p) d -> p kt d", p=128), writes=["vw"])
        F0 = C.sb([128, 128], F32, "F0_sb")
        vmask = C.sb([128, 4], F32, "vmask_sb")
        P.dma(F0[:], f0_d, writes=["F0"])
        P.dma(vmask[:], vm_d, writes=["vmask"])
        P.dma(pek[:], pek_d, writes=["pek"], key="cc", eng="pool")
        P.dma(pev[:], pev_d, writes=["pev"], key="cc", eng="pool")
        P.dma(w1k[:], w1k_d.rearrange("(j d) h -> d j h", d=64), writes=["w1k"], key="cc", eng="pool")
        P.dma(w1v[:], w1v_d.rearrange("(j d) h -> d j h", d=64), writes=["w1v"], key="cc", eng="pool")
        P.dma(w2k[:], w2k_d.rearrange("(c p) d -> p c d", p=128), writes=["w2k"], key="cc", eng="pool")
        P.dma(w2v[:], w2v_d.rearrange("(c p) d -> p c d", p=128), writes=["w2v"], key="cc", eng="pool")
        P.dma(ovl[:], ovl_d, writes=["ovl"], key="cc", eng="pool")
        P.dma(Et[:], E_d, writes=["E"], key="cc", eng="pool")

        sp = [C.ps([128, 512], F32, f"sp{i}") for i in range(3)]
        accC = C.ps([128, 512], F32, "accC")[:, 0:260].rearrange("p (h d) -> p h d", d=65)
        accS = C.ps([128, 512], F32, "accS")[:, 0:260].rearrange("p (h d) -> p h d", d=65)
        accW = C.ps([128, 512], F32, "accW")[:, 0:260].rearrange("p (h d) -> p h d", d=65)
        impP = C.ps([128, 4, 128], F32, "impP")
        tpP = C.ps([128, 8, 128], BF16, "tpP")

        kcmpT = C.sb([64, 512], BF16, "kcmpT")
        vcmp = C.sb([128, 4, 65], BF16, "vcmp")
        P.op("pool", lambda e: e.memset(vcmp[:], 1.0), writes=["vcmp"])
        hid = [C.sb([128, 512], BF16, f"hid{i}") for i in range(2)]
        cb = C.sb([128, 2], F32, "cbias")
        for which, (srcT, skey, w1, w1key, w2, w2key, pe, pekey) in enumerate(((kcT, "kcT", w1k, "w1k", w2k, "w2k", pek, "pek"),
                                                                                 (vcT, "vcT", w1v, "w1v", w2v, "w2v", pev, "pev"))):
            view = srcT[:].rearrange("p (n s) -> p n s", s=16)
            for hc in range(2):
                for j in range(32):
                    P.op("pe", lambda e, j=j, hc=hc, w1=w1, pe=pe: e.matmul(impP[:, 0, 0:1], lhsT=w1[:, j, hc * 128:(hc + 1) * 128], rhs=pe[:, j:j + 1],
                                                                          start=(j == 0), stop=(j == 31)), reads=[w1key, pekey], writes=["impP"])
                P.op("act", lambda e, hc=hc: e.copy(out=cb[:, hc:hc + 1], in_=impP[:, 0, 0:1]), reads=["impP"], writes=[("cb", hc)])
                for j in range(32):
                    n0, s_ = j // 16, j % 16
                    P.op("pe", lambda e, j=j, hc=hc, w1=w1, n0=n0, s_=s_, view=view: e.matmul(sp[hc][:, 0:511], lhsT=w1[:, j, hc * 128:(hc + 1) * 128],
                                                                                              rhs=view[:, n0:n0 + 511, s_], start=(j == 0), stop=(j == 31)),
                         reads=[w1key, skey], writes=[("sp", hc)])
                P.op("pool", lambda e, hc=hc: e.memset(hid[hc][:], 0.0), writes=[("hid", hc)])
                P.op("act", lambda e, hc=hc: e.activation(out=hid[hc][:, 0:511], in_=sp[hc][:, 0:511], func=AF.Silu, bias=cb[:, hc:hc + 1]),
                     reads=[("sp", hc), ("cb", hc)], writes=[("hid", hc)])
            if which == 0:
                for hc in range(2):
                    P.op("pe", lambda e, hc=hc, w2=w2: e.matmul(sp[0][0:64, :], lhsT=w2[:, hc, :], rhs=hid[hc][:], start=(hc == 0), stop=(hc == 1)),
                         reads=[w2key, ("hid", hc)], writes=[("sp", 0)])
                P.op("act", lambda e: e.copy(out=kcmpT[:], in_=sp[0][0:64, :]), reads=[("sp", 0)], writes=["kcmpT"])
            else:
                for nt in range(4):
                    for hc in range(2):
                        P.op("pe", lambda e, hc=hc, nt=nt, w2=w2: e.matmul(sp[1][:, nt * 64:(nt + 1) * 64], lhsT=hid[hc][:, nt * 128:(nt + 1) * 128], rhs=w2[:, hc, :],
                                                                            start=(hc == 0), stop=(hc == 1)), reads=[w2key, ("hid", hc)], writes=[("sp", 1)])
                P.op("act", lambda e: e.copy(out=vcmp[:, :, 0:64], in_=sp[1][:, 0:256].rearrange("p (n d) -> p n d", d=64)), reads=[("sp", 1), "vcmp"], writes=["vcmp"])
                for nt in range(4):
                    P.op("dve", lambda e, nt=nt: e.tensor_scalar(out=vcmp[:, nt, :], in0=vcmp[:, nt, :], scalar1=vmask[:, nt:nt + 1], scalar2=None, op0=ALU.mult),
                         reads=["vcmp", "vmask"], writes=["vcmp"])

        pex = [C.sb([128, 512], BF16, f"pex{i}") for i in range(4)]
        pcnt = [0]
        scnt = [0]
        accs = [C.sb([128, 4, 65], F32, f"accs{i}") for i in range(3)]
        imps = C.sb([128, 4, 128], F32, "imps")
        rc = C.sb([128, 12], F32, "rc")
        wg = C.sb([128, 12], F32, "wg")
        impa = C.sb([128, 128], F32, "impa")
        sc = C.sb([128, 128], F32, "sc")
        sc2 = C.sb([128, 128], F32, "sc2")
        m8 = C.sb([128, 16], F32, "m8")
        selb = C.sb([128, 128], BF16, "selb")
        selT = C.sb([128, 4, 128], BF16, "selT")
        o32 = C.sb([128, 4, 64], F32, "o32")
        obf = C.sb([128, 256], BF16, "obf")
        onb = [C.sb([128, 2, 128], BF16, f"onb{i}") for i in range(2)]
        PAT = [[0, 4], [1, 128]]

        def score_tile(kT, kkey, c0, qv, extra=None):
            s_i = scnt[0] % 3
            scnt[0] += 1
            spt, spk = sp[s_i], ("sp", s_i)
            P.op("pe", lambda e: e.matmul(spt[:], lhsT=kT[:, c0:c0 + 128], rhs=qv, start=True, stop=(extra is None)), reads=[kkey, "q"], writes=[spk])
            if extra is not None:
                extra(spt, spk)
            p_i = pcnt[0] % 4
            pcnt[0] += 1
            pt, pk = pex[p_i], ("pex", p_i)
            P.op("act", lambda e: e.activation(out=pt[:], in_=spt[:], func=AF.Exp, scale=0.125), reads=[spk], writes=[pk])
            return pt, pk

        def mask_tile(pt, pk, base, cm, pat):
            P.op("pool", lambda e: e.affine_select(out=pt[:], in_=pt[:], pattern=pat, compare_op=ALU.is_ge, fill=0.0, base=base, channel_multiplier=cm),
                 reads=[pk], writes=[pk])

        def pv(pt, pk, acc, akey, vsb, vkey, vidx, first, last):
            for h in range(4):
                P.op("pe", lambda e, h=h: e.matmul(acc[:, h, :], lhsT=pt[:, h * 128:(h + 1) * 128], rhs=vsb[:, vidx, :], start=(first and h == 0), stop=last),
                     reads=[pk, vkey], writes=[akey])

        pend = []

        def submit(fn):
            pend.append(fn)
            if len(pend) > 1:
                pend.pop(0)()

        def flush():
            while pend:
                pend.pop(0)()

        for i in (range(NQB) if blocks is None else blocks):
            qb = 2 * i + parity
            t0 = qb * 128
            qv = q_sb[:, :, i * 128:(i + 1) * 128]
            ncmp = min(4, (8 * qb + 6) // 128 + 1)
            for j in range(ncmp):
                pt, pk = score_tile(kcmpT, "kcmpT", j * 128, qv)
                if (j + 1) * 128 - 1 >= 8 * qb - 1:
                    mask_tile(pt, pk, t0 - 31 - 16 * 128 * j, -16, PAT)
                def fin_c(pt=pt, pk=pk, j=j, ncmp=ncmp):
                    pv(pt, pk, accC, "accC", vcmp, "vcmp", j, j == 0, j == ncmp - 1)
                    for h in range(4):
                        P.op("pe", lambda e, h=h: e.matmul(impP[:, h, :], lhsT=pt[:, h * 128:(h + 1) * 128], rhs=ovl[:, j, :], start=(j == 0 and h == 0), stop=(j == ncmp - 1)),
                             reads=[pk, "ovl"], writes=["impP"])
                submit(fin_c)
            flush()
            P.op("act", lambda e: e.copy(out=accs[0][:], in_=accC[:]), reads=["accC"], writes=[("accs", 0)])
            P.op("act", lambda e: e.copy(out=imps[:], in_=impP[:]), reads=["impP"], writes=["imps"])
            k0 = max(0, qb - 4)
            for kt in range(k0, qb + 1):
                pt, pk = score_tile(kwT, "kwT", kt * 128, qv)
                if kt == qb:
                    mask_tile(pt, pk, 0, -1, PAT)
                elif kt == qb - 4:
                    mask_tile(pt, pk, -1, 1, [[0, 4], [-1, 128]])
                submit(lambda pt=pt, pk=pk, kt=kt, qb=qb, k0=k0: pv(pt, pk, accW, "accW", vw_sb, "vw", kt, kt == k0, kt == qb))
            flush()
            P.op("act", lambda e: e.copy(out=accs[2][:], in_=accW[:]), reads=["accW"], writes=[("accs", 2)])
            P.op("dve", lambda e: e.tensor_scalar_max(out=rc[:, 0:4], in0=accs[0][:, :, 64], scalar1=1e-30), reads=[("accs", 0)], writes=[("rc", 0)])
            P.op("dve", lambda e: e.reciprocal(out=rc[:, 0:4], in_=rc[:, 0:4]), reads=[("rc", 0)], writes=[("rc", 0)])
            P.op("dve", lambda e: e.tensor_scalar(out=impa[:], in0=imps[:, 0, :], scalar1=rc[:, 0:1], scalar2=None, op0=ALU.mult), reads=["imps", ("rc", 0)], writes=["impa"])
            for h in range(1, 4):
                P.op("dve", lambda e, h=h: e.scalar_tensor_tensor(out=impa[:], in0=imps[:, h, :], scalar=rc[:, h:h + 1], in1=impa[:], op0=ALU.mult, op1=ALU.add),
                     reads=["imps", ("rc", 0), "impa"], writes=["impa"])
            so = 126 - 2 * qb
            P.op("dve", lambda e, so=so: e.tensor_tensor(out=sc[:], in0=impa[:], in1=Mm[:, so:so + 128], op=ALU.mult), reads=["impa", "Mm"], writes=["sc"])
            P.op("dve", lambda e, so=so: e.tensor_tensor(out=sc[:], in0=sc[:], in1=Ma[:, so:so + 128], op=ALU.add), reads=["sc", "Ma"], writes=["sc"])
            P.op("dve", lambda e: e.tensor_tensor(out=sc[:], in0=sc[:], in1=F0[:], op=ALU.max), reads=["sc", "F0"], writes=["sc"])
            P.op("dve", lambda e: e.max(out=m8[:, 0:8], in_=sc[:]), reads=["sc"], writes=[("m8", 0)])
            P.op("dve", lambda e: e.match_replace(out=sc2[:], in_to_replace=m8[:, 0:8], in_values=sc[:], imm_value=-2.0), reads=["sc", ("m8", 0)], writes=["sc2"])
            P.op("dve", lambda e: e.max(out=m8[:, 8:16], in_=sc2[:]), reads=["sc2"], writes=[("m8", 1)])
            P.op("dve", lambda e: e.tensor_scalar_max(out=m8[:, 15:16], in0=m8[:, 15:16], scalar1=0.0), reads=[("m8", 1)], writes=[("m8", 1)])
            P.op("dve", lambda e: e.tensor_scalar(out=sc2[:], in0=sc[:], scalar1=m8[:, 15:16], scalar2=None, op0=ALU.is_ge), reads=["sc", ("m8", 1), "sc2"], writes=["sc2"])
            P.op("dve", lambda e: e.tensor_scalar(out=selb[:], in0=sc2[:], scalar1=-1.0, scalar2=30000.0, op0=ALU.add, op1=ALU.mult), reads=["sc2"], writes=["selb"])
            for h in range(4):
                P.op("pe", lambda e, h=h: e.transpose(out=tpP[:, h, :], in_=selb[:], identity=ident[:]), reads=["selb", "ident"], writes=["tpP"])
            P.op("act", lambda e: e.copy(out=selT[:], in_=tpP[:, 0:4, :]), reads=["tpP"], writes=["selT"])
            for kt in range(qb + 1):
                def extra(spt, spk, kt=kt):
                    P.op("pe", lambda e: e.matmul(spt[:], lhsT=Et[:, kt, :], rhs=selT[:], start=False, stop=True), reads=["E", "selT"], writes=[spk])
                pt, pk = score_tile(ksT, "ksT", kt * 128, qv, extra)
                if kt == qb:
                    mask_tile(pt, pk, 0, -1, PAT)
                submit(lambda pt=pt, pk=pk, kt=kt, qb=qb: pv(pt, pk, accS, "accS", vs_sb, "vs", kt, kt == 0, kt == qb))
            flush()
            P.op("act", lambda e: e.copy(out=accs[1][:], in_=accS[:]), reads=["accS"], writes=[("accs", 1)])
            gv = gsb[:, i, :].rearrange("p (h b) -> p h b", b=3)
            for b in range(3):
                if b > 0:
                    P.op("dve", lambda e, b=b: e.tensor_scalar_max(out=rc[:, 4 * b:4 * b + 4], in0=accs[b][:, :, 64], scalar1=1e-30), reads=[("accs", b)], writes=[("rc", b)])
                    P.op("dve", lambda e, b=b: e.reciprocal(out=rc[:, 4 * b:4 * b + 4], in_=rc[:, 4 * b:4 * b + 4]), reads=[("rc", b)], writes=[("rc", b)])
                P.op("dve", lambda e, b=b, gv=gv: e.tensor_tensor(out=wg[:, 4 * b:4 * b + 4], in0=rc[:, 4 * b:4 * b + 4], in1=gv[:, :, b], op=ALU.mult), reads=[("rc", b), "gsb"], writes=[("wg", b)])
                for h in range(4):
                    if b == 0:
                        P.op("dve", lambda e, h=h: e.tensor_scalar(out=o32[:, h, :], in0=accs[0][:, h, 0:64], scalar1=wg[:, h:h + 1], scalar2=None, op0=ALU.mult),
                             reads=[("accs", 0), ("wg", 0)], writes=[("o32", h)])
                    else:
                        P.op("dve", lambda e, h=h, b=b: e.scalar_tensor_tensor(out=o32[:, h, :], in0=accs[b][:, h, 0:64], scalar=wg[:, 4 * b + h:4 * b + h + 1], in1=o32[:, h, :],
                                                                               op0=ALU.mult, op1=ALU.add), reads=[("accs", b), ("wg", b), ("o32", h)], writes=[("o32", h)])
            P.op("pool", lambda e: e.tensor_copy(out=obf[:], in_=o32[:].rearrange("p h d -> p (h d)")), reads=[("o32", h) for h in range(4)], writes=["obf"])
            for h2 in range(2):
                P.op("pe", lambda e, h2=h2: e.transpose(out=tpP[:, h2, :], in_=obf[:, h2 * 128:(h2 + 1) * 128], identity=ident[:]), reads=["obf", "ident"], writes=["tpP"])
            ob_ = onb[i % 2]
            P.op("act", lambda e, ob_=ob_: e.copy(out=ob_[:], in_=tpP[:, 0:2, :]), reads=["tpP"], writes=[("onb", i % 2)])
            P.dma(on_d[:, i * 128:(i + 1) * 128].rearrange("(c p) t -> p c t", p=128), ob_[:], reads=[("onb", i % 2)], key=f"on{i % 2}")
        P.emit()
    return nc


def n_consts():
    n = np.arange(512)
    blk = np.arange(128)
    ov = ((16 * n[:, None] < 64 * blk[None, :] + 64) & (16 * n[:, None] + 32 > 64 * blk[None, :])).astype(np.float32)
    ov[511] = 0.0
    ovl = np.ascontiguousarray(ov.reshape(4, 128, 128).transpose(1, 0, 2))
    E = np.zeros((128, 64, 128), np.float32)
    for kt in range(64):
        E[2 * kt, kt, 0:64] = 1.0
        E[2 * kt + 1, kt, 64:128] = 1.0
    tl = np.arange(128)
    rel = np.arange(254) - 126
    cur = (tl // 64)[:, None]
    Mm = (rel[None, :] <= cur - 2).astype(np.float32)
    Ma = np.where((rel[None, :] == cur) | (rel[None, :] == cur - 1), 1e9, np.where(rel[None, :] > cur, -1.0, 0.0)).astype(np.float32)
    return ovl, E, Mm, Ma


def n_inputs(c, hfb, htb, htg, prm, l, consts):
    g, par = c // 2, c % 2
    ovl, E, Mm, Ma = consts
    sh = 128 if par == 0 else 0

    def shT(a):
        if sh == 0:
            return np.ascontiguousarray(a)
        out = np.zeros_like(a)
        out[:, sh:] = a[:, :SEQ - sh]
        return out

    def shV(a):
        out = np.zeros((SEQ, 65), NPBF)
        out[sh:, 0:64] = a[:SEQ - sh]
        out[sh:, 64] = 1.0
        return out
    qg = hfb[g * 256:(g + 1) * 256].reshape(4, 64, SEQ // 128, 128)[:, :, par::2, :]
    qsel = np.ascontiguousarray(qg.transpose(1, 0, 2, 3).reshape(64, 4, NQB * 128))
    gs = htg[:, g * 12:(g + 1) * 12].reshape(SEQ // 128, 128, 12)[par::2]
    nsh, bsh = sh // 16, sh // 64
    ov = ovl.transpose(1, 0, 2).reshape(512, 128)
    ov2 = np.zeros_like(ov)
    ov2[nsh:, bsh:] = ov[:512 - nsh, :128 - bsh]
    ov2[511] = 0.0
    F0 = np.full((128, 128), -3.0e38, np.float32)
    F0[:, bsh] = 1e9
    vm = np.ones(512, np.float32)
    vm[:nsh] = 0.0
    return {"qsel": qsel, "kcT": shT(hfb[1024 + g * 64:1024 + (g + 1) * 64]),
            "ksT": shT(hfb[1280 + g * 64:1280 + (g + 1) * 64]), "kwT": shT(hfb[1536 + g * 64:1536 + (g + 1) * 64]),
            "vcT": shT(hfb[1792 + g * 64:1792 + (g + 1) * 64]),
            "vs": shV(htb[:, 1024 + g * 64:1024 + (g + 1) * 64]), "vw": shV(htb[:, 1280 + g * 64:1280 + (g + 1) * 64]),
            "gsel": np.ascontiguousarray(gs.transpose(1, 0, 2)),
            "pekT": np.ascontiguousarray(prm["cmp_pe_k"][l].T), "pevT": np.ascontiguousarray(prm["cmp_pe_v"][l].T),
            "w1k": prm["cmp_w1_k"][l], "w1v": prm["cmp_w1_v"][l], "w2k": prm["cmp_w2_k"][l], "w2v": prm["cmp_w2_v"][l],
            "ovl": np.ascontiguousarray(ov2.reshape(4, 128, 128).transpose(1, 0, 2)), "Etab": E, "Mm": Mm, "Ma": Ma,
            "F0": F0, "vmask": np.ascontiguousarray(vm.reshape(4, 128).T)}


_CACHE = {}
_DBG = None


def _prog(name, fn):
    if name not in _CACHE:
        _CACHE[name] = fn()
    return _CACHE[name]


def kernel(**inp):
    prm = {k: np.asarray(v) for k, v in inp.items()}
    prm["w_f"] = [np.ascontiguousarray(prm["w_in"][l][:, FM_COLS]) for l in range(DEPTH)]
    prm["w_t"] = [np.ascontiguousarray(prm["w_in"][l][:, TM_COLS]) for l in range(DEPTH)]
    tabs = rope_tables()
    nconst = n_consts()
    cores = list(range(NCORE))
    xT = np.ascontiguousarray(prm["x"][0].T)
    res = run_bass_kernel_spmd(_prog("A", lambda: build_T(False, True)),
                               [t_inputs(c, xT, None, prm, None, 0, tabs) for c in cores], core_ids=cores).results
    for l in range(DEPTH):
        hf32 = np.concatenate([r["hf32"] for r in res], axis=1)
        htb = np.concatenate([r["htb"] for r in res], axis=0)
        cm, sm, lm = b_consts(l)
        maps = []
        for c in cores:
            cs = slice(c * 128, (c + 1) * 128)
            maps.append({"qsT": np.ascontiguousarray(hf32[0:1024][cs]), "zT": np.ascontiguousarray(hf32[1024:2048][cs]),
                         "sgT": np.ascontiguousarray(hf32[2048:3072][cs]), "v": np.ascontiguousarray(htb[:, cs]),
                         "lbl": np.ascontiguousarray(prm["hgrn_lb_logits"][:, cs].T), "lmask": lm,
                         "nw": np.ascontiguousarray(prm["hgrn_norm_w"][l][cs].reshape(128, 1)), "cmask": cm, "smask": sm})
        resB = run_bass_kernel_spmd(_prog("B", build_B), maps, core_ids=cores).results
        hfb = np.concatenate([r["hfb"] for r in res], axis=1)
        htg = np.concatenate([r["htg"] for r in res], axis=0)
        resN = run_bass_kernel_spmd(_prog("N", build_N), [n_inputs(c, hfb, htb, htg, prm, l, nconst) for c in cores], core_ids=cores).results
        oT = np.zeros((D, SEQ), NPBF)
        for c in cores:
            oT[c * 128:(c + 1) * 128] = resB[c]["ohT"]
            g, par = c // 2, c % 2
            on = resN[c]["onT"].reshape(256, NQB, 128)
            oT[1024 + g * 256:1024 + (g + 1) * 256].reshape(256, SEQ // 128, 128)[:, par::2, :] = on
        if l < DEPTH - 1:
            res = run_bass_kernel_spmd(_prog("CA", lambda: build_T(True, True)),
                                       [t_inputs(c, xT, oT, prm, l, l + 1, tabs) for c in cores], core_ids=cores).results
        else:
            res = run_bass_kernel_spmd(_prog("C", lambda: build_T(True, False)),
                                       [t_inputs(c, xT, oT, prm, l, None, tabs) for c in cores], core_ids=cores).results
        xT = np.concatenate([r["xT_out"] for r in res], axis=1)
        if _DBG is not None:
            _DBG[("oT", l)] = oT
            _DBG[("xT", l)] = xT
            if l == _DBG.get("stop", DEPTH):
                break
    return np.ascontiguousarray(xT.T)[None].astype(np.float32)
```
